# Optimizing a Trainium2 kernel written in Bass

```python
import math
import jax, jax.numpy as jnp
from jax import lax
import numpy as np

D_MODEL = 1024
BATCH = 2
SEQ = 16384
DEPTH = 4
DEC_BATCH = 4
DEC_SEQ = 8192
PAST_LEN = 128

N_MIXERS = 3
EPS = 1e-6
A_GROUPS = ((128, 1), (512, 4), (2048, 16))
A_N_GROUPS = 3
A_HEADS = 8
A_HEAD_DIM = 128
A_WIDTH = A_HEADS * A_HEAD_DIM
A_IN = 3 * A_N_GROUPS * A_WIDTH + A_WIDTH
ROPE_THETA = 500000.0
ROPE_DIM = A_HEAD_DIM // 4
B_WIDTH = D_MODEL
B_BLOCKS = 8
B_BLOCK_DIM = B_WIDTH // B_BLOCKS
B_CONV = 4
B_C = 8.0
C_HEADS = 8
C_KDIM = 128
C_VDIM = D_MODEL // C_HEADS
C_FWIDTH = C_HEADS * C_KDIM
C_VWIDTH = C_HEADS * C_VDIM
C_CHUNK = 64
C_IN = 3 * C_FWIDTH + 2 * C_VWIDTH

N_A = (DEPTH + 2) // 3
N_B = (DEPTH + 1) // 3
N_C = DEPTH // 3

kernel_name = "hybrid_dilated_attn_rglru_hgrn2_encoder"


def rms_norm(x, g):
    xf = x.astype(jnp.float32)
    y = xf * lax.rsqrt(jnp.mean(xf * xf, axis=-1, keepdims=True) + EPS)
    return (y * g.astype(jnp.float32)).astype(x.dtype)


def partial_rope(x, pos):
    half = ROPE_DIM // 2
    inv = ROPE_THETA ** (-(jnp.arange(half, dtype=jnp.float32) * 2.0) / ROPE_DIM)
    ang = pos[:, None] * inv[None, :]
    cos = jnp.cos(ang)[:, None, None, :]
    sin = jnp.sin(ang)[:, None, None, :]
    xf = x.astype(jnp.float32)
    x1 = xf[..., :half]
    x2 = xf[..., half:ROPE_DIM]
    out = jnp.concatenate([x1 * cos - x2 * sin, x2 * cos + x1 * sin, xf[..., ROPE_DIM:]], axis=-1)
    return out


def dilated_window_attention(q, k, v, window, dilation):
    Bsz, S, H, E = q.shape
    half = window // (2 * dilation)
    blk = half
    L = S // dilation
    n = -(-L // blk)
    Lp = n * blk

    def to_strided(t):
        return t.reshape(Bsz, L, dilation, H, E).transpose(0, 2, 1, 3, 4)

    qs = jnp.pad(to_strided(q), ((0, 0), (0, 0), (0, Lp - L), (0, 0), (0, 0)))
    qs = qs.reshape(Bsz, dilation, n, blk, H, E)
    padk = ((0, 0), (0, 0), (blk, Lp - L + blk), (0, 0), (0, 0))

    def windows(t):
        t = jnp.pad(to_strided(t), padk).reshape(Bsz, dilation, n + 2, blk, H, E)
        return jnp.concatenate([t[:, :, :-2], t[:, :, 1:-1], t[:, :, 2:]], axis=3)

    kw = windows(k)
    vw = windows(v)
    s = jnp.einsum('bdnqhe,bdnkhe->bdnhqk', qs, kw) * (E ** -0.5)
    qi = jnp.arange(n)[:, None, None] * blk + jnp.arange(blk)[None, :, None]
    ki = (jnp.arange(n)[:, None, None] - 1) * blk + jnp.arange(3 * blk)[None, None, :]
    valid = (jnp.abs(qi - ki) <= half) & (ki >= 0) & (ki < L)
    s = jnp.where(valid[:, None], s, -1e30)
    m = jnp.max(s, axis=-1, keepdims=True)
    p = jnp.exp(s - m)
    den = jnp.sum(p, axis=-1, keepdims=True)
    o = jnp.einsum('bdnhqk,bdnkhe->bdnqhe', p, vw) / jnp.swapaxes(den, 3, 4)
    lse = jnp.swapaxes((m + jnp.log(den))[..., 0], 3, 4)
    o = o.reshape(Bsz, dilation, Lp, H, E)[:, :, :L].transpose(0, 2, 1, 3, 4).reshape(Bsz, S, H, E)
    lse = lse.reshape(Bsz, dilation, Lp, H)[:, :, :L].transpose(0, 2, 1, 3).reshape(Bsz, S, H)
    return o, lse


def mixer_a(h, w_in, w_out, pos):
    Bsz, S, _ = h.shape
    proj = h @ w_in
    qkv = proj[..., :3 * A_N_GROUPS * A_WIDTH].reshape(Bsz, S, 3, A_N_GROUPS, A_HEADS, A_HEAD_DIM)
    gate = proj[..., 3 * A_N_GROUPS * A_WIDTH:]
    q = partial_rope(qkv[:, :, 0], pos)
    k = partial_rope(qkv[:, :, 1], pos)
    v = qkv[:, :, 2].astype(jnp.float32)
    outs, lses = [], []
    for g, (window, dil) in enumerate(A_GROUPS):
        o, l = dilated_window_attention(q[:, :, g], k[:, :, g], v[:, :, g], window, dil)
        outs.append(o)
        lses.append(l)
    wts = jax.nn.softmax(jnp.stack(lses, 0), axis=0)
    o = jnp.einsum('gbsh,gbshe->bshe', wts, jnp.stack(outs, 0))
    o = o.reshape(Bsz, S, A_WIDTH).astype(h.dtype) * jax.nn.silu(gate)
    return o @ w_out


def centred_depthwise_conv(x, w, b):
    S = x.shape[1]
    left = B_CONV // 2
    right = B_CONV - 1 - left
    xp = jnp.pad(x, ((0, 0), (left, right), (0, 0)))
    y = xp[:, 0:S] * w[0]
    for t in range(1, B_CONV):
        y = y + xp[:, t:t + S] * w[t]
    return y + b


def linear_combine(left, right):
    a1, b1 = left
    a2, b2 = right
    return a1 * a2, a2 * b1 + b2


def rg_lru(x, w_r, b_r, w_i, b_i, lam, reverse):
    Bsz, S, E = x.shape
    xb = x.reshape(Bsz, S, B_BLOCKS, B_BLOCK_DIM)
    r = jax.nn.sigmoid(jnp.einsum('bsnc,ncd->bsnd', xb, w_r.astype(jnp.float32)).reshape(Bsz, S, E) + b_r)
    i = jax.nn.sigmoid(jnp.einsum('bsnc,ncd->bsnd', xb, w_i.astype(jnp.float32)).reshape(Bsz, S, E) + b_i)
    log_a = -B_C * r * jax.nn.softplus(-lam.astype(jnp.float32))
    a = jnp.exp(log_a)
    u = jnp.sqrt(-jnp.expm1(2.0 * log_a)) * (i * x)
    _, hs = lax.associative_scan(linear_combine, (a, u), reverse=reverse, axis=1)
    return hs


def mixer_b(h, w_in, conv_w, conv_b, w_r, b_r, w_i, b_i, lam, w_out):
    proj = h @ w_in
    xb = proj[..., :B_WIDTH].astype(jnp.float32)
    gate = proj[..., B_WIDTH:]
    xc = centred_depthwise_conv(xb, conv_w.astype(jnp.float32), conv_b.astype(jnp.float32))
    y = (rg_lru(xc, w_r[0], b_r[0], w_i[0], b_i[0], lam[0], False)
         + rg_lru(xc, w_r[1], b_r[1], w_i[1], b_i[1], lam[1], True))
    return (y.astype(h.dtype) * jax.nn.silu(gate)) @ w_out


def hgrn2_chunk_scan(q, k, v, logf):
    Bsz, S, H, K = q.shape
    V = v.shape[-1]
    n = S // C_CHUNK
    q, k, v, logf = [t.reshape(Bsz, n, C_CHUNK, H, t.shape[-1]) for t in (q, k, v, logf)]
    b = jnp.cumsum(logf, axis=2)
    b_last = b[:, :, -1:]
    ref = b[:, :, C_CHUNK // 2:C_CHUNK // 2 + 1]
    att = jnp.einsum('bnthk,bnshk->bnhts', q * jnp.exp(b - ref), k * jnp.exp(ref - b))
    causal = jnp.tril(jnp.ones((C_CHUNK, C_CHUNK), dtype=bool))
    att = jnp.where(causal, att, 0.0)
    o_intra = jnp.einsum('bnhts,bnshv->bnthv', att, v)
    q_dec = q * jnp.exp(b)
    k_dec = k * jnp.exp(b_last - b)
    g = jnp.exp(b_last[:, :, 0])

    def step(state, xs):
        qd, kd, vc, gc = xs
        o = jnp.einsum('bthk,bhkv->bthv', qd, state)
        state = gc[..., None] * state + jnp.einsum('bshk,bshv->bhkv', kd, vc)
        return state, o

    xs = (jnp.moveaxis(q_dec, 1, 0), jnp.moveaxis(k_dec, 1, 0), jnp.moveaxis(v, 1, 0), jnp.moveaxis(g, 1, 0))
    state0 = jnp.zeros((Bsz, H, K, V), jnp.float32)
    _, o_inter = lax.scan(step, state0, xs)
    o = o_intra + jnp.moveaxis(o_inter, 0, 1)
    return o.reshape(Bsz, S, H, V)


def mixer_c(h, w_in, lb, gnorm_g, w_out):
    Bsz, S, _ = h.shape
    proj = h @ w_in
    F = C_FWIDTH
    q = proj[..., :F].astype(jnp.float32).reshape(Bsz, S, C_HEADS, C_KDIM)
    zf = proj[..., F:2 * F]
    zb = proj[..., 2 * F:3 * F]
    vi = proj[..., 3 * F:3 * F + C_VWIDTH].astype(jnp.float32).reshape(Bsz, S, C_HEADS, C_VDIM)
    gate = proj[..., 3 * F + C_VWIDTH:]

    def gates(z, lbd):
        z = z.astype(jnp.float32)
        logf = jnp.logaddexp(jnp.log(lbd), jnp.log1p(-lbd) + jax.nn.log_sigmoid(z))
        kk = (1.0 - lbd) * jax.nn.sigmoid(-z)
        return logf.reshape(Bsz, S, C_HEADS, C_KDIM), kk.reshape(Bsz, S, C_HEADS, C_KDIM)

    lf_f, k_f = gates(zf, lb[0])
    lf_b, k_b = gates(zb, lb[1])
    o_f = hgrn2_chunk_scan(q, k_f, vi, lf_f)
    o_b = jnp.flip(hgrn2_chunk_scan(jnp.flip(q, 1), jnp.flip(k_b, 1), jnp.flip(vi, 1), jnp.flip(lf_b, 1)), 1)
    o = o_f + o_b
    o = o * lax.rsqrt(jnp.mean(o * o, axis=-1, keepdims=True) + EPS) * gnorm_g.astype(jnp.float32)
    o = o.reshape(Bsz, S, C_VWIDTH).astype(h.dtype) * jax.nn.silu(gate)
    return o @ w_out


def trunk(x, norm_g, final_g, a_w_in, a_w_out, b_w_in, b_conv_w, b_conv_b, b_w_r, b_b_r, b_w_i, b_b_i,
          b_lambda, b_w_out, c_w_in, c_lower_bounds, c_gnorm_g, c_w_out):
    S = x.shape[1]
    pos = jnp.arange(S, dtype=jnp.float32)
    sm = jax.nn.softmax(c_lower_bounds.astype(jnp.float32), axis=0)
    lbs = jnp.cumsum(sm, axis=0) - sm[0]
    for layer in range(DEPTH):
        kind = layer % N_MIXERS
        j = layer // N_MIXERS
        h = rms_norm(x, norm_g[layer])
        if kind == 0:
            x = x + mixer_a(h, a_w_in[j], a_w_out[j], pos)
        elif kind == 1:
            x = x + mixer_b(h, b_w_in[j], b_conv_w[j], b_conv_b[j], b_w_r[j], b_b_r[j], b_w_i[j], b_b_i[j],
                            b_lambda[j], b_w_out[j])
        else:
            x = x + mixer_c(h, c_w_in[j], lbs[layer], c_gnorm_g[j], c_w_out[j])
    return rms_norm(x, final_g)


def setup_inputs(seed: int = 0) -> dict:
    key = jax.random.key(seed)
    ks = jax.random.split(key, 20)
    f32 = jnp.float32
    nrm = lambda k, shape, scale: jax.random.normal(k, shape, f32) * scale
    u = jax.random.uniform(ks[12], (N_B, 2, B_WIDTH), f32, 0.9, 0.999) ** (1.0 / B_C)
    return {
        "x_prompt": nrm(ks[0], (BATCH, SEQ, D_MODEL), 1.0),
        "x_sample": nrm(ks[1], (DEC_BATCH, DEC_SEQ, D_MODEL), 1.0),
        "norm_g": 1.0 + nrm(ks[2], (DEPTH, D_MODEL), 0.01),
        "final_g": 1.0 + nrm(ks[3], (D_MODEL,), 0.01),
        "a_w_in": nrm(ks[4], (N_A, D_MODEL, A_IN), D_MODEL ** -0.5),
        "a_w_out": nrm(ks[5], (N_A, A_WIDTH, D_MODEL), A_WIDTH ** -0.5),
        "b_w_in": nrm(ks[6], (N_B, D_MODEL, 2 * B_WIDTH), D_MODEL ** -0.5),
        "b_conv_w": nrm(ks[7], (N_B, B_CONV, B_WIDTH), B_CONV ** -0.5),
        "b_conv_b": nrm(ks[8], (N_B, B_WIDTH), 0.01),
        "b_w_r": nrm(ks[9], (N_B, 2, B_BLOCKS, B_BLOCK_DIM, B_BLOCK_DIM), B_BLOCK_DIM ** -0.5),
        "b_b_r": nrm(ks[10], (N_B, 2, B_WIDTH), 0.01),
        "b_w_i": nrm(ks[11], (N_B, 2, B_BLOCKS, B_BLOCK_DIM, B_BLOCK_DIM), B_BLOCK_DIM ** -0.5),
        "b_b_i": nrm(ks[13], (N_B, 2, B_WIDTH), 0.01),
        "b_lambda": jnp.log(u) - jnp.log1p(-u),
        "b_w_out": nrm(ks[14], (N_B, B_WIDTH, D_MODEL), B_WIDTH ** -0.5),
        "c_w_in": nrm(ks[15], (N_C, D_MODEL, C_IN), D_MODEL ** -0.5),
        "c_lower_bounds": nrm(ks[16], (DEPTH, 2, C_FWIDTH), 0.1),
        "c_gnorm_g": 1.0 + nrm(ks[17], (N_C, C_VDIM), 0.01),
        "c_w_out": nrm(ks[18], (N_C, C_VWIDTH, D_MODEL), C_VWIDTH ** -0.5),
    }


def reference(x_prompt, x_sample, norm_g, final_g, a_w_in, a_w_out, b_w_in, b_conv_w, b_conv_b, b_w_r, b_b_r,
              b_w_i, b_b_i, b_lambda, b_w_out, c_w_in, c_lower_bounds, c_gnorm_g, c_w_out):
    y_prompt = trunk(x_prompt, norm_g, final_g, a_w_in, a_w_out, b_w_in, b_conv_w, b_conv_b, b_w_r, b_b_r,
                     b_w_i, b_b_i, b_lambda, b_w_out, c_w_in, c_lower_bounds, c_gnorm_g, c_w_out)
    y_sample = trunk(x_sample, norm_g, final_g, a_w_in, a_w_out, b_w_in, b_conv_w, b_conv_b, b_w_r, b_b_r,
                     b_w_i, b_b_i, b_lambda, b_w_out, c_w_in, c_lower_bounds, c_gnorm_g, c_w_out)
    return (y_prompt, y_sample)
```

```python
import numpy as np
import ml_dtypes
from contextlib import ExitStack
import concourse.bass as bass
import concourse.mybir as mybir
from concourse.bass_utils import run_bass_kernel_spmd

F32 = mybir.dt.float32
BF16 = mybir.dt.bfloat16
ALU = mybir.AluOpType
AF = mybir.ActivationFunctionType
AX = mybir.AxisListType

D = 1024
EPS = 1e-6
HALO = 2048
SAME_SYNC = True
INTERLEAVE_OFF = True
ATT_SCALE = 128.0 ** -0.5
GROUPS = ((0, 1), (1, 4), (2, 16))


class Buf:
    def __init__(self, t, name):
        self.t = t
        self.name = name
        self.w = None
        self.r = {}
        self.dsem = None
        self.dcnt = 0

    def __getitem__(self, idx):
        return self.t[idx]


def bcast(ap, pos, n):
    l = [list(x) for x in ap.ap]
    l.insert(1 + pos, [0, n])
    return bass.AP(ap.tensor, ap.offset, l)


class Ctx:
    def __init__(self, nc, es):
        self.nc = nc
        self.es = es
        self.eng = {"pe": nc.tensor, "act": nc.scalar, "dve": nc.vector, "pool": nc.gpsimd, "sp": nc.sync}
        self.esem = {e: es.enter_context(nc.semaphore("s_" + e)) for e in ("pe", "act", "dve", "pool")}
        self.cnt = {e: 0 for e in self.esem}
        self.waited = {e: {} for e in self.eng}
        self.slots = []
        self.free_slots = {}
        self.active = []
        self.nsem = 0
        self.nb = 0

    def sb(self, scope, name, shape, dt):
        self.nb += 1
        return Buf(scope.enter_context(self.nc.sbuf_tensor("%s_%d" % (name, self.nb), shape, dt)), name)

    def ps(self, scope, name, shape, dt):
        self.nb += 1
        return Buf(scope.enter_context(self.nc.psum_tensor("%s_%d" % (name, self.nb), shape, dt)), name)

    def _deps(self, e, reads, writes):
        deps = []
        for b in reads:
            if b.w is not None:
                deps.append(b.w)
        for b in writes:
            if b.w is not None:
                deps.append(b.w)
            deps.extend(b.r.values())
        for (key, sem, val) in deps:
            if key == e and (e == "pe" or not SAME_SYNC):
                continue
            if self.waited[e].get(key, 0) >= val:
                continue
            self.eng[e].wait_ge(sem, val)
            self.waited[e][key] = val

    def op(self, e, fn, reads=(), writes=()):
        self._deps(e, reads, writes)
        ins = fn(self.eng[e])
        self.cnt[e] += 1
        ins.then_inc(self.esem[e], 1)
        tok = (e, self.esem[e], self.cnt[e])
        for b in reads:
            b.r[e] = tok
        for b in writes:
            b.w = tok
            b.r = {}
        return ins

    def dma(self, e, out, in_, reads=(), writes=()):
        self._deps(e, reads, writes)
        prim = writes[0] if writes else reads[0]
        if prim.dsem is None:
            prim.dsem = {}
        if e not in prim.dsem:
            fl = self.free_slots.setdefault(e, [])
            if fl:
                prim.dsem[e] = fl.pop()
            else:
                self.nsem += 1
                prim.dsem[e] = [self.es.enter_context(self.nc.semaphore("d%d" % self.nsem)), 0, self.nsem]
                self.slots.append(prim.dsem[e])
            self.active.append((prim, e))
        slot = prim.dsem[e]
        slot[1] += 16
        self.eng[e].dma_start(out=out, in_=in_).then_inc(slot[0], 16)
        tok = (("d", slot[2]), slot[0], slot[1])
        for b in reads:
            b.r[tok[0]] = tok
        for b in writes:
            b.w = tok
            b.r = {}

    def barrier(self):
        toks = [(e, self.esem[e], self.cnt[e]) for e in self.esem if self.cnt[e] > 0]
        toks += [(("d", sl[2]), sl[0], sl[1]) for sl in self.slots if sl[1] > 0]
        for e in self.eng:
            for (key, sem, val) in toks:
                if key == e and e == "pe":
                    continue
                if self.waited[e].get(key, 0) >= val:
                    continue
                self.eng[e].wait_ge(sem, val)
                self.waited[e][key] = val
        for (b, e) in self.active:
            self.free_slots.setdefault(e, []).append(b.dsem.pop(e))
        self.active = []


PAIRS = [[0, 1], [2, 3], [4, 5], [6, 7]]
XN = [0]


def allgather(cx, pk, ga):
    nc = cx.nc
    XN[0] += 1
    sem = cx.es.enter_context(nc.semaphore("cc%d" % XN[0]))
    nc.gpsimd.collective_compute("AllGather", ALU.bypass, replica_groups=PAIRS, ins=[pk], outs=[ga]).then_inc(sem)
    for e in cx.eng:
        cx.eng[e].wait_ge(sem, 1)


def exchange_sb(cx, sc, src, dst, F, sel):
    nc = cx.nc
    XN[0] += 1
    pk = nc.dram_tensor("pk%d" % XN[0], [128, F], F32, kind="Internal").ap()
    ga = nc.dram_tensor("ga%d" % XN[0], [256, F], F32, kind="Internal", addr_space="Local").ap()
    gt = cx.sb(sc, "gt", [128, 2, F], F32)
    cx.dma("pool", pk[:, :], src, reads=[gt])
    cx.barrier()
    allgather(cx, pk[:, :], ga[:, :])
    cx.dma("pool", gt[:], ga.rearrange("(r p) f -> p r f", p=128), writes=[gt])
    cx.op("dve", lambda en: en.tensor_scalar(out=dst, in0=gt[:, 0, :], scalar1=sel[:, 0:1], scalar2=None, op0=ALU.mult),
          reads=[gt, sel], writes=[gt])
    cx.op("dve", lambda en: en.scalar_tensor_tensor(out=dst, in0=gt[:, 1, :], scalar=sel[:, 1:2], in1=dst,
                                                    op0=ALU.mult, op1=ALU.add), reads=[gt, sel], writes=[gt])
    cx.barrier()


def load_weight(cx, stg, dst, col_off, w_ap, ncols, g=None, kchunks=8):
    for kc in range(kchunks):
        s = stg[kc % len(stg)]
        cx.dma("sp", s[:, 0:ncols], w_ap[kc * 128:(kc + 1) * 128, :], writes=[s])
        e = ("dve", "pool", "act")[kc % 3]
        if e == "act":
            if g is not None:
                cx.op("act", lambda en, kc=kc, s=s: en.activation(
                    out=dst[:, kc, col_off:col_off + ncols], in_=s[:, 0:ncols], func=AF.Copy,
                    scale=g[0][:, g[1], kc:kc + 1]), reads=[s, g[0]], writes=[dst])
            else:
                cx.op("act", lambda en, kc=kc, s=s: en.activation(
                    out=dst[:, kc, col_off:col_off + ncols], in_=s[:, 0:ncols], func=AF.Copy),
                    reads=[s], writes=[dst])
            continue
        if g is not None:
            cx.op(e, lambda en, kc=kc, s=s: en.tensor_scalar(
                out=dst[:, kc, col_off:col_off + ncols], in0=s[:, 0:ncols], scalar1=g[0][:, g[1], kc:kc + 1],
                scalar2=1.0, op0=ALU.mult, op1=ALU.mult), reads=[s, g[0]], writes=[dst])
        else:
            cx.op(e, lambda en, kc=kc, s=s: en.tensor_copy(
                out=dst[:, kc, col_off:col_off + ncols], in_=s[:, 0:ncols]), reads=[s], writes=[dst])


def norm_tile(cx, xt, junk, ssq, std, rstd, xn):
    cx.op("act", lambda en: en.activation(out=junk[:], in_=xt[:], func=AF.Square), reads=[xt], writes=[junk])
    cx.op("dve", lambda en: en.tensor_reduce(out=ssq[:], in_=junk[:], axis=AX.X, op=ALU.add), reads=[junk], writes=[ssq])
    cx.op("act", lambda en: en.activation(out=std[:], in_=ssq[:], func=AF.Sqrt, scale=1.0 / D, bias=EPS_AP[0][:]),
          reads=[ssq], writes=[std])
    cx.op("dve", lambda en: en.reciprocal(out=rstd[:], in_=std[:]), reads=[std], writes=[rstd])
    cx.op("pool", lambda en: en.tensor_scalar(out=xn[:], in0=xt[:], scalar1=rstd[:, 0:1], scalar2=1.0,
                                               op0=ALU.mult, op1=ALU.mult), reads=[xt, rstd], writes=[xn])


EPS_AP = [None]


def transpose8(cx, src, tp, dst, ident, evac_eng="dve", ncol=8, rev=None):
    for kc in range(ncol):
        if rev is None:
            cx.op("pe", lambda en, kc=kc: en.transpose(out=tp[:, kc * 128:(kc + 1) * 128],
                                                       in_=src[:, kc * 128:(kc + 1) * 128], identity=ident[:]),
                  reads=[src, ident], writes=[tp])
    if evac_eng == "act_copy":
        cx.op("act", lambda en: en.activation(out=dst[:, 0:ncol, :],
                                              in_=tp[:, 0:ncol * 128].rearrange("p (c t) -> p c t", t=128),
                                              func=AF.Copy), reads=[tp], writes=[dst])
    else:
        cx.op(evac_eng, lambda en: en.tensor_copy(out=dst[:, 0:ncol, :],
                                                  in_=tp[:, 0:ncol * 128].rearrange("p (c t) -> p c t", t=128)),
              reads=[tp], writes=[dst])


def attn_group_pass(cx, T, g, d, x_src, cs_src, w_in, gvec, part, consts, halo_x=None, halo_cs=None):
    n = T // (128 * d)
    ident, maskb, ones8, flag8 = consts["ident"], consts["maskb"], consts["ones8"], consts["flag8"]
    with ExitStack() as sc:
        wt = cx.sb(sc, "wt", [128, 8, 3072], BF16)
        stg = [cx.sb(sc, "stg", [128, 1024], F32) for _ in range(4)]
        for j in range(3):
            col = (j * 3 + g) * 1024
            load_weight(cx, stg, wt, j * 1024, w_in[:, col:col + 1024], 1024, g=gvec)
        xt = [cx.sb(sc, "xt", [128, D], F32) for _ in range(2)]
        cst = [cx.sb(sc, "cst", [128, 32], F32) for _ in range(3)]
        junk = cx.sb(sc, "junk", [128, D], F32)
        ssq = [cx.sb(sc, "ssq", [128, 1], F32) for _ in range(2)]
        std = [cx.sb(sc, "std", [128, 1], F32) for _ in range(2)]
        rstd = [cx.sb(sc, "rstd", [128, 1], F32) for _ in range(2)]
        xn = [cx.sb(sc, "xn", [128, D], BF16) for _ in range(2)]
        xnT = [cx.sb(sc, "xnT", [128, 8, 128], BF16) for _ in range(2)]
        qk = [cx.sb(sc, "qk", [128, 2048], F32) for _ in range(2)]
        qkr = [cx.sb(sc, "qkr", [128, 2048], BF16) for _ in range(2)]
        rt = [cx.sb(sc, "rt", [128, 4, 16, 16], F32) for _ in range(2)]
        QT = [cx.sb(sc, "QT", [128, 8, 128], BF16) for _ in range(3)]
        KT = [cx.sb(sc, "KT", [128, 8, 128], BF16) for _ in range(4)]
        Vp = [cx.sb(sc, "Vp", [128, 8, 132], BF16) for _ in range(4)]
        PT = [cx.sb(sc, "PT", [128, 384], BF16) for _ in range(2)]
        osb = [cx.sb(sc, "osb", [128, 8, 129], F32) for _ in range(2)]
        pj = [cx.ps(sc, "pj", [128, 512], F32) for _ in range(2)]
        tp = cx.ps(sc, "tp", [128, 1024], BF16)
        scp = [cx.ps(sc, "scp", [128, 512], F32) for _ in range(2)]
        po = [cx.ps(sc, "po", [128, 512], F32) for _ in range(3)]

        items = [(r, i) for r in range(d) for i in range(n + 1)]
        NI = len(items)

        def views(r):
            return (x_src.rearrange("(l d) f -> d l f", d=d)[r], cs_src.rearrange("(l d) f -> d l f", d=d)[r],
                    part.rearrange("(l d) f -> d l f", d=d)[r])

        def a_norm(t):
            r, i = items[t]
            s2 = t % 2
            xv, cv, _ = views(r)
            if i == n and halo_x is not None:
                u0 = 2047 - r - 127 * d
                cx.dma("sp", xt[s2][:], bass.AP(halo_x.tensor, halo_x.offset + u0 * D, [[d * D, 128], [1, D]]),
                       writes=[xt[s2]])
                cx.dma("sp", cst[t % 3][:], bass.AP(halo_cs.tensor, halo_cs.offset + u0 * 32, [[d * 32, 128], [1, 32]]),
                       writes=[cst[t % 3]])
            else:
                cx.dma("sp", xt[s2][:], xv[i * 128:(i + 1) * 128, :], writes=[xt[s2]])
                cx.dma("sp", cst[t % 3][:], cv[i * 128:(i + 1) * 128, :], writes=[cst[t % 3]])
            norm_tile(cx, xt[s2], junk, ssq[s2], std[s2], rstd[s2], xn[s2])

        def a_tr(t):
            transpose8(cx, xn[t % 2], tp, xnT[t % 2], ident)

        def proj(t):
            r, i = items[t]
            s2 = t % 2
            k4 = t % 4
            for c in range(6):
                if i == n and c < 2:
                    continue
                pb = pj[c % 2]
                for kc in range(8):
                    cx.op("pe", lambda en, kc=kc, c=c, pb=pb: en.matmul(
                        pb[:, :], lhsT=xnT[s2][:, kc, :], rhs=wt[:, kc, c * 512:(c + 1) * 512],
                        start=(kc == 0), stop=(kc == 7)), reads=[xnT[s2], wt], writes=[pb])
                if c < 4:
                    if c % 2 == 0:
                        cx.op("act", lambda en, c=c, pb=pb: en.activation(
                            out=qk[s2][:, c * 512:(c + 1) * 512], in_=pb[:, :], func=AF.Copy),
                            reads=[pb], writes=[qk[s2]])
                    else:
                        cx.op("dve", lambda en, c=c, pb=pb: en.tensor_copy(
                            out=qk[s2][:, c * 512:(c + 1) * 512], in_=pb[:, :]),
                            reads=[pb], writes=[qk[s2]])
                else:
                    h0 = (c - 4) * 4
                    cx.op("act", lambda en, pb=pb, h0=h0: en.activation(
                        out=Vp[k4][:, h0:h0 + 4, 0:128], in_=pb[:, :].rearrange("p (h e) -> p h e", e=128),
                        func=AF.Copy), reads=[pb], writes=[Vp[k4]])
            vsrc = flag8 if i == n else ones8
            cx.op("pool", lambda en, vsrc=vsrc: en.tensor_copy(out=Vp[k4][:, :, 128:129], in_=vsrc[:, :, 0:1]),
                  reads=[vsrc], writes=[Vp[k4]])

        def rope(t):
            r, i = items[t]
            s2 = t % 2
            c3 = cst[t % 3]
            lo = 8 if i == n else 0
            nh = 16 - lo
            v3 = qk[s2][:, :].rearrange("p (h e) -> p h e", e=128)
            o3 = qkr[s2][:, :].rearrange("p (h e) -> p h e", e=128)
            x1 = v3[:, lo:16, 0:16]
            x2 = v3[:, lo:16, 16:32]
            cosb = bcast(c3[:, 0:16], 0, nh)
            sinb = bcast(c3[:, 16:32], 0, nh)
            rtb = rt[s2]
            for (k_, a_, b_) in ((0, x1, cosb), (1, x2, sinb), (2, x2, cosb), (3, x1, sinb)):
                cx.op("dve", lambda en, k_=k_, a_=a_, b_=b_: en.tensor_tensor(
                    out=rtb[:, k_, lo:16, :], in0=a_, in1=b_, op=ALU.mult),
                    reads=[qk[s2], c3], writes=[rtb])
            cx.op("dve", lambda en: en.tensor_tensor(out=o3[:, lo:16, 0:16], in0=rtb[:, 0, lo:16, :],
                                                     in1=rtb[:, 1, lo:16, :], op=ALU.subtract),
                  reads=[rtb], writes=[qkr[s2]])
            cx.op("dve", lambda en: en.tensor_tensor(out=o3[:, lo:16, 16:32], in0=rtb[:, 2, lo:16, :],
                                                     in1=rtb[:, 3, lo:16, :], op=ALU.add),
                  reads=[rtb], writes=[qkr[s2]])
            cx.op("pool", lambda en: en.tensor_copy(out=o3[:, lo:16, 32:128], in_=v3[:, lo:16, 32:128]),
                  reads=[qk[s2]], writes=[qkr[s2]])

        def qk_tr(t):
            r, i = items[t]
            s2 = t % 2
            if i < n:
                for h in range(8):
                    cx.op("pe", lambda en, h=h: en.transpose(out=tp[:, h * 128:(h + 1) * 128],
                                                             in_=qkr[s2][:, h * 128:(h + 1) * 128],
                                                             identity=ident[:]),
                          reads=[qkr[s2], ident], writes=[tp])
                cx.op("dve", lambda en: en.tensor_copy(out=QT[t % 3][:, :, :],
                                                       in_=tp[:, :].rearrange("p (c t) -> p c t", t=128)),
                      reads=[tp], writes=[QT[t % 3]])
            for h in range(8):
                cx.op("pe", lambda en, h=h: en.transpose(out=tp[:, h * 128:(h + 1) * 128],
                                                         in_=qkr[s2][:, 1024 + h * 128:1024 + (h + 1) * 128],
                                                         identity=ident[:]),
                      reads=[qkr[s2], ident], writes=[tp])
            cx.op("act", lambda en: en.activation(out=KT[t % 4][:, :, :],
                                                  in_=tp[:, :].rearrange("p (c t) -> p c t", t=128), func=AF.Copy),
                  reads=[tp], writes=[KT[t % 4]])

        def att(t):
            r, j = items[t]
            if j >= n:
                return
            _, _, pv = views(r)
            blocks = []
            if j >= 1:
                blocks.append(((t - 1) % 4, 0))
            blocks.append((t % 4, 1))
            blocks.append(((t + 1) % 4, 3 if (j + 1 == n and halo_x is not None) else 2))
            nb = len(blocks)
            ob = osb[t % 2]
            qt = QT[t % 3]

            def scores(h):
                sp_ = scp[h % 2]
                for bi, (ks, mi) in enumerate(blocks):
                    cx.op("pe", lambda en, bi=bi, ks=ks, h=h, sp_=sp_: en.matmul(
                        sp_[:, bi * 128:(bi + 1) * 128], lhsT=KT[ks][:, h, :], rhs=qt[:, h, :],
                        start=True, stop=False), reads=[KT[ks], qt], writes=[sp_])
                    cx.op("pe", lambda en, bi=bi, mi=mi, sp_=sp_: en.matmul(
                        sp_[:, bi * 128:(bi + 1) * 128], lhsT=ident[:], rhs=maskb[:, mi, :],
                        start=False, stop=True), reads=[ident, maskb], writes=[sp_])
                cx.op("act", lambda en, sp_=sp_, h=h: en.activation(
                    out=PT[h % 2][:, 0:nb * 128], in_=sp_[:, 0:nb * 128], func=AF.Exp, scale=ATT_SCALE),
                    reads=[sp_], writes=[PT[h % 2]])

            def pvm(h):
                pt_ = PT[h % 2]
                pob = po[h // 3]
                off = (h % 3) * 129
                for bi, (ks, mi) in enumerate(blocks):
                    cx.op("pe", lambda en, bi=bi, ks=ks, h=h, pob=pob, off=off, pt_=pt_: en.matmul(
                        pob[:, off:off + 129], lhsT=pt_[:, bi * 128:(bi + 1) * 128], rhs=Vp[ks][:, h, 0:129],
                        start=(bi == 0), stop=(bi == nb - 1)), reads=[pt_, Vp[ks]], writes=[pob])

            scores(0)
            for h in range(8):
                if h + 1 < 8:
                    scores(h + 1)
                pvm(h)
            for bk in range(3):
                nhb = 3 if bk < 2 else 2
                if bk != 1:
                    cx.op("dve", lambda en, bk=bk, nhb=nhb: en.tensor_copy(
                        out=ob[:, bk * 3:bk * 3 + nhb, :],
                        in_=po[bk][:, 0:nhb * 129].rearrange("p (h e) -> p h e", e=129)),
                        reads=[po[bk]], writes=[ob])
                else:
                    cx.op("act", lambda en, bk=bk, nhb=nhb: en.activation(
                        out=ob[:, bk * 3:bk * 3 + nhb, :],
                        in_=po[bk][:, 0:nhb * 129].rearrange("p (h e) -> p h e", e=129), func=AF.Copy),
                        reads=[po[bk]], writes=[ob])
            cx.dma("pool", pv[j * 128:(j + 1) * 128, :], ob[:, :, :].rearrange("p h e -> p (h e)"), reads=[ob])

        a_norm(0)
        a_tr(0)
        for t in range(NI):
            if t + 1 < NI:
                a_norm(t + 1)
            proj(t)
            if t + 1 < NI:
                a_tr(t + 1)
            rope(t)
            if t - 2 >= 0:
                att(t - 2)
            qk_tr(t)
        att(NI - 2)
        att(NI - 1)
        cx.barrier()


def attn_final_pass(cx, T, x_src, parts, w_in, w_out, gvec, x_dst, consts, final_g=None):
    ident = consts["ident"]
    nt = T // 128
    with ExitStack() as sc:
        wg = cx.sb(sc, "wg", [128, 8, 1024], BF16)
        wo = cx.sb(sc, "wo", [128, 8, 1024], BF16)
        stg = [cx.sb(sc, "stg", [128, 1024], F32) for _ in range(4)]
        load_weight(cx, stg, wg, 0, w_in[:, 9216:10240], 1024, g=gvec)
        load_weight(cx, stg, wo, 0, w_out[:, :], 1024, g=None)
        xt = [cx.sb(sc, "xt", [128, D], F32) for _ in range(3)]
        pp = [[cx.sb(sc, "pp", [128, 8, 129], F32) for _ in range(3)] for _ in range(2)]
        junk = cx.sb(sc, "junk", [128, D], F32)
        junk2 = cx.sb(sc, "junk2", [128, D], F32)
        ssq = [cx.sb(sc, "ssq", [128, 1], F32) for _ in range(2)]
        std = [cx.sb(sc, "std", [128, 1], F32) for _ in range(2)]
        rstd = [cx.sb(sc, "rstd", [128, 1], F32) for _ in range(2)]
        xn = [cx.sb(sc, "xn", [128, D], BF16) for _ in range(2)]
        xnT = [cx.sb(sc, "xnT", [128, 8, 128], BF16) for _ in range(2)]
        sg = [cx.sb(sc, "sg", [128, D], F32) for _ in range(2)]
        den = [cx.sb(sc, "den", [128, 8, 1], F32) for _ in range(2)]
        rden = [cx.sb(sc, "rden", [128, 8, 1], F32) for _ in range(2)]
        o = [cx.sb(sc, "o", [128, 8, 128], F32) for _ in range(2)]
        og = [cx.sb(sc, "og", [128, D], BF16) for _ in range(2)]
        ogT = [cx.sb(sc, "ogT", [128, 8, 128], BF16) for _ in range(2)]
        xo = [cx.sb(sc, "xo", [128, D], F32) for _ in range(2)]
        yo = [cx.sb(sc, "yo", [128, D], F32) for _ in range(2)]
        ssq2 = [cx.sb(sc, "ssq2", [128, 1], F32) for _ in range(2)]
        std2 = [cx.sb(sc, "std2", [128, 1], F32) for _ in range(2)]
        rstd2 = [cx.sb(sc, "rstd2", [128, 1], F32) for _ in range(2)]
        pj = [cx.ps(sc, "pj", [128, 512], F32) for _ in range(4)]
        tp = [cx.ps(sc, "tp", [128, 1024], BF16) for _ in range(2)]

        def a_norm(i):
            s2 = i % 2
            rows = slice(i * 128, (i + 1) * 128)
            cx.dma("sp", xt[i % 3][:], x_src[rows, :], writes=[xt[i % 3]])
            for g in range(3):
                cx.dma("sp", pp[s2][g][:, :, :].rearrange("p h e -> p (h e)"), parts[g][rows, :], writes=[pp[s2][g]])
            norm_tile(cx, xt[i % 3], junk, ssq[s2], std[s2], rstd[s2], xn[s2])

        def a_tr(i):
            transpose8(cx, xn[i % 2], tp[0], xnT[i % 2], ident)

        def gate(i):
            s2 = i % 2
            for c in range(2):
                pb = pj[c]
                for kc in range(8):
                    cx.op("pe", lambda en, kc=kc, c=c, pb=pb: en.matmul(
                        pb[:, :], lhsT=xnT[s2][:, kc, :], rhs=wg[:, kc, c * 512:(c + 1) * 512],
                        start=(kc == 0), stop=(kc == 7)), reads=[xnT[s2], wg], writes=[pb])
                cx.op("act", lambda en, c=c, pb=pb: en.activation(out=sg[s2][:, c * 512:(c + 1) * 512], in_=pb[:, :],
                                                                 func=AF.Silu), reads=[pb], writes=[sg[s2]])

        def chain(i):
            s2 = i % 2
            p0, p1, p2 = pp[s2]
            cx.op("pool", lambda en: en.tensor_tensor(out=p0[:, :, :], in0=p0[:, :, :], in1=p1[:, :, :], op=ALU.add),
                  reads=[p0, p1], writes=[p0])
            cx.op("dve", lambda en: en.tensor_tensor(out=p0[:, :, :], in0=p0[:, :, :], in1=p2[:, :, :], op=ALU.add),
                  reads=[p0, p2], writes=[p0])
            cx.op("dve", lambda en: en.tensor_scalar(out=den[s2][:, :, :], in0=p0[:, :, 128:129], scalar1=1e-30,
                                                     scalar2=None, op0=ALU.max), reads=[p0], writes=[den[s2]])
            cx.op("dve", lambda en: en.reciprocal(out=rden[s2][:, :, :], in_=den[s2][:, :, :]),
                  reads=[den[s2]], writes=[rden[s2]])
            cx.op("dve", lambda en: en.tensor_tensor(out=o[s2][:, :, :], in0=p0[:, :, 0:128],
                                                     in1=bcast(rden[s2][:, :, 0], 1, 128), op=ALU.mult),
                  reads=[p0, rden[s2]], writes=[o[s2]])
            cx.op("pool", lambda en: en.tensor_tensor(out=og[s2][:, :], in0=o[s2][:, :, :].rearrange("p h e -> p (h e)"),
                                                      in1=sg[s2][:, :], op=ALU.mult),
                  reads=[o[s2], sg[s2]], writes=[og[s2]])

        def out(i):
            s2 = i % 2
            rows = slice(i * 128, (i + 1) * 128)
            x3 = xt[i % 3]
            transpose8(cx, og[s2], tp[1], ogT[s2], ident, evac_eng="act_copy")
            for c in range(2):
                pb = pj[2 + c]
                for kc in range(8):
                    cx.op("pe", lambda en, kc=kc, c=c, pb=pb: en.matmul(
                        pb[:, :], lhsT=ogT[s2][:, kc, :], rhs=wo[:, kc, c * 512:(c + 1) * 512],
                        start=(kc == 0), stop=(kc == 7)), reads=[ogT[s2], wo], writes=[pb])
                cx.op("dve", lambda en, c=c, pb=pb: en.tensor_tensor(
                    out=xo[s2][:, c * 512:(c + 1) * 512], in0=pb[:, :], in1=x3[:, c * 512:(c + 1) * 512],
                    op=ALU.add), reads=[pb, x3], writes=[xo[s2]])
            if final_g is None:
                cx.dma("pool", x_dst[rows, :], xo[s2][:], reads=[xo[s2]])
            else:
                cx.op("act", lambda en: en.activation(out=junk2[:], in_=xo[s2][:], func=AF.Square),
                      reads=[xo[s2]], writes=[junk2])
                cx.op("dve", lambda en: en.tensor_reduce(out=ssq2[s2][:], in_=junk2[:], axis=AX.X, op=ALU.add),
                      reads=[junk2], writes=[ssq2[s2]])
                cx.op("act", lambda en: en.activation(out=std2[s2][:], in_=ssq2[s2][:], func=AF.Sqrt, scale=1.0 / D,
                                                      bias=EPS_AP[0][:]), reads=[ssq2[s2]], writes=[std2[s2]])
                cx.op("dve", lambda en: en.reciprocal(out=rstd2[s2][:], in_=std2[s2][:]), reads=[std2[s2]], writes=[rstd2[s2]])
                cx.op("dve", lambda en: en.scalar_tensor_tensor(out=yo[s2][:], in0=xo[s2][:], scalar=rstd2[s2][:, 0:1],
                                                                in1=final_g[:, :], op0=ALU.mult, op1=ALU.mult),
                      reads=[xo[s2], rstd2[s2], final_g], writes=[yo[s2]])
                cx.dma("pool", x_dst[rows, :], yo[s2][:], reads=[yo[s2]])

        a_norm(0)
        a_tr(0)
        for i in range(nt):
            if i + 1 < nt:
                a_norm(i + 1)
            gate(i)
            if i + 1 < nt:
                a_tr(i + 1)
            if i >= 1:
                out(i - 1)
            chain(i)
        out(nt - 1)
        cx.barrier()


def build(T, NL=4):
    nc = bass.Bass("TRN2", target_bir_lowering=False)
    TH = T + HALO
    din = {}

    def inp(name, shape, dt=F32):
        din[name] = nc.dram_tensor(name, shape, dt, kind="ExternalInput").ap()
        return din[name]

    x0 = inp("x0", [TH, D])
    cs = inp("cs", [TH, 32])
    a_w_in = inp("a_w_in", [2, D, 10240])
    a_w_out = inp("a_w_out", [2, D, D])
    norm_gT = inp("norm_gT", [128, 4, 8])
    final_gb = inp("final_gb", [128, D])
    ident_d = inp("ident", [128, 128], BF16)
    maskb_d = inp("maskb", [128, 4, 128], BF16)
    flag8_d = inp("flag8", [128, 8, 1])
    inp("b_w_in", [D, 2048])
    inp("b_w_out", [D, D])
    inp("b_w_r", [2, 8, 128, 128])
    inp("b_w_i", [2, 8, 128, 128])
    inp("b_b_rT", [128, 2, 8])
    inp("b_b_iT", [128, 2, 8])
    inp("b_lamT", [128, 2, 8])
    inp("b_w5T", [128, 5, 8])
    inp("b_cbT", [128, 8])
    inp("c_w_in", [D, 5120])
    inp("c_w_out", [D, D])
    inp("c_lbT", [128, 4, 16])
    inp("c_gnb", [128, 128])
    cmask_d = inp("cmask", [128, 2, 128], BF16)
    sel_d = inp("sel", [128, 2])
    cs_h3 = inp("cs_h3", [2048, 32])
    y = nc.dram_tensor("y", [T, D], F32, kind="ExternalOutput").ap()
    scr = {}
    scr["of"] = nc.dram_tensor("of", [T, D], F32, kind="Internal").ap()
    scr["xbT"] = nc.dram_tensor("xbT", [8, 128, T + 4], F32, kind="Internal").ap()
    scr["sgT"] = nc.dram_tensor("sgT", [8, 128, T], F32, kind="Internal").ap()
    scr["xp"] = nc.dram_tensor("xp", [T, D], F32, kind="Internal").ap()
    xs = [nc.dram_tensor("xs%d" % i, [TH, D], F32, kind="Internal").ap() for i in range(2)]
    parts = [nc.dram_tensor("part%d" % i, [T, 8 * 129], F32, kind="Internal").ap() for i in range(3)]

    with ExitStack() as es:
        cx = Ctx(nc, es)
        consts = {}
        consts["ident"] = cx.sb(es, "ident", [128, 128], BF16)
        consts["maskb"] = cx.sb(es, "maskb", [128, 4, 128], BF16)
        consts["ones8"] = cx.sb(es, "ones8", [128, 8, 1], F32)
        consts["flag8"] = cx.sb(es, "flag8", [128, 8, 1], F32)
        ngT = cx.sb(es, "ngT", [128, 4, 8], F32)
        epsb = cx.sb(es, "epsb", [128, 1], F32)
        EPS_AP[0] = epsb
        cx.dma("sp", consts["ident"][:], ident_d[:, :], writes=[consts["ident"]])
        cx.dma("sp", consts["maskb"][:], maskb_d[:, :, :], writes=[consts["maskb"]])
        cx.dma("sp", consts["flag8"][:], flag8_d[:, :, :], writes=[consts["flag8"]])
        cx.dma("sp", ngT[:], norm_gT[:, :, :], writes=[ngT])
        consts["sel"] = cx.sb(es, "sel", [128, 2], F32)
        cx.dma("sp", consts["sel"][:], sel_d[:, :], writes=[consts["sel"]])
        consts["cmask"] = cx.sb(es, "cmask", [128, 2, 128], BF16)
        cx.dma("sp", consts["cmask"][:], cmask_d[:, :, :], writes=[consts["cmask"]])
        cx.op("dve", lambda en: en.memset(consts["ones8"][:], 1.0), writes=[consts["ones8"]])
        cx.op("dve", lambda en: en.memset(epsb[:], EPS), writes=[epsb])

        last = (NL == 1)
        for (g, d) in GROUPS:
            attn_group_pass(cx, T, g, d, x0, cs, a_w_in[0], (ngT, 0), parts[g], consts)
        attn_final_pass(cx, T, x0, parts, a_w_in[0], a_w_out[0], (ngT, 0), y if last else xs[0], consts,
                        final_g=None)
        if NL >= 2:
            layer_b(cx, T, xs[0], y if NL == 2 else xs[1], din, ngT, 1, scr, consts)
        if NL >= 3:
            layer_c(cx, T, xs[1], y if NL == 3 else xs[0], din, ngT, 2, scr, consts)
        if NL >= 4:
            x3 = xs[0]
            pk3 = nc.dram_tensor("pkx3", [1024, D], F32, kind="Internal").ap()
            ga3 = nc.dram_tensor("gax3", [2048, D], F32, kind="Internal", addr_space="Local").ap()
            halo3 = nc.dram_tensor("halo3", [2048, D], F32, kind="Internal").ap()
            with ExitStack() as s3:
                g0 = [cx.sb(s3, "g0", [128, D], F32) for _ in range(2)]
                g1 = [cx.sb(s3, "g1", [128, D], F32) for _ in range(2)]
                zt = cx.sb(s3, "zt3", [128, D], F32)
                cx.op("dve", lambda en: en.memset(zt[:], 0.0), writes=[zt])
                for i in range(8):
                    a = g0[i % 2]
                    cx.dma("sp", a[:], x3[T - 1024 + i * 128:T - 1024 + (i + 1) * 128, :], writes=[a])
                    cx.dma("pool", pk3[i * 128:(i + 1) * 128, :], a[:], reads=[a])
                    cx.dma("pool", halo3[i * 128:(i + 1) * 128, :], zt[:], reads=[zt])
                cx.barrier()
                for i in range(8):
                    allgather(cx, pk3[i * 128:(i + 1) * 128, :], ga3[i * 256:(i + 1) * 256, :])
                sel = consts["sel"]
                for i in range(8):
                    a, b = g0[i % 2], g1[i % 2]
                    cx.dma("sp", a[:], ga3[i * 256:i * 256 + 128, :], writes=[a])
                    cx.dma("sp", b[:], ga3[i * 256 + 128:(i + 1) * 256, :], writes=[b])
                    cx.op("dve", lambda en, a=a: en.tensor_scalar(out=a[:], in0=a[:], scalar1=sel[:, 0:1], scalar2=None,
                                                                 op0=ALU.mult), reads=[a, sel], writes=[a])
                    cx.op("dve", lambda en, a=a, b=b: en.scalar_tensor_tensor(out=a[:], in0=b[:], scalar=sel[:, 1:2], in1=a[:],
                                                                            op0=ALU.mult, op1=ALU.add),
                          reads=[a, b, sel], writes=[a])
                    cx.dma("pool", halo3[1024 + i * 128:1024 + (i + 1) * 128, :], a[:], reads=[a])
                cx.barrier()
            for (g, d) in GROUPS:
                attn_group_pass(cx, T, g, d, x3, cs, a_w_in[1], (ngT, 3), parts[g], consts, halo_x=halo3, halo_cs=cs_h3)
            with ExitStack() as s4:
                fgb = cx.sb(s4, "fgb", [128, D], F32)
                cx.dma("sp", fgb[:], final_gb[:, :], writes=[fgb])
                attn_final_pass(cx, T, x3, parts, a_w_in[1], a_w_out[1], (ngT, 3), y, consts, final_g=fgb)
        cx.barrier()
    return nc


def fm(v):
    v = np.asarray(v, np.float32)
    lead = v.shape[:-1]
    return np.ascontiguousarray(np.moveaxis(v.reshape(lead + (8, 128)), -1, 0))


def const_inputs():
    ident = np.eye(128, dtype=np.float32).astype(ml_dtypes.bfloat16)
    p = np.arange(128)[:, None]
    f = np.arange(128)[None, :]
    NEG = -30000.0
    m = np.zeros((128, 4, 128), np.float32)
    m[:, 3, :] = np.where((127 - p) <= f - 64, 0.0, NEG)
    m[:, 0, :] = np.where(p >= f + 64, 0.0, NEG)
    m[:, 1, :] = np.where(np.abs(f - p) <= 64, 0.0, NEG)
    m[:, 2, :] = np.where(p <= f - 64, 0.0, NEG)
    cm = np.zeros((128, 2, 128), np.float32)
    cm[:, 0, :] = (p <= f)
    cm[:, 1, :] = (p >= f)
    return {"ident": ident, "maskb": m.astype(ml_dtypes.bfloat16), "cmask": cm.astype(ml_dtypes.bfloat16)}


def rope_table(pos):
    half = 16
    inv = (np.float32(500000.0) ** (-(np.arange(half, dtype=np.float32) * np.float32(2.0)) / np.float32(32.0))).astype(np.float32)
    ang = pos.astype(np.float32)[:, None] * inv[None, :]
    return np.concatenate([np.cos(ang), np.sin(ang)], axis=1).astype(np.float32)


def core_inputs(x_ext, pos, flag, W, reverse, sel=(0.0, 0.0), pos_h3=None):
    d = dict(const_inputs())
    d["x0"] = np.ascontiguousarray(x_ext, np.float32)
    d["cs"] = rope_table(pos)
    d["a_w_in"] = W["a_w_in"]
    d["a_w_out"] = W["a_w_out"]
    d["norm_gT"] = fm(W["norm_g"])
    d["final_gb"] = np.ascontiguousarray(np.broadcast_to(W["final_g"][None, :], (128, D)), np.float32)
    d["flag8"] = np.full((128, 8, 1), flag, np.float32)
    dd = [1, 0] if reverse else [0, 1]
    d["b_w_in"] = W["b_w_in"][0]
    d["b_w_out"] = W["b_w_out"][0]
    d["b_w_r"] = np.ascontiguousarray(W["b_w_r"][0][dd])
    d["b_w_i"] = np.ascontiguousarray(W["b_w_i"][0][dd])
    d["b_b_rT"] = fm(W["b_b_r"][0][dd])
    d["b_b_iT"] = fm(W["b_b_i"][0][dd])
    d["b_lamT"] = fm(W["b_lambda"][0][dd])
    cw = W["b_conv_w"][0]
    z = np.zeros((1, D), np.float32)
    w5 = np.concatenate([cw, z], 0) if not reverse else np.concatenate([z, cw[::-1]], 0)
    d["b_w5T"] = fm(w5)
    d["b_cbT"] = fm(W["b_conv_b"][0])
    cw_in = W["c_w_in"][0]
    if reverse:
        cw_in = np.concatenate([cw_in[:, 0:1024], cw_in[:, 2048:3072], cw_in[:, 1024:2048], cw_in[:, 3072:]], 1)
    d["c_w_in"] = np.ascontiguousarray(cw_in)
    d["c_w_out"] = W["c_w_out"][0]
    d["c_lbT"] = fm(W["c_lower_bounds"][:, dd, :]).reshape(128, 4, 16)
    d["sel"] = np.ascontiguousarray(np.broadcast_to(np.asarray(sel, np.float32)[None, :], (128, 2)))
    d["cs_h3"] = rope_table(pos_h3 if pos_h3 is not None else np.zeros(2048))
    d["c_gnb"] = np.ascontiguousarray(np.broadcast_to(W["c_gnorm_g"][0][None, :], (128, 128)), np.float32)
    return d


def rglru_stage1(cx, T, x_src, w_in, gl, xbT, sgT, consts):
    ident = consts["ident"]
    NT = 512
    with ExitStack() as sc:
        wt = cx.sb(sc, "wt", [128, 8, 2048], BF16)
        stg = [cx.sb(sc, "stg", [128, 2048], F32) for _ in range(3)]
        load_weight(cx, stg, wt, 0, w_in[:, :], 2048, g=gl)
        xt = [cx.sb(sc, "xt", [128, D], F32) for _ in range(2)]
        junk = cx.sb(sc, "junk", [128, D], F32)
        ssq = [cx.sb(sc, "ssq", [128, 1], F32) for _ in range(2)]
        std = [cx.sb(sc, "std", [128, 1], F32) for _ in range(2)]
        rstd = [cx.sb(sc, "rstd", [128, 1], F32) for _ in range(2)]
        xn = [cx.sb(sc, "xn", [128, D], BF16) for _ in range(2)]
        xnT = [cx.sb(sc, "xnT", [128, 8, NT], BF16) for _ in range(2)]
        ob = [cx.sb(sc, "ob", [128, NT], F32) for _ in range(4)]
        zt = cx.sb(sc, "zt", [128, 8, 2], F32)
        pj = [cx.ps(sc, "pj", [128, 512], F32) for _ in range(2)]
        tp = [cx.ps(sc, "tp", [128, 1024], BF16) for _ in range(2)]
        cx.op("dve", lambda en: en.memset(zt[:], 0.0), writes=[zt])
        cx.dma("pool", xbT[:, :, 0:2].rearrange("b p c -> p b c"), zt[:, :, :], reads=[zt])
        cx.dma("pool", xbT[:, :, T + 2:T + 4].rearrange("b p c -> p b c"), zt[:, :, :], reads=[zt])
        k = 0
        for ci in range(T // NT):
            t0 = ci * NT
            xs_ = xnT[ci % 2]
            for tt in range(4):
                s2 = k % 2
                k += 1
                cx.dma("sp", xt[s2][:], x_src[t0 + tt * 128:t0 + (tt + 1) * 128, :], writes=[xt[s2]])
                norm_tile(cx, xt[s2], junk, ssq[s2], std[s2], rstd[s2], xn[s2])
                tpb = tp[k % 2]
                for kc in range(8):
                    cx.op("pe", lambda en, kc=kc, tpb=tpb, s2=s2: en.transpose(
                        out=tpb[:, kc * 128:(kc + 1) * 128], in_=xn[s2][:, kc * 128:(kc + 1) * 128],
                        identity=ident[:]), reads=[xn[s2], ident], writes=[tpb])
                cx.op("dve", lambda en, tpb=tpb, tt=tt: en.tensor_copy(
                    out=xs_[:, :, tt * 128:(tt + 1) * 128], in_=tpb[:, :].rearrange("p (c t) -> p c t", t=128)),
                    reads=[tpb], writes=[xs_])
            for cb in range(16):
                pb = pj[cb % 2]
                o_ = ob[cb % 4]
                for kc in range(8):
                    cx.op("pe", lambda en, kc=kc, cb=cb, pb=pb: en.matmul(
                        pb[:, :], lhsT=wt[:, kc, cb * 128:(cb + 1) * 128], rhs=xs_[:, kc, :],
                        start=(kc == 0), stop=(kc == 7)), reads=[wt, xs_], writes=[pb])
                if cb < 8:
                    cx.op("dve", lambda en, pb=pb, o_=o_: en.tensor_copy(out=o_[:, :], in_=pb[:, :]),
                          reads=[pb], writes=[o_])
                    cx.dma("pool", xbT[cb, :, 2 + t0:2 + t0 + NT], o_[:, :], reads=[o_])
                else:
                    cx.op("act", lambda en, pb=pb, o_=o_: en.activation(out=o_[:, :], in_=pb[:, :], func=AF.Silu),
                          reads=[pb], writes=[o_])
                    cx.dma("pool", sgT[cb - 8, :, t0:t0 + NT], o_[:, :], reads=[o_])
        cx.barrier()


def rglru_pass(cx, T, dr, xbT, sgT, res_src, dst, P, state, consts):
    NT = 512
    RD = 4
    ones8 = consts["ones8"]
    wr, wi, wo = P["wr"], P["wi"], P["wo"]
    with ExitStack() as sc:
        def ring(name, shape, dt, k=RD):
            return [cx.sb(sc, name, shape, dt) for _ in range(k)]
        xbh = ring("xbh", [128, NT + 4], F32)
        sgc = ring("sgc", [128, NT], F32)
        xc = ring("xc", [128, NT], F32)
        xcb = ring("xcb", [128, NT], BF16)
        rr = ring("rr", [128, NT], F32)
        ii = ring("ii", [128, NT], F32)
        aa = ring("aa", [128, NT], F32)
        mm = ring("mm", [128, NT], F32)
        uu = ring("uu", [128, NT], F32)
        hh = ring("hh", [128, NT], F32)
        yg = [cx.sb(sc, "yg", [128, 8, NT], BF16) for _ in range(2)]
        rt_ = [cx.sb(sc, "rt", [128, D], F32) for _ in range(3)]
        xo = [cx.sb(sc, "xo", [128, D], F32) for _ in range(2)]
        pr = [cx.ps(sc, "pr", [128, 512], F32) for _ in range(3)]
        pi = [cx.ps(sc, "pi", [128, 512], F32) for _ in range(3)]
        pj = [cx.ps(sc, "pj", [128, 512], F32) for _ in range(2)]
        nch = T // NT
        order = list(range(nch)) if dr == 0 else list(range(nch - 1, -1, -1))
        items = [(oi, ci, cb) for oi, ci in enumerate(order) for cb in range(8)]
        NI = len(items)
        kk = [0]

        def s1(u):
            oi, ci, cb = items[u]
            t0 = ci * NT
            s = u % RD
            X, XC = xbh[s], xc[s]
            cx.dma("sp", X[:], xbT[cb, :, t0:t0 + NT + 4], writes=[X])
            cx.dma("sp", sgc[s][:], sgT[cb, :, t0:t0 + NT], writes=[sgc[s]])
            cx.op("act", lambda en: en.activation(
                out=XC[:, :], in_=X[:, 0:NT], func=AF.Identity, scale=P["w5"][:, 0, cb:cb + 1],
                bias=P["cb"][:, cb:cb + 1]), reads=[X, P["w5"], P["cb"]], writes=[XC])
            for j in range(1, 5):
                cx.op("dve", lambda en, j=j: en.scalar_tensor_tensor(
                    out=XC[:, :], in0=X[:, j:j + NT], scalar=P["w5"][:, j, cb:cb + 1], in1=XC[:, :],
                    op0=ALU.mult, op1=ALU.add), reads=[X, P["w5"], XC], writes=[XC])
            cx.op("pool", lambda en: en.tensor_copy(out=xcb[s][:, :], in_=XC[:, :]), reads=[XC], writes=[xcb[s]])
            p3 = u % 3
            cx.op("pe", lambda en: en.matmul(pr[p3][:, :], lhsT=wr[:, dr * 8 + cb, :], rhs=xcb[s][:, :],
                                             start=True, stop=True), reads=[wr, xcb[s]], writes=[pr[p3]])
            cx.op("pe", lambda en: en.matmul(pi[p3][:, :], lhsT=wi[:, dr * 8 + cb, :], rhs=xcb[s][:, :],
                                             start=True, stop=True), reads=[wi, xcb[s]], writes=[pi[p3]])

        def s2(u):
            oi, ci, cb = items[u]
            s = u % RD
            p3 = u % 3
            cx.op("act", lambda en: en.activation(out=rr[s][:, :], in_=pr[p3][:, :], func=AF.Sigmoid,
                                                  bias=P["br"][:, dr, cb:cb + 1]), reads=[pr[p3], P["br"]], writes=[rr[s]])
            cx.op("act", lambda en: en.activation(out=ii[s][:, :], in_=pi[p3][:, :], func=AF.Sigmoid,
                                                  bias=P["bi"][:, dr, cb:cb + 1]), reads=[pi[p3], P["bi"]], writes=[ii[s]])
            cx.op("act", lambda en: en.activation(out=aa[s][:, :], in_=rr[s][:, :], func=AF.Exp,
                                                  scale=P["cneg"][:, dr, cb:cb + 1]), reads=[rr[s], P["cneg"]], writes=[aa[s]])
            cx.op("act", lambda en: en.activation(out=mm[s][:, :], in_=rr[s][:, :], func=AF.Exp,
                                                  scale=P["cneg2"][:, dr, cb:cb + 1]), reads=[rr[s], P["cneg2"]], writes=[mm[s]])
            cx.op("act", lambda en: en.activation(out=mm[s][:, :], in_=mm[s][:, :], func=AF.Sqrt, scale=-1.0,
                                                  bias=ones8[:, 0, :]), reads=[mm[s], ones8], writes=[mm[s]])
            cx.op("pool", lambda en: en.tensor_tensor(out=uu[s][:, :], in0=ii[s][:, :], in1=xc[s][:, :], op=ALU.mult),
                  reads=[ii[s], xc[s]], writes=[uu[s]])
            cx.op("pool", lambda en: en.tensor_tensor(out=uu[s][:, :], in0=uu[s][:, :], in1=mm[s][:, :], op=ALU.mult),
                  reads=[uu[s], mm[s]], writes=[uu[s]])

        def s3(u):
            oi, ci, cb = items[u]
            s = u % RD
            ygc = yg[oi % 2]
            if dr == 0:
                cx.op("dve", lambda en: en.tensor_tensor_scan(
                    out=hh[s][:, :], data0=aa[s][:, :], data1=uu[s][:, :], initial=state[:, cb:cb + 1],
                    op0=ALU.mult, op1=ALU.add), reads=[aa[s], uu[s], state], writes=[hh[s]])
                lastc = NT - 1
            else:
                cx.op("dve", lambda en: en.tensor_tensor_scan(
                    out=hh[s][:, ::-1], data0=aa[s][:, ::-1], data1=uu[s][:, ::-1], initial=state[:, cb:cb + 1],
                    op0=ALU.mult, op1=ALU.add), reads=[aa[s], uu[s], state], writes=[hh[s]])
                lastc = 0
            cx.op("dve", lambda en: en.tensor_copy(out=state[:, cb:cb + 1], in_=hh[s][:, lastc:lastc + 1]),
                  reads=[hh[s]], writes=[state])
            cx.op("pool", lambda en: en.tensor_tensor(out=ygc[:, cb, :], in0=hh[s][:, :], in1=sgc[s][:, :], op=ALU.mult),
                  reads=[hh[s], sgc[s]], writes=[ygc])
            if cb == 7:
                out_chunk(oi, ci)

        def out_chunk(oi, ci):
            t0 = ci * NT
            ygc = yg[oi % 2]
            for tt in range(4):
                k = kk[0]
                kk[0] += 1
                r3 = rt_[k % 3]
                x2 = xo[k % 2]
                rows = slice(t0 + tt * 128, t0 + (tt + 1) * 128)
                cx.dma("sp", r3[:], res_src[rows, :], writes=[r3])
                for c2 in range(2):
                    pb = pj[c2]
                    for cb in range(8):
                        cx.op("pe", lambda en, cb=cb, c2=c2, pb=pb, tt=tt: en.matmul(
                            pb[:, :], lhsT=ygc[:, cb, tt * 128:(tt + 1) * 128], rhs=wo[:, cb, c2 * 512:(c2 + 1) * 512],
                            start=(cb == 0), stop=(cb == 7)), reads=[ygc, wo], writes=[pb])
                    cx.op("dve", lambda en, c2=c2, pb=pb, x2=x2, r3=r3: en.tensor_tensor(
                        out=x2[:, c2 * 512:(c2 + 1) * 512], in0=pb[:, :], in1=r3[:, c2 * 512:(c2 + 1) * 512],
                        op=ALU.add), reads=[pb, r3], writes=[x2])
                cx.dma("pool", dst[rows, :], x2[:], reads=[x2])

        for u in range(NI + 2):
            if u < NI:
                s1(u)
            if 0 <= u - 1 < NI:
                s2(u - 1)
            if 0 <= u - 2 < NI:
                s3(u - 2)
        cx.barrier()


def layer_b(cx, T, x_src, x_dst, din, ngT, layer, scr, consts):
    ones8 = consts["ones8"]
    with ExitStack() as sc:
        P = {}
        P["wr"] = cx.sb(sc, "wr", [128, 16, 128], BF16)
        P["wi"] = cx.sb(sc, "wi", [128, 16, 128], BF16)
        P["wo"] = cx.sb(sc, "wo", [128, 8, 1024], BF16)
        for nm in ("br", "bi", "lam", "cneg", "cneg2"):
            P[nm] = cx.sb(sc, nm, [128, 2, 8], F32)
        P["w5"] = cx.sb(sc, "w5", [128, 5, 8], F32)
        P["cb"] = cx.sb(sc, "cb", [128, 8], F32)
        state = cx.sb(sc, "state", [128, 8], F32)
        with ExitStack() as s2:
            stg = [cx.sb(s2, "stg", [128, 2048], F32) for _ in range(2)]
            for (nm, src) in (("wr", din["b_w_r"]), ("wi", din["b_w_i"])):
                cx.dma("sp", stg[0][:, :].rearrange("p (a o) -> p a o", o=128),
                       src.rearrange("d b c o -> c (d b) o"), writes=[stg[0]])
                cx.op("dve", lambda en, nm=nm: en.tensor_copy(out=P[nm][:, :, :],
                                                              in_=stg[0][:, :].rearrange("p (a o) -> p a o", o=128)),
                      reads=[stg[0]], writes=[P[nm]])
            load_weight(cx, stg, P["wo"], 0, din["b_w_out"][:, :], 1024, g=None)
            cx.dma("sp", P["br"][:], din["b_b_rT"][:, :, :], writes=[P["br"]])
            cx.dma("sp", P["bi"][:], din["b_b_iT"][:, :, :], writes=[P["bi"]])
            cx.dma("sp", P["lam"][:], din["b_lamT"][:, :, :], writes=[P["lam"]])
            cx.dma("sp", P["w5"][:], din["b_w5T"][:, :, :], writes=[P["w5"]])
            cx.dma("sp", P["cb"][:], din["b_cbT"][:, :], writes=[P["cb"]])
            cx.op("act", lambda en: en.activation(out=P["cneg"][:], in_=P["lam"][:], func=AF.Exp, scale=-1.0),
                  reads=[P["lam"]], writes=[P["cneg"]])
            cx.op("act", lambda en: en.activation(out=P["cneg"][:], in_=P["cneg"][:], func=AF.Ln, bias=ones8[:, 0, :]),
                  reads=[P["cneg"], ones8], writes=[P["cneg"]])
            cx.op("dve", lambda en: en.tensor_scalar(out=P["cneg2"][:], in0=P["cneg"][:], scalar1=-16.0, scalar2=None,
                                                     op0=ALU.mult), reads=[P["cneg"]], writes=[P["cneg2"]])
            cx.op("dve", lambda en: en.tensor_scalar(out=P["cneg"][:], in0=P["cneg"][:], scalar1=-8.0, scalar2=None,
                                                     op0=ALU.mult), reads=[P["cneg"]], writes=[P["cneg"]])
            cx.barrier()
        rglru_stage1(cx, T, x_src, din["b_w_in"], (ngT, layer), scr["xbT"], scr["sgT"], consts)
        sel = consts["sel"]
        state2 = cx.sb(sc, "state2", [128, 8], F32)
        with ExitStack() as s3:
            hin = cx.sb(s3, "hin", [128, 16], F32)
            hout = cx.sb(s3, "hout", [128, 16], F32)
            hsw = cx.sb(s3, "hsw", [128, 8, 2], F32)
            cx.dma("sp", hin[:, :].rearrange("p (b c) -> p b c", c=2),
                   scr["xbT"][:, :, T:T + 2].rearrange("b p c -> p b c"), writes=[hin])
            cx.barrier()
            exchange_sb(cx, s3, hin[:, :], hout[:, :], 16, sel)
            hv = hout[:, :].rearrange("p (b c) -> p b c", c=2)
            cx.op("dve", lambda en: en.tensor_copy(out=hsw[:, :, 0:1], in_=hv[:, :, 1:2]), reads=[hout], writes=[hsw])
            cx.op("dve", lambda en: en.tensor_copy(out=hsw[:, :, 1:2], in_=hv[:, :, 0:1]), reads=[hout], writes=[hsw])
            cx.dma("pool", scr["xbT"][:, :, T + 2:T + 4].rearrange("b p c -> p b c"), hsw[:, :, :], reads=[hsw])
            cx.barrier()
        cx.op("dve", lambda en: en.memset(state[:], 0.0), writes=[state])
        rglru_pass(cx, T, 0, scr["xbT"], scr["sgT"], x_src, scr["xp"], P, state, consts)
        with ExitStack() as s3:
            exchange_sb(cx, s3, state[:, :], state2[:, :], 8, sel)
        rglru_pass(cx, T, 1, scr["xbT"], scr["sgT"], scr["xp"], x_dst, P, state2, consts)


def hgrn_pass(cx, T, dr, x_src, din, gl, P, of, dst, consts, final, S):
    ident = consts["ident"]
    NT = 256
    NC = NT // 128
    w_in = din["c_w_in"]
    with ExitStack() as sc:
        ncol = 4096 if final else 3072
        wt = cx.sb(sc, "wt", [128, 8, ncol], BF16)
        if final:
            wo = cx.sb(sc, "wo", [128, 8, 1024], BF16)
        with ExitStack() as s2:
            stg = [cx.sb(s2, "stg", [128, 1024], F32) for _ in range(4)]
            zc = 1024 + 1024 * dr
            load_weight(cx, stg, wt, 0, w_in[:, 0:1024], 1024, g=gl)
            load_weight(cx, stg, wt, 1024, w_in[:, zc:zc + 1024], 1024, g=gl)
            load_weight(cx, stg, wt, 2048, w_in[:, 3072:4096], 1024, g=gl)
            if final:
                load_weight(cx, stg, wt, 3072, w_in[:, 4096:5120], 1024, g=gl)
                load_weight(cx, stg, wo, 0, din["c_w_out"][:, :], 1024, g=None)
            cx.barrier()
        Sp = cx.sb(sc, "Sp", [128, 8, 128], BF16)
        tmpS = cx.sb(sc, "tmpS", [128, 8, 128], F32)
        xt = [cx.sb(sc, "xt", [128, D], F32)] * 2
        ssq = [cx.sb(sc, "ssq", [128, 1], F32) for _ in range(2)]
        std = [cx.sb(sc, "std", [128, 1], F32) for _ in range(2)]
        rstd = [cx.sb(sc, "rstd", [128, 1], F32) for _ in range(2)]
        xn = [cx.sb(sc, "xn", [128, D], BF16)] * 2
        xnT = cx.sb(sc, "xnT", [128, 8, NT], BF16)
        nbq = 1 if final else 2
        qTs = [cx.sb(sc, "qT", [128, 8, NT], F32) for _ in range(nbq)]
        As = [cx.sb(sc, "A", [128, 8, NT], F32) for _ in range(nbq)]
        L = cx.sb(sc, "L", [128, 8, NT], F32)
        bb = cx.sb(sc, "bb", [128, 8, NT], F32)
        junk = cx.sb(sc, "junk", [128, D], BF16)
        qd = [cx.sb(sc, "qd", [128, 8, NT], BF16) for _ in range(2)]
        kd = [cx.sb(sc, "kd", [128, 8, NT], BF16) for _ in range(2)]
        vt = [cx.sb(sc, "vt", [128, NC, 1024], BF16) for _ in range(2)]
        gg = [cx.sb(sc, "gg", [128, 8 * NC], F32) for _ in range(2)]
        e2 = [cx.sb(sc, "e2", [128, 8 * NC], F32) for _ in range(2)]
        er = [cx.sb(sc, "er", [128, 8 * NC], F32) for _ in range(2)]
        attm = [cx.sb(sc, "attm", [128, 8, 128], BF16) for _ in range(2)]
        kdtok = [cx.sb(sc, "kdtok", [128, 8, 128], BF16) for _ in range(2)]
        osb = [cx.sb(sc, "osb", [128, D], F32)] * 2
        onesr = cx.sb(sc, "onesr", [128, NT], F32)
        cx.op("dve", lambda en: en.memset(onesr[:], 1.0), writes=[onesr])
        rc_ = 0 if dr == 0 else 127
        for c_ in range(NC):
            cx.op("dve", lambda en, c_=c_: en.memset(onesr[:, c_ * 128 + rc_:c_ * 128 + rc_ + 1], 0.0), writes=[onesr])
        if final:
            sgt = [cx.sb(sc, "sgt", [128, NC, 1024], F32) for _ in range(2)]
            oft = [cx.sb(sc, "oft", [128, D], F32)] * 2
            rt_ = [cx.sb(sc, "rt", [128, D], F32)] * 2
            s8 = cx.sb(sc, "s8", [128, 8], F32)
            r8 = cx.sb(sc, "r8", [128, 8], F32)
            og = [cx.sb(sc, "og", [128, D], BF16)] * 2
            ogT = [cx.sb(sc, "ogT", [128, 8, 128], BF16)] * 2
            xo = rt_
        pj = [cx.ps(sc, "pj", [128, 512], F32) for _ in range(2)]
        tp = cx.ps(sc, "tp", [128, 1024], BF16)
        scp = cx.ps(sc, "scp", [128, 512], F32)
        po = [cx.ps(sc, "po", [128, 512], F32) for _ in range(2)]
        su = [cx.ps(sc, "su", [128, 512], F32) for _ in range(2)]
        nst = T // NT
        order = list(range(nst)) if dr == 0 else list(range(nst - 1, -1, -1))
        corder = list(range(NC)) if dr == 0 else list(range(NC - 1, -1, -1))
        lastc = 127 if dr == 0 else 0
        kcnt = [0, 0]

        def X(q):
            si = order[q]
            t0 = si * NT
            vq = vt[q % 2]
            fm, tm = [], []
            qT, A = qTs[q % nbq], As[q % nbq]

            def tile_piece(tt):
                s2 = kcnt[0] % 2
                kcnt[0] += 1
                cx.dma("sp", xt[s2][:], x_src[t0 + tt * 128:t0 + (tt + 1) * 128, :], writes=[xt[s2]])
                norm_tile(cx, xt[s2], junk, ssq[s2], std[s2], rstd[s2], xn[s2])
                for kc in range(8):
                    cx.op("pe", lambda en, kc=kc, s2=s2: en.transpose(
                        out=tp[:, kc * 128:(kc + 1) * 128], in_=xn[s2][:, kc * 128:(kc + 1) * 128],
                        identity=ident[:]), reads=[xn[s2], ident], writes=[tp])
                cx.op("dve", lambda en, tt=tt: en.tensor_copy(
                    out=xnT[:, :, tt * 128:(tt + 1) * 128], in_=tp[:, :].rearrange("p (c t) -> p c t", t=128)),
                    reads=[tp], writes=[xnT])

            for tt in range(NC):
                fm.append(lambda tt=tt: tile_piece(tt))

            def fm_piece(cb):
                cbo = (cb % 8) + (8 if cb < 8 else 0)
                pb = pj[cb % 2]
                for kc in range(8):
                    cx.op("pe", lambda en, kc=kc, cbo=cbo, pb=pb: en.matmul(
                        pb[:, 0:NT], lhsT=wt[:, kc, cbo * 128:(cbo + 1) * 128], rhs=xnT[:, kc, :],
                        start=(kc == 0), stop=(kc == 7)), reads=[wt, xnT], writes=[pb])
                if cbo < 8:
                    cx.op("act", lambda en, pb=pb, cbo=cbo: en.activation(out=qT[:, cbo, :], in_=pb[:, 0:NT], func=AF.Copy),
                          reads=[pb], writes=[qT])
                else:
                    cx.op("act", lambda en, pb=pb, cbo=cbo: en.activation(out=A[:, cbo - 8, :], in_=pb[:, 0:NT],
                                                                         func=AF.Sigmoid), reads=[pb], writes=[A])

            for cb in range(16):
                fm.append(lambda cb=cb: fm_piece(cb))

            def tm_piece(tt, c):
                if True:
                    pb = pj[c % 2]
                    for kc in range(8):
                        cx.op("pe", lambda en, kc=kc, c=c, pb=pb, tt=tt: en.matmul(
                            pb[:, :], lhsT=xnT[:, kc, tt * 128:(tt + 1) * 128],
                            rhs=wt[:, kc, 2048 + c * 512:2048 + (c + 1) * 512],
                            start=(kc == 0), stop=(kc == 7)), reads=[xnT, wt], writes=[pb])
                    if c < 2:
                        cx.op("dve", lambda en, pb=pb, c=c, tt=tt: en.tensor_copy(
                            out=vq[:, tt, c * 512:(c + 1) * 512], in_=pb[:, :]), reads=[pb], writes=[vq])
                    else:
                        cx.op("act", lambda en, pb=pb, c=c, tt=tt: en.activation(
                            out=sgt[q % 2][:, tt, (c - 2) * 512:(c - 1) * 512], in_=pb[:, :], func=AF.Silu),
                            reads=[pb], writes=[sgt[q % 2]])

            for tt in range(NC):
                for c in range(4 if final else 2):
                    tm.append(lambda tt=tt, c=c: tm_piece(tt, c))
            return fm, tm

        def Y(q):
            qq = q % 2
            qT, A = qTs[q % nbq], As[q % nbq]
            cx.op("dve", lambda en: en.tensor_tensor(out=A[:, :, :], in0=A[:, :, :], in1=bcast(P["oml"][:, dr, :], 1, NT),
                                                     op=ALU.mult), reads=[A, P["oml"]], writes=[A])
            cx.op("dve", lambda en: en.tensor_tensor(out=A[:, :, :], in0=A[:, :, :], in1=bcast(P["lb"][:, dr, :], 1, NT),
                                                     op=ALU.add), reads=[A, P["lb"]], writes=[A])
            cx.op("act", lambda en: en.activation(out=L[:], in_=A[:], func=AF.Ln), reads=[A], writes=[L])
            cx.op("pool", lambda en: en.tensor_scalar(out=A[:], in0=A[:], scalar1=-1.0, scalar2=1.0,
                                                      op0=ALU.mult, op1=ALU.add), reads=[A], writes=[A])
            for h in range(8):
                if dr == 0:
                    cx.op("dve", lambda en, h=h: en.tensor_tensor_scan(
                        out=bb[:, h, :], data0=onesr[:, :], data1=L[:, h, :], initial=0.0,
                        op0=ALU.mult, op1=ALU.add), reads=[onesr, L], writes=[bb])
                else:
                    cx.op("dve", lambda en, h=h: en.tensor_tensor_scan(
                        out=bb[:, h, ::-1], data0=onesr[:, ::-1], data1=L[:, h, ::-1], initial=0.0,
                        op0=ALU.mult, op1=ALU.add), reads=[onesr, L], writes=[bb])
            bbv = bb[:, :, :].rearrange("p h (c t) -> p (h c) t", t=128)
            Lv = L[:, :, :].rearrange("p h (c t) -> p (h c) t", t=128)
            cx.op("dve", lambda en: en.tensor_tensor(out=Lv, in0=bbv, in1=bcast(bbv[:, :, 64], 1, 128), op=ALU.subtract),
                  reads=[bb], writes=[L])
            cx.op("act", lambda en: en.activation(out=gg[qq][:, :], in_=bbv[:, :, lastc], func=AF.Exp), reads=[bb], writes=[gg[qq]])
            cx.op("act", lambda en: en.activation(out=er[qq][:, :], in_=bbv[:, :, 64], func=AF.Exp), reads=[bb], writes=[er[qq]])
            cx.op("act", lambda en: en.activation(out=e2[qq][:, :], in_=Lv[:, :, lastc], func=AF.Exp), reads=[L], writes=[e2[qq]])
            cx.op("act", lambda en: en.activation(out=bb[:], in_=L[:], func=AF.Exp), reads=[L], writes=[bb])
            cx.op("act", lambda en: en.activation(out=L[:], in_=L[:], func=AF.Exp, scale=-1.0), reads=[L], writes=[L])
            cx.op("dve", lambda en: en.tensor_tensor(out=qd[qq][:], in0=qT[:], in1=bb[:], op=ALU.mult),
                  reads=[qT, bb], writes=[qd[qq]])
            cx.op("pool", lambda en: en.tensor_tensor(out=kd[qq][:], in0=A[:], in1=L[:], op=ALU.mult),
                  reads=[A, L], writes=[kd[qq]])

        def Z(q):
            si = order[q]
            t0 = si * NT
            qq = q % 2
            QD, KD, VT, GG, E2, ER = qd[qq], kd[qq], vt[qq], gg[qq], e2[qq], er[qq]
            segs1, segs2, segs3 = [], [], []
            for ci_, c in enumerate(corder):
                a2 = ci_ % 2
                segs1.append(lambda c=c, a2=a2: seg1(c, a2))
                segs2.append(lambda c=c, a2=a2: seg2(c, a2))
                if final:
                    segs3.append(lambda c=c, a2=a2: seg3(c, a2))

            def seg1(c, a2):
                cs_ = slice(c * 128, (c + 1) * 128)
                am, kt = attm[a2], kdtok[a2]
                for half in range(2):
                    for hh in range(4):
                        h = half * 4 + hh
                        cx.op("pe", lambda en, h=h, hh=hh: en.matmul(
                            scp[:, hh * 128:(hh + 1) * 128], lhsT=KD[:, h, cs_], rhs=QD[:, h, cs_],
                            start=True, stop=True), reads=[KD, QD], writes=[scp])
                    cx.op("dve", lambda en, half=half: en.tensor_tensor(
                        out=am[:, half * 4:half * 4 + 4, :], in0=scp[:, :].rearrange("p (h t) -> p h t", t=128),
                        in1=bcast(consts["cmask"][:, dr, :], 0, 4), op=ALU.mult),
                        reads=[scp, consts["cmask"]], writes=[am])
                for h in range(8):
                    cx.op("pe", lambda en, h=h: en.transpose(out=tp[:, h * 128:(h + 1) * 128], in_=KD[:, h, cs_],
                                                             identity=ident[:]),
                          reads=[KD, ident], writes=[tp])
                cx.op("act", lambda en: en.activation(out=kt[:, :, :], in_=tp[:, :].rearrange("p (c t) -> p c t", t=128),
                                                      func=AF.Copy), reads=[tp], writes=[kt])

            def seg2(c, a2):
                cs_ = slice(c * 128, (c + 1) * 128)
                am, kt = attm[a2], kdtok[a2]
                cx.op("pool", lambda en: en.tensor_tensor(out=Sp[:, :, :], in0=S[:, :, :],
                                                          in1=bcast(ER[:, c::NC], 1, 128), op=ALU.mult),
                      reads=[S, ER], writes=[Sp])
                for h in range(8):
                    pob = po[h // 4]
                    oc = slice((h % 4) * 128, (h % 4 + 1) * 128)
                    cx.op("pe", lambda en, h=h, pob=pob, oc=oc: en.matmul(
                        pob[:, oc], lhsT=QD[:, h, cs_], rhs=Sp[:, h, :], start=True, stop=False),
                        reads=[QD, Sp], writes=[pob])
                    cx.op("pe", lambda en, h=h, pob=pob, oc=oc: en.matmul(
                        pob[:, oc], lhsT=am[:, h, :], rhs=VT[:, c, h * 128:(h + 1) * 128], start=False, stop=True),
                        reads=[am, VT], writes=[pob])
                    cx.op("pe", lambda en, h=h, oc=oc: en.matmul(
                        su[h // 4][:, oc], lhsT=kt[:, h, :], rhs=VT[:, c, h * 128:(h + 1) * 128], start=True, stop=True),
                        reads=[kt, VT], writes=[su[h // 4]])
                cx.op("dve", lambda en: en.tensor_tensor(out=S[:, :, :], in0=S[:, :, :], in1=bcast(GG[:, c::NC], 1, 128),
                                                         op=ALU.mult), reads=[S, GG], writes=[S])
                for b2 in range(2):
                    e2v = E2[:, c::NC]
                    cx.op("dve", lambda en, b2=b2, e2v=e2v: en.tensor_tensor(
                        out=tmpS[:, b2 * 4:b2 * 4 + 4, :], in0=su[b2][:, :].rearrange("p (h v) -> p h v", v=128),
                        in1=bcast(e2v[:, b2 * 4:b2 * 4 + 4], 1, 128), op=ALU.mult), reads=[su[b2], E2], writes=[tmpS])
                cx.op("dve", lambda en: en.tensor_tensor(out=S[:, :, :], in0=S[:, :, :], in1=tmpS[:, :, :], op=ALU.add),
                      reads=[S, tmpS], writes=[S])
                rows = slice(t0 + c * 128, t0 + (c + 1) * 128)
                ob = osb[a2]
                if not final:
                    cx.op("act", lambda en: en.activation(out=ob[:, 0:512], in_=po[0][:, :], func=AF.Copy),
                          reads=[po[0]], writes=[ob])
                    cx.op("act", lambda en: en.activation(out=ob[:, 512:1024], in_=po[1][:, :], func=AF.Copy),
                          reads=[po[1]], writes=[ob])
                    cx.dma("pool", of[rows, :], ob[:, :], reads=[ob])
                    return
                cx.dma("sp", oft[a2][:], of[rows, :], writes=[oft[a2]])
                cx.dma("sp", rt_[a2][:], x_src[rows, :], writes=[rt_[a2]])
                for b2 in range(2):
                    cx.op("dve", lambda en, b2=b2: en.tensor_tensor(
                        out=ob[:, b2 * 512:(b2 + 1) * 512], in0=po[b2][:, :], in1=oft[a2][:, b2 * 512:(b2 + 1) * 512],
                        op=ALU.add), reads=[po[b2], oft[a2]], writes=[ob])

            def seg3(c, a2):
                rows = slice(t0 + c * 128, t0 + (c + 1) * 128)
                ob = osb[a2]
                obv = ob[:, :].rearrange("p (h v) -> p h v", v=128)
                sq = oft[a2]
                cx.op("pool", lambda en: en.tensor_tensor(out=sq[:, :], in0=ob[:, :], in1=ob[:, :], op=ALU.mult),
                      reads=[ob], writes=[sq])
                cx.op("dve", lambda en: en.tensor_reduce(out=s8[:, :], in_=sq[:, :].rearrange("p (h v) -> p h v", v=128),
                                                         axis=AX.X, op=ALU.add), reads=[sq], writes=[s8])
                cx.op("act", lambda en: en.activation(out=s8[:, :], in_=s8[:, :], func=AF.Sqrt, scale=1.0 / 128,
                                                      bias=EPS_AP[0][:]), reads=[s8], writes=[s8])
                cx.op("dve", lambda en: en.reciprocal(out=r8[:, :], in_=s8[:, :]), reads=[s8], writes=[r8])
                obv = ob[:, :].rearrange("p (h v) -> p h v", v=128)
                cx.op("dve", lambda en: en.tensor_tensor(out=obv, in0=obv, in1=bcast(r8[:, :], 1, 128), op=ALU.mult),
                      reads=[ob, r8], writes=[ob])
                cx.op("pool", lambda en: en.tensor_tensor(out=obv, in0=obv, in1=bcast(P["gnb"][:, :], 0, 8),
                                                          op=ALU.mult), reads=[ob, P["gnb"]], writes=[ob])
                cx.op("pool", lambda en: en.tensor_tensor(out=og[a2][:, :], in0=ob[:, :], in1=sgt[qq][:, c, :],
                                                          op=ALU.mult), reads=[ob, sgt[qq]], writes=[og[a2]])
                for kc in range(8):
                    cx.op("pe", lambda en, kc=kc: en.transpose(out=tp[:, kc * 128:(kc + 1) * 128],
                                                               in_=og[a2][:, kc * 128:(kc + 1) * 128], identity=ident[:]),
                          reads=[og[a2], ident], writes=[tp])
                cx.op("act", lambda en: en.activation(out=ogT[a2][:, :, :], in_=tp[:, :].rearrange("p (c t) -> p c t", t=128),
                                                      func=AF.Copy), reads=[tp], writes=[ogT[a2]])
                for c2 in range(2):
                    pb = pj[c2]
                    for kc in range(8):
                        cx.op("pe", lambda en, kc=kc, c2=c2, pb=pb: en.matmul(
                            pb[:, :], lhsT=ogT[a2][:, kc, :], rhs=wo[:, kc, c2 * 512:(c2 + 1) * 512],
                            start=(kc == 0), stop=(kc == 7)), reads=[ogT[a2], wo], writes=[pb])
                    cx.op("dve", lambda en, c2=c2, pb=pb: en.tensor_tensor(
                        out=xo[a2][:, c2 * 512:(c2 + 1) * 512], in0=pb[:, :], in1=rt_[a2][:, c2 * 512:(c2 + 1) * 512],
                        op=ALU.add), reads=[pb, rt_[a2]], writes=[xo[a2]])
                cx.dma("pool", dst[rows, :], xo[a2][:], reads=[xo[a2]])
            if final:
                return segs1 + [segs2[0], segs3[0], segs2[1], segs3[1]]
            return segs1 + segs2

        fm0, tm0 = X(0)
        for p_ in fm0 + tm0:
            p_()
        Y(0)
        for q in range(nst):
            zs = Z(q)
            if q + 1 < nst:
                fm, tm = X(q + 1)
                step = 10 ** 9 if (INTERLEAVE_OFF or final) else max(1, len(fm) // (len(zs) + 1))
                zi = 0
                for k_, p_ in enumerate(fm):
                    p_()
                    if (k_ + 1) % step == 0 and zi < len(zs):
                        zs[zi]()
                        zi += 1
                while zi < len(zs):
                    zs[zi]()
                    zi += 1
                Y(q + 1)
                for p_ in tm:
                    p_()
            else:
                for z_ in zs:
                    z_()
        cx.barrier()


def layer_c(cx, T, x_src, x_dst, din, ngT, layer, scr, consts):
    with ExitStack() as sc:
        P = {}
        P["lb"] = cx.sb(sc, "lb", [128, 2, 8], F32)
        P["oml"] = cx.sb(sc, "oml", [128, 2, 8], F32)
        P["gnb"] = cx.sb(sc, "gnb", [128, 128], F32)
        lbe = cx.sb(sc, "lbe", [128, 4, 16], F32)
        tot = cx.sb(sc, "tot", [128, 16], F32)
        cx.dma("sp", lbe[:], din["c_lbT"][:, :, :], writes=[lbe])
        cx.dma("sp", P["gnb"][:], din["c_gnb"][:, :], writes=[P["gnb"]])
        cx.op("act", lambda en: en.activation(out=lbe[:], in_=lbe[:], func=AF.Exp), reads=[lbe], writes=[lbe])
        lbv = P["lb"][:, :, :].rearrange("p a b -> p (a b)")
        omv = P["oml"][:, :, :].rearrange("p a b -> p (a b)")
        cx.op("dve", lambda en: en.tensor_tensor(out=tot[:, :], in0=lbe[:, 0, :], in1=lbe[:, 3, :], op=ALU.add),
              reads=[lbe], writes=[tot])
        cx.op("dve", lambda en: en.tensor_tensor(out=lbv, in0=lbe[:, 1, :], in1=lbe[:, 2, :], op=ALU.add),
              reads=[lbe], writes=[P["lb"]])
        cx.op("dve", lambda en: en.tensor_tensor(out=tot[:, :], in0=tot[:, :], in1=lbv, op=ALU.add),
              reads=[tot, P["lb"]], writes=[tot])
        cx.op("dve", lambda en: en.reciprocal(out=tot[:, :], in_=tot[:, :]), reads=[tot], writes=[tot])
        cx.op("dve", lambda en: en.tensor_tensor(out=lbv, in0=lbv, in1=tot[:, :], op=ALU.mult),
              reads=[tot, P["lb"]], writes=[P["lb"]])
        cx.op("dve", lambda en: en.tensor_scalar(out=omv, in0=lbv, scalar1=-1.0, scalar2=1.0, op0=ALU.mult, op1=ALU.add),
              reads=[P["lb"]], writes=[P["oml"]])
        cx.barrier()
        S = cx.sb(sc, "S", [128, 8, 128], F32)
        S2 = cx.sb(sc, "S2", [128, 8, 128], F32)
        cx.op("dve", lambda en: en.memset(S[:], 0.0), writes=[S])
        hgrn_pass(cx, T, 0, x_src, din, (ngT, layer), P, scr["of"], None, consts, False, S)
        with ExitStack() as s3:
            exchange_sb(cx, s3, S[:, :, :].rearrange("p h v -> p (h v)"), S2[:, :, :].rearrange("p h v -> p (h v)"),
                        1024, consts["sel"])
        hgrn_pass(cx, T, 1, x_src, din, (ngT, layer), P, scr["of"], x_dst, consts, True, S2)


def make_in_maps(xp, xsamp, W, T):
    maps = []
    meta = []
    for b in range(xp.shape[0]):
        seq = np.asarray(xp[b], np.float32)
        for half in range(2):
            if half == 0:
                x_ext = seq[0:T + HALO]
                pos = np.arange(T + HALO)
                pos_h3 = T + 2047 - np.arange(2048)
                maps.append(core_inputs(x_ext, pos, 1.0, W, False, (0.0, 1.0), pos_h3))
            else:
                x_ext = seq[::-1][0:T + HALO]
                pos = 2 * T - 1 - np.arange(T + HALO)
                pos_h3 = T - 2048 + np.arange(2048)
                maps.append(core_inputs(x_ext, pos, 1.0, W, True, (1.0, 0.0), pos_h3))
            meta.append(("p", b, half))
    for b in range(xsamp.shape[0]):
        seq = np.asarray(xsamp[b], np.float32)
        x_ext = np.concatenate([seq, np.zeros((HALO, D), np.float32)], 0)
        maps.append(core_inputs(x_ext, np.arange(T + HALO), 0.0, W, False, (0.0, 0.0), None))
        meta.append(("s", b, 0))
    return maps, meta


_NC_CACHE = {}


def kernel(**inputs):
    T = 8192
    W = {k: np.asarray(v, np.float32) for k, v in inputs.items() if k not in ("x_prompt", "x_sample")}
    xp = np.asarray(inputs["x_prompt"], np.float32)
    xsamp = np.asarray(inputs["x_sample"], np.float32)
    maps, meta = make_in_maps(xp, xsamp, W, T)
    if T not in _NC_CACHE:
        _NC_CACHE[T] = build(T, 4)
    res = run_bass_kernel_spmd(_NC_CACHE[T], maps, core_ids=list(range(8)))
    yp = np.zeros(xp.shape, np.float32)
    ys = np.zeros(xsamp.shape, np.float32)
    for c, (kind, b, half) in enumerate(meta):
        y = np.asarray(res.results[c]["y"], np.float32)
        if kind == "s":
            ys[b] = y
        elif half == 0:
            yp[b, 0:T] = y
        else:
            yp[b, T:2 * T] = y[::-1]
    return (yp, ys)
```

```python
import numpy as np
import ml_dtypes
from contextlib import ExitStack
import concourse.bass as bass
import concourse.mybir as mybir
from concourse.bass_utils import run_bass_kernel_spmd

F32 = mybir.dt.float32
BF16 = mybir.dt.bfloat16
ALU = mybir.AluOpType
AF = mybir.ActivationFunctionType
AX = mybir.AxisListType

D = 1024
EPS = 1e-6
HALO = 2048
SAME_SYNC = True
INTERLEAVE_OFF = True
ATT_SCALE = 128.0 ** -0.5
GROUPS = ((0, 1), (1, 4), (2, 16))


class Buf:
    def __init__(self, t, name):
        self.t = t
        self.name = name
        self.w = None
        self.r = {}
        self.dsem = None
        self.dcnt = 0

    def __getitem__(self, idx):
        return self.t[idx]


def bcast(ap, pos, n):
    l = [list(x) for x in ap.ap]
    l.insert(1 + pos, [0, n])
    return bass.AP(ap.tensor, ap.offset, l)


class Ctx:
    def __init__(self, nc, es):
        self.nc = nc
        self.es = es
        self.eng = {"pe": nc.tensor, "act": nc.scalar, "dve": nc.vector, "pool": nc.gpsimd, "sp": nc.sync}
        self.esem = {e: es.enter_context(nc.semaphore("s_" + e)) for e in ("pe", "act", "dve", "pool")}
        self.cnt = {e: 0 for e in self.esem}
        self.waited = {e: {} for e in self.eng}
        self.slots = []
        self.free_slots = {}
        self.active = []
        self.nsem = 0
        self.nb = 0

    def sb(self, scope, name, shape, dt):
        self.nb += 1
        return Buf(scope.enter_context(self.nc.sbuf_tensor("%s_%d" % (name, self.nb), shape, dt)), name)

    def ps(self, scope, name, shape, dt):
        self.nb += 1
        return Buf(scope.enter_context(self.nc.psum_tensor("%s_%d" % (name, self.nb), shape, dt)), name)

    def _deps(self, e, reads, writes):
        deps = []
        for b in reads:
            if b.w is not None:
                deps.append(b.w)
        for b in writes:
            if b.w is not None:
                deps.append(b.w)
            deps.extend(b.r.values())
        for (key, sem, val) in deps:
            if key == e and (e == "pe" or not SAME_SYNC):
                continue
            if self.waited[e].get(key, 0) >= val:
                continue
            self.eng[e].wait_ge(sem, val)
            self.waited[e][key] = val

    def op(self, e, fn, reads=(), writes=()):
        self._deps(e, reads, writes)
        ins = fn(self.eng[e])
        self.cnt[e] += 1
        ins.then_inc(self.esem[e], 1)
        tok = (e, self.esem[e], self.cnt[e])
        for b in reads:
            b.r[e] = tok
        for b in writes:
            b.w = tok
            b.r = {}
        return ins

    def dma(self, e, out, in_, reads=(), writes=()):
        self._deps(e, reads, writes)
        prim = writes[0] if writes else reads[0]
        if prim.dsem is None:
            prim.dsem = {}
        if e not in prim.dsem:
            fl = self.free_slots.setdefault(e, [])
            if fl:
                prim.dsem[e] = fl.pop()
            else:
                self.nsem += 1
                prim.dsem[e] = [self.es.enter_context(self.nc.semaphore("d%d" % self.nsem)), 0, self.nsem]
                self.slots.append(prim.dsem[e])
            self.active.append((prim, e))
        slot = prim.dsem[e]
        slot[1] += 16
        self.eng[e].dma_start(out=out, in_=in_).then_inc(slot[0], 16)
        tok = (("d", slot[2]), slot[0], slot[1])
        for b in reads:
            b.r[tok[0]] = tok
        for b in writes:
            b.w = tok
            b.r = {}

    def barrier(self):
        toks = [(e, self.esem[e], self.cnt[e]) for e in self.esem if self.cnt[e] > 0]
        toks += [(("d", sl[2]), sl[0], sl[1]) for sl in self.slots if sl[1] > 0]
        for e in self.eng:
            for (key, sem, val) in toks:
                if key == e and e == "pe":
                    continue
                if self.waited[e].get(key, 0) >= val:
                    continue
                self.eng[e].wait_ge(sem, val)
                self.waited[e][key] = val
        for (b, e) in self.active:
            self.free_slots.setdefault(e, []).append(b.dsem.pop(e))
        self.active = []


PAIRS = [[0, 1], [2, 3], [4, 5], [6, 7]]
XN = [0]


def allgather(cx, pk, ga):
    nc = cx.nc
    XN[0] += 1
    sem = cx.es.enter_context(nc.semaphore("cc%d" % XN[0]))
    nc.gpsimd.collective_compute("AllGather", ALU.bypass, replica_groups=PAIRS, ins=[pk], outs=[ga]).then_inc(sem)
    for e in cx.eng:
        cx.eng[e].wait_ge(sem, 1)


def exchange_sb(cx, sc, src, dst, F, sel):
    nc = cx.nc
    XN[0] += 1
    pk = nc.dram_tensor("pk%d" % XN[0], [128, F], F32, kind="Internal").ap()
    ga = nc.dram_tensor("ga%d" % XN[0], [256, F], F32, kind="Internal", addr_space="Local").ap()
    gt = cx.sb(sc, "gt", [128, 2, F], F32)
    cx.dma("pool", pk[:, :], src, reads=[gt])
    cx.barrier()
    allgather(cx, pk[:, :], ga[:, :])
    cx.dma("pool", gt[:], ga.rearrange("(r p) f -> p r f", p=128), writes=[gt])
    cx.op("dve", lambda en: en.tensor_scalar(out=dst, in0=gt[:, 0, :], scalar1=sel[:, 0:1], scalar2=None, op0=ALU.mult),
          reads=[gt, sel], writes=[gt])
    cx.op("dve", lambda en: en.scalar_tensor_tensor(out=dst, in0=gt[:, 1, :], scalar=sel[:, 1:2], in1=dst,
                                                    op0=ALU.mult, op1=ALU.add), reads=[gt, sel], writes=[gt])
    cx.barrier()


def load_weight(cx, stg, dst, col_off, w_ap, ncols, g=None, kchunks=8):
    for kc in range(kchunks):
        s = stg[kc % len(stg)]
        cx.dma("sp", s[:, 0:ncols], w_ap[kc * 128:(kc + 1) * 128, :], writes=[s])
        e = ("dve", "pool", "act")[kc % 3]
        if e == "act":
            if g is not None:
                cx.op("act", lambda en, kc=kc, s=s: en.activation(
                    out=dst[:, kc, col_off:col_off + ncols], in_=s[:, 0:ncols], func=AF.Copy,
                    scale=g[0][:, g[1], kc:kc + 1]), reads=[s, g[0]], writes=[dst])
            else:
                cx.op("act", lambda en, kc=kc, s=s: en.activation(
                    out=dst[:, kc, col_off:col_off + ncols], in_=s[:, 0:ncols], func=AF.Copy),
                    reads=[s], writes=[dst])
            continue
        if g is not None:
            cx.op(e, lambda en, kc=kc, s=s: en.tensor_scalar(
                out=dst[:, kc, col_off:col_off + ncols], in0=s[:, 0:ncols], scalar1=g[0][:, g[1], kc:kc + 1],
                scalar2=1.0, op0=ALU.mult, op1=ALU.mult), reads=[s, g[0]], writes=[dst])
        else:
            cx.op(e, lambda en, kc=kc, s=s: en.tensor_copy(
                out=dst[:, kc, col_off:col_off + ncols], in_=s[:, 0:ncols]), reads=[s], writes=[dst])


def norm_tile(cx, xt, junk, ssq, std, rstd, xn):
    cx.op("act", lambda en: en.activation(out=junk[:], in_=xt[:], func=AF.Square), reads=[xt], writes=[junk])
    cx.op("dve", lambda en: en.tensor_reduce(out=ssq[:], in_=junk[:], axis=AX.X, op=ALU.add), reads=[junk], writes=[ssq])
    cx.op("act", lambda en: en.activation(out=std[:], in_=ssq[:], func=AF.Sqrt, scale=1.0 / D, bias=EPS_AP[0][:]),
          reads=[ssq], writes=[std])
    cx.op("dve", lambda en: en.reciprocal(out=rstd[:], in_=std[:]), reads=[std], writes=[rstd])
    cx.op("pool", lambda en: en.tensor_scalar(out=xn[:], in0=xt[:], scalar1=rstd[:, 0:1], scalar2=1.0,
                                               op0=ALU.mult, op1=ALU.mult), reads=[xt, rstd], writes=[xn])


EPS_AP = [None]


def transpose8(cx, src, tp, dst, ident, evac_eng="dve", ncol=8, rev=None):
    for kc in range(ncol):
        if rev is None:
            cx.op("pe", lambda en, kc=kc: en.transpose(out=tp[:, kc * 128:(kc + 1) * 128],
                                                       in_=src[:, kc * 128:(kc + 1) * 128], identity=ident[:]),
                  reads=[src, ident], writes=[tp])
    if evac_eng == "act_copy":
        cx.op("act", lambda en: en.activation(out=dst[:, 0:ncol, :],
                                              in_=tp[:, 0:ncol * 128].rearrange("p (c t) -> p c t", t=128),
                                              func=AF.Copy), reads=[tp], writes=[dst])
    else:
        cx.op(evac_eng, lambda en: en.tensor_copy(out=dst[:, 0:ncol, :],
                                                  in_=tp[:, 0:ncol * 128].rearrange("p (c t) -> p c t", t=128)),
              reads=[tp], writes=[dst])


def attn_group_pass(cx, T, g, d, x_src, cs_src, w_in, gvec, part, consts, halo_x=None, halo_cs=None):
    n = T // (128 * d)
    ident, maskb, ones8, flag8 = consts["ident"], consts["maskb"], consts["ones8"], consts["flag8"]
    with ExitStack() as sc:
        wt = cx.sb(sc, "wt", [128, 8, 3072], BF16)
        stg = [cx.sb(sc, "stg", [128, 1024], F32) for _ in range(4)]
        for j in range(3):
            col = (j * 3 + g) * 1024
            load_weight(cx, stg, wt, j * 1024, w_in[:, col:col + 1024], 1024, g=gvec)
        xt = [cx.sb(sc, "xt", [128, D], F32) for _ in range(2)]
        cst = [cx.sb(sc, "cst", [128, 32], F32) for _ in range(3)]
        junk = cx.sb(sc, "junk", [128, D], F32)
        ssq = [cx.sb(sc, "ssq", [128, 1], F32) for _ in range(2)]
        std = [cx.sb(sc, "std", [128, 1], F32) for _ in range(2)]
        rstd = [cx.sb(sc, "rstd", [128, 1], F32) for _ in range(2)]
        xn = [cx.sb(sc, "xn", [128, D], BF16) for _ in range(2)]
        xnT = [cx.sb(sc, "xnT", [128, 8, 128], BF16) for _ in range(2)]
        qk = [cx.sb(sc, "qk", [128, 2048], F32) for _ in range(2)]
        qkr = [cx.sb(sc, "qkr", [128, 2048], BF16) for _ in range(2)]
        rt = [cx.sb(sc, "rt", [128, 4, 16, 16], F32) for _ in range(2)]
        QT = [cx.sb(sc, "QT", [128, 8, 128], BF16) for _ in range(3)]
        KT = [cx.sb(sc, "KT", [128, 8, 128], BF16) for _ in range(4)]
        Vp = [cx.sb(sc, "Vp", [128, 8, 132], BF16) for _ in range(4)]
        PT = [cx.sb(sc, "PT", [128, 384], BF16) for _ in range(2)]
        osb = [cx.sb(sc, "osb", [128, 8, 129], F32) for _ in range(2)]
        pj = [cx.ps(sc, "pj", [128, 512], F32) for _ in range(2)]
        tp = cx.ps(sc, "tp", [128, 1024], BF16)
        scp = [cx.ps(sc, "scp", [128, 512], F32) for _ in range(2)]
        po = [cx.ps(sc, "po", [128, 512], F32) for _ in range(3)]

        items = [(r, i) for r in range(d) for i in range(n + 1)]
        NI = len(items)

        def views(r):
            return (x_src.rearrange("(l d) f -> d l f", d=d)[r], cs_src.rearrange("(l d) f -> d l f", d=d)[r],
                    part.rearrange("(l d) f -> d l f", d=d)[r])

        def a_norm(t):
            r, i = items[t]
            s2 = t % 2
            xv, cv, _ = views(r)
            if i == n and halo_x is not None:
                u0 = 2047 - r - 127 * d
                cx.dma("sp", xt[s2][:], bass.AP(halo_x.tensor, halo_x.offset + u0 * D, [[d * D, 128], [1, D]]),
                       writes=[xt[s2]])
                cx.dma("sp", cst[t % 3][:], bass.AP(halo_cs.tensor, halo_cs.offset + u0 * 32, [[d * 32, 128], [1, 32]]),
                       writes=[cst[t % 3]])
            else:
                cx.dma("sp", xt[s2][:], xv[i * 128:(i + 1) * 128, :], writes=[xt[s2]])
                cx.dma("sp", cst[t % 3][:], cv[i * 128:(i + 1) * 128, :], writes=[cst[t % 3]])
            norm_tile(cx, xt[s2], junk, ssq[s2], std[s2], rstd[s2], xn[s2])

        def a_tr(t):
            transpose8(cx, xn[t % 2], tp, xnT[t % 2], ident)

        def proj(t):
            r, i = items[t]
            s2 = t % 2
            k4 = t % 4
            for c in range(6):
                if i == n and c < 2:
                    continue
                pb = pj[c % 2]
                for kc in range(8):
                    cx.op("pe", lambda en, kc=kc, c=c, pb=pb: en.matmul(
                        pb[:, :], lhsT=xnT[s2][:, kc, :], rhs=wt[:, kc, c * 512:(c + 1) * 512],
                        start=(kc == 0), stop=(kc == 7)), reads=[xnT[s2], wt], writes=[pb])
                if c < 4:
                    if c % 2 == 0:
                        cx.op("act", lambda en, c=c, pb=pb: en.activation(
                            out=qk[s2][:, c * 512:(c + 1) * 512], in_=pb[:, :], func=AF.Copy),
                            reads=[pb], writes=[qk[s2]])
                    else:
                        cx.op("dve", lambda en, c=c, pb=pb: en.tensor_copy(
                            out=qk[s2][:, c * 512:(c + 1) * 512], in_=pb[:, :]),
                            reads=[pb], writes=[qk[s2]])
                else:
                    h0 = (c - 4) * 4
                    cx.op("act", lambda en, pb=pb, h0=h0: en.activation(
                        out=Vp[k4][:, h0:h0 + 4, 0:128], in_=pb[:, :].rearrange("p (h e) -> p h e", e=128),
                        func=AF.Copy), reads=[pb], writes=[Vp[k4]])
            vsrc = flag8 if i == n else ones8
            cx.op("pool", lambda en, vsrc=vsrc: en.tensor_copy(out=Vp[k4][:, :, 128:129], in_=vsrc[:, :, 0:1]),
                  reads=[vsrc], writes=[Vp[k4]])

        def rope(t):
            r, i = items[t]
            s2 = t % 2
            c3 = cst[t % 3]
            lo = 8 if i == n else 0
            nh = 16 - lo
            v3 = qk[s2][:, :].rearrange("p (h e) -> p h e", e=128)
            o3 = qkr[s2][:, :].rearrange("p (h e) -> p h e", e=128)
            x1 = v3[:, lo:16, 0:16]
            x2 = v3[:, lo:16, 16:32]
            cosb = bcast(c3[:, 0:16], 0, nh)
            sinb = bcast(c3[:, 16:32], 0, nh)
            rtb = rt[s2]
            for (k_, a_, b_) in ((0, x1, cosb), (1, x2, sinb), (2, x2, cosb), (3, x1, sinb)):
                cx.op("dve", lambda en, k_=k_, a_=a_, b_=b_: en.tensor_tensor(
                    out=rtb[:, k_, lo:16, :], in0=a_, in1=b_, op=ALU.mult),
                    reads=[qk[s2], c3], writes=[rtb])
            cx.op("dve", lambda en: en.tensor_tensor(out=o3[:, lo:16, 0:16], in0=rtb[:, 0, lo:16, :],
                                                     in1=rtb[:, 1, lo:16, :], op=ALU.subtract),
                  reads=[rtb], writes=[qkr[s2]])
            cx.op("dve", lambda en: en.tensor_tensor(out=o3[:, lo:16, 16:32], in0=rtb[:, 2, lo:16, :],
                                                     in1=rtb[:, 3, lo:16, :], op=ALU.add),
                  reads=[rtb], writes=[qkr[s2]])
            cx.op("pool", lambda en: en.tensor_copy(out=o3[:, lo:16, 32:128], in_=v3[:, lo:16, 32:128]),
                  reads=[qk[s2]], writes=[qkr[s2]])

        def qk_tr(t):
            r, i = items[t]
            s2 = t % 2
            if i < n:
                for h in range(8):
                    cx.op("pe", lambda en, h=h: en.transpose(out=tp[:, h * 128:(h + 1) * 128],
                                                             in_=qkr[s2][:, h * 128:(h + 1) * 128],
                                                             identity=ident[:]),
                          reads=[qkr[s2], ident], writes=[tp])
                cx.op("dve", lambda en: en.tensor_copy(out=QT[t % 3][:, :, :],
                                                       in_=tp[:, :].rearrange("p (c t) -> p c t", t=128)),
                      reads=[tp], writes=[QT[t % 3]])
            for h in range(8):
                cx.op("pe", lambda en, h=h: en.transpose(out=tp[:, h * 128:(h + 1) * 128],
                                                         in_=qkr[s2][:, 1024 + h * 128:1024 + (h + 1) * 128],
                                                         identity=ident[:]),
                      reads=[qkr[s2], ident], writes=[tp])
            cx.op("act", lambda en: en.activation(out=KT[t % 4][:, :, :],
                                                  in_=tp[:, :].rearrange("p (c t) -> p c t", t=128), func=AF.Copy),
                  reads=[tp], writes=[KT[t % 4]])

        def att(t):
            r, j = items[t]
            if j >= n:
                return
            _, _, pv = views(r)
            blocks = []
            if j >= 1:
                blocks.append(((t - 1) % 4, 0))
            blocks.append((t % 4, 1))
            blocks.append(((t + 1) % 4, 3 if (j + 1 == n and halo_x is not None) else 2))
            nb = len(blocks)
            ob = osb[t % 2]
            qt = QT[t % 3]

            def scores(h):
                sp_ = scp[h % 2]
                for bi, (ks, mi) in enumerate(blocks):
                    cx.op("pe", lambda en, bi=bi, ks=ks, h=h, sp_=sp_: en.matmul(
                        sp_[:, bi * 128:(bi + 1) * 128], lhsT=KT[ks][:, h, :], rhs=qt[:, h, :],
                        start=True, stop=False), reads=[KT[ks], qt], writes=[sp_])
                    cx.op("pe", lambda en, bi=bi, mi=mi, sp_=sp_: en.matmul(
                        sp_[:, bi * 128:(bi + 1) * 128], lhsT=ident[:], rhs=maskb[:, mi, :],
                        start=False, stop=True), reads=[ident, maskb], writes=[sp_])
                cx.op("act", lambda en, sp_=sp_, h=h: en.activation(
                    out=PT[h % 2][:, 0:nb * 128], in_=sp_[:, 0:nb * 128], func=AF.Exp, scale=ATT_SCALE),
                    reads=[sp_], writes=[PT[h % 2]])

            def pvm(h):
                pt_ = PT[h % 2]
                pob = po[h // 3]
                off = (h % 3) * 129
                for bi, (ks, mi) in enumerate(blocks):
                    cx.op("pe", lambda en, bi=bi, ks=ks, h=h, pob=pob, off=off, pt_=pt_: en.matmul(
                        pob[:, off:off + 129], lhsT=pt_[:, bi * 128:(bi + 1) * 128], rhs=Vp[ks][:, h, 0:129],
                        start=(bi == 0), stop=(bi == nb - 1)), reads=[pt_, Vp[ks]], writes=[pob])

            scores(0)
            for h in range(8):
                if h + 1 < 8:
                    scores(h + 1)
                pvm(h)
            for bk in range(3):
                nhb = 3 if bk < 2 else 2
                if bk != 1:
                    cx.op("dve", lambda en, bk=bk, nhb=nhb: en.tensor_copy(
                        out=ob[:, bk * 3:bk * 3 + nhb, :],
                        in_=po[bk][:, 0:nhb * 129].rearrange("p (h e) -> p h e", e=129)),
                        reads=[po[bk]], writes=[ob])
                else:
                    cx.op("act", lambda en, bk=bk, nhb=nhb: en.activation(
                        out=ob[:, bk * 3:bk * 3 + nhb, :],
                        in_=po[bk][:, 0:nhb * 129].rearrange("p (h e) -> p h e", e=129), func=AF.Copy),
                        reads=[po[bk]], writes=[ob])
            cx.dma("pool", pv[j * 128:(j + 1) * 128, :], ob[:, :, :].rearrange("p h e -> p (h e)"), reads=[ob])

        a_norm(0)
        a_tr(0)
        for t in range(NI):
            if t + 1 < NI:
                a_norm(t + 1)
            proj(t)
            if t + 1 < NI:
                a_tr(t + 1)
            rope(t)
            if t - 2 >= 0:
                att(t - 2)
            qk_tr(t)
        att(NI - 2)
        att(NI - 1)
        cx.barrier()


def attn_final_pass(cx, T, x_src, parts, w_in, w_out, gvec, x_dst, consts, final_g=None):
    ident = consts["ident"]
    nt = T // 128
    with ExitStack() as sc:
        wg = cx.sb(sc, "wg", [128, 8, 1024], BF16)
        wo = cx.sb(sc, "wo", [128, 8, 1024], BF16)
        stg = [cx.sb(sc, "stg", [128, 1024], F32) for _ in range(4)]
        load_weight(cx, stg, wg, 0, w_in[:, 9216:10240], 1024, g=gvec)
        load_weight(cx, stg, wo, 0, w_out[:, :], 1024, g=None)
        xt = [cx.sb(sc, "xt", [128, D], F32) for _ in range(3)]
        pp = [[cx.sb(sc, "pp", [128, 8, 129], F32) for _ in range(3)] for _ in range(2)]
        junk = cx.sb(sc, "junk", [128, D], F32)
        junk2 = cx.sb(sc, "junk2", [128, D], F32)
        ssq = [cx.sb(sc, "ssq", [128, 1], F32) for _ in range(2)]
        std = [cx.sb(sc, "std", [128, 1], F32) for _ in range(2)]
        rstd = [cx.sb(sc, "rstd", [128, 1], F32) for _ in range(2)]
        xn = [cx.sb(sc, "xn", [128, D], BF16) for _ in range(2)]
        xnT = [cx.sb(sc, "xnT", [128, 8, 128], BF16) for _ in range(2)]
        sg = [cx.sb(sc, "sg", [128, D], F32) for _ in range(2)]
        den = [cx.sb(sc, "den", [128, 8, 1], F32) for _ in range(2)]
        rden = [cx.sb(sc, "rden", [128, 8, 1], F32) for _ in range(2)]
        o = [cx.sb(sc, "o", [128, 8, 128], F32) for _ in range(2)]
        og = [cx.sb(sc, "og", [128, D], BF16) for _ in range(2)]
        ogT = [cx.sb(sc, "ogT", [128, 8, 128], BF16) for _ in range(2)]
        xo = [cx.sb(sc, "xo", [128, D], F32) for _ in range(2)]
        yo = [cx.sb(sc, "yo", [128, D], F32) for _ in range(2)]
        ssq2 = [cx.sb(sc, "ssq2", [128, 1], F32) for _ in range(2)]
        std2 = [cx.sb(sc, "std2", [128, 1], F32) for _ in range(2)]
        rstd2 = [cx.sb(sc, "rstd2", [128, 1], F32) for _ in range(2)]
        pj = [cx.ps(sc, "pj", [128, 512], F32) for _ in range(4)]
        tp = [cx.ps(sc, "tp", [128, 1024], BF16) for _ in range(2)]

        def a_norm(i):
            s2 = i % 2
            rows = slice(i * 128, (i + 1) * 128)
            cx.dma("sp", xt[i % 3][:], x_src[rows, :], writes=[xt[i % 3]])
            for g in range(3):
                cx.dma("sp", pp[s2][g][:, :, :].rearrange("p h e -> p (h e)"), parts[g][rows, :], writes=[pp[s2][g]])
            norm_tile(cx, xt[i % 3], junk, ssq[s2], std[s2], rstd[s2], xn[s2])

        def a_tr(i):
            transpose8(cx, xn[i % 2], tp[0], xnT[i % 2], ident)

        def gate(i):
            s2 = i % 2
            for c in range(2):
                pb = pj[c]
                for kc in range(8):
                    cx.op("pe", lambda en, kc=kc, c=c, pb=pb: en.matmul(
                        pb[:, :], lhsT=xnT[s2][:, kc, :], rhs=wg[:, kc, c * 512:(c + 1) * 512],
                        start=(kc == 0), stop=(kc == 7)), reads=[xnT[s2], wg], writes=[pb])
                cx.op("act", lambda en, c=c, pb=pb: en.activation(out=sg[s2][:, c * 512:(c + 1) * 512], in_=pb[:, :],
                                                                 func=AF.Silu), reads=[pb], writes=[sg[s2]])

        def chain(i):
            s2 = i % 2
            p0, p1, p2 = pp[s2]
            cx.op("pool", lambda en: en.tensor_tensor(out=p0[:, :, :], in0=p0[:, :, :], in1=p1[:, :, :], op=ALU.add),
                  reads=[p0, p1], writes=[p0])
            cx.op("dve", lambda en: en.tensor_tensor(out=p0[:, :, :], in0=p0[:, :, :], in1=p2[:, :, :], op=ALU.add),
                  reads=[p0, p2], writes=[p0])
            cx.op("dve", lambda en: en.tensor_scalar(out=den[s2][:, :, :], in0=p0[:, :, 128:129], scalar1=1e-30,
                                                     scalar2=None, op0=ALU.max), reads=[p0], writes=[den[s2]])
            cx.op("dve", lambda en: en.reciprocal(out=rden[s2][:, :, :], in_=den[s2][:, :, :]),
                  reads=[den[s2]], writes=[rden[s2]])
            cx.op("dve", lambda en: en.tensor_tensor(out=o[s2][:, :, :], in0=p0[:, :, 0:128],
                                                     in1=bcast(rden[s2][:, :, 0], 1, 128), op=ALU.mult),
                  reads=[p0, rden[s2]], writes=[o[s2]])
            cx.op("pool", lambda en: en.tensor_tensor(out=og[s2][:, :], in0=o[s2][:, :, :].rearrange("p h e -> p (h e)"),
                                                      in1=sg[s2][:, :], op=ALU.mult),
                  reads=[o[s2], sg[s2]], writes=[og[s2]])

        def out(i):
            s2 = i % 2
            rows = slice(i * 128, (i + 1) * 128)
            x3 = xt[i % 3]
            transpose8(cx, og[s2], tp[1], ogT[s2], ident, evac_eng="act_copy")
            for c in range(2):
                pb = pj[2 + c]
                for kc in range(8):
                    cx.op("pe", lambda en, kc=kc, c=c, pb=pb: en.matmul(
                        pb[:, :], lhsT=ogT[s2][:, kc, :], rhs=wo[:, kc, c * 512:(c + 1) * 512],
                        start=(kc == 0), stop=(kc == 7)), reads=[ogT[s2], wo], writes=[pb])
                cx.op("dve", lambda en, c=c, pb=pb: en.tensor_tensor(
                    out=xo[s2][:, c * 512:(c + 1) * 512], in0=pb[:, :], in1=x3[:, c * 512:(c + 1) * 512],
                    op=ALU.add), reads=[pb, x3], writes=[xo[s2]])
            if final_g is None:
                cx.dma("pool", x_dst[rows, :], xo[s2][:], reads=[xo[s2]])
            else:
                cx.op("act", lambda en: en.activation(out=junk2[:], in_=xo[s2][:], func=AF.Square),
                      reads=[xo[s2]], writes=[junk2])
                cx.op("dve", lambda en: en.tensor_reduce(out=ssq2[s2][:], in_=junk2[:], axis=AX.X, op=ALU.add),
                      reads=[junk2], writes=[ssq2[s2]])
                cx.op("act", lambda en: en.activation(out=std2[s2][:], in_=ssq2[s2][:], func=AF.Sqrt, scale=1.0 / D,
                                                      bias=EPS_AP[0][:]), reads=[ssq2[s2]], writes=[std2[s2]])
                cx.op("dve", lambda en: en.reciprocal(out=rstd2[s2][:], in_=std2[s2][:]), reads=[std2[s2]], writes=[rstd2[s2]])
                cx.op("dve", lambda en: en.scalar_tensor_tensor(out=yo[s2][:], in0=xo[s2][:], scalar=rstd2[s2][:, 0:1],
                                                                in1=final_g[:, :], op0=ALU.mult, op1=ALU.mult),
                      reads=[xo[s2], rstd2[s2], final_g], writes=[yo[s2]])
                cx.dma("pool", x_dst[rows, :], yo[s2][:], reads=[yo[s2]])

        a_norm(0)
        a_tr(0)
        for i in range(nt):
            if i + 1 < nt:
                a_norm(i + 1)
            gate(i)
            if i + 1 < nt:
                a_tr(i + 1)
            if i >= 1:
                out(i - 1)
            chain(i)
        out(nt - 1)
        cx.barrier()


def build(T, NL=4):
    nc = bass.Bass("TRN2", target_bir_lowering=False)
    TH = T + HALO
    din = {}

    def inp(name, shape, dt=F32):
        din[name] = nc.dram_tensor(name, shape, dt, kind="ExternalInput").ap()
        return din[name]

    x0 = inp("x0", [TH, D])
    cs = inp("cs", [TH, 32])
    a_w_in = inp("a_w_in", [2, D, 10240])
    a_w_out = inp("a_w_out", [2, D, D])
    norm_gT = inp("norm_gT", [128, 4, 8])
    final_gb = inp("final_gb", [128, D])
    ident_d = inp("ident", [128, 128], BF16)
    maskb_d = inp("maskb", [128, 4, 128], BF16)
    flag8_d = inp("flag8", [128, 8, 1])
    inp("b_w_in", [D, 2048])
    inp("b_w_out", [D, D])
    inp("b_w_r", [2, 8, 128, 128])
    inp("b_w_i", [2, 8, 128, 128])
    inp("b_b_rT", [128, 2, 8])
    inp("b_b_iT", [128, 2, 8])
    inp("b_lamT", [128, 2, 8])
    inp("b_w5T", [128, 5, 8])
    inp("b_cbT", [128, 8])
    inp("c_w_in", [D, 5120])
    inp("c_w_out", [D, D])
    inp("c_lbT", [128, 4, 16])
    inp("c_gnb", [128, 128])
    cmask_d = inp("cmask", [128, 2, 128], BF16)
    sel_d = inp("sel", [128, 2])
    cs_h3 = inp("cs_h3", [2048, 32])
    y = nc.dram_tensor("y", [T, D], F32, kind="ExternalOutput").ap()
    scr = {}
    scr["of"] = nc.dram_tensor("of", [T, D], F32, kind="Internal").ap()
    scr["xbT"] = nc.dram_tensor("xbT", [8, 128, T + 4], F32, kind="Internal").ap()
    scr["sgT"] = nc.dram_tensor("sgT", [8, 128, T], F32, kind="Internal").ap()
    scr["xp"] = nc.dram_tensor("xp", [T, D], F32, kind="Internal").ap()
    xs = [nc.dram_tensor("xs%d" % i, [TH, D], F32, kind="Internal").ap() for i in range(2)]
    parts = [nc.dram_tensor("part%d" % i, [T, 8 * 129], F32, kind="Internal").ap() for i in range(3)]

    with ExitStack() as es:
        cx = Ctx(nc, es)
        consts = {}
        consts["ident"] = cx.sb(es, "ident", [128, 128], BF16)
        consts["maskb"] = cx.sb(es, "maskb", [128, 4, 128], BF16)
        consts["ones8"] = cx.sb(es, "ones8", [128, 8, 1], F32)
        consts["flag8"] = cx.sb(es, "flag8", [128, 8, 1], F32)
        ngT = cx.sb(es, "ngT", [128, 4, 8], F32)
        epsb = cx.sb(es, "epsb", [128, 1], F32)
        EPS_AP[0] = epsb
        cx.dma("sp", consts["ident"][:], ident_d[:, :], writes=[consts["ident"]])
        cx.dma("sp", consts["maskb"][:], maskb_d[:, :, :], writes=[consts["maskb"]])
        cx.dma("sp", consts["flag8"][:], flag8_d[:, :, :], writes=[consts["flag8"]])
        cx.dma("sp", ngT[:], norm_gT[:, :, :], writes=[ngT])
        consts["sel"] = cx.sb(es, "sel", [128, 2], F32)
        cx.dma("sp", consts["sel"][:], sel_d[:, :], writes=[consts["sel"]])
        consts["cmask"] = cx.sb(es, "cmask", [128, 2, 128], BF16)
        cx.dma("sp", consts["cmask"][:], cmask_d[:, :, :], writes=[consts["cmask"]])
        cx.op("dve", lambda en: en.memset(consts["ones8"][:], 1.0), writes=[consts["ones8"]])
        cx.op("dve", lambda en: en.memset(epsb[:], EPS), writes=[epsb])

        last = (NL == 1)
        for (g, d) in GROUPS:
            attn_group_pass(cx, T, g, d, x0, cs, a_w_in[0], (ngT, 0), parts[g], consts)
        attn_final_pass(cx, T, x0, parts, a_w_in[0], a_w_out[0], (ngT, 0), y if last else xs[0], consts,
                        final_g=None)
        if NL >= 2:
            layer_b(cx, T, xs[0], y if NL == 2 else xs[1], din, ngT, 1, scr, consts)
        if NL >= 3:
            layer_c(cx, T, xs[1], y if NL == 3 else xs[0], din, ngT, 2, scr, consts)
        if NL >= 4:
            x3 = xs[0]
            pk3 = nc.dram_tensor("pkx3", [1024, D], F32, kind="Internal").ap()
            ga3 = nc.dram_tensor("gax3", [2048, D], F32, kind="Internal", addr_space="Local").ap()
            halo3 = nc.dram_tensor("halo3", [2048, D], F32, kind="Internal").ap()
            with ExitStack() as s3:
                g0 = [cx.sb(s3, "g0", [128, D], F32) for _ in range(2)]
                g1 = [cx.sb(s3, "g1", [128, D], F32) for _ in range(2)]
                zt = cx.sb(s3, "zt3", [128, D], F32)
                cx.op("dve", lambda en: en.memset(zt[:], 0.0), writes=[zt])
                for i in range(8):
                    a = g0[i % 2]
                    cx.dma("sp", a[:], x3[T - 1024 + i * 128:T - 1024 + (i + 1) * 128, :], writes=[a])
                    cx.dma("pool", pk3[i * 128:(i + 1) * 128, :], a[:], reads=[a])
                    cx.dma("pool", halo3[i * 128:(i + 1) * 128, :], zt[:], reads=[zt])
                cx.barrier()
                for i in range(8):
                    allgather(cx, pk3[i * 128:(i + 1) * 128, :], ga3[i * 256:(i + 1) * 256, :])
                sel = consts["sel"]
                for i in range(8):
                    a, b = g0[i % 2], g1[i % 2]
                    cx.dma("sp", a[:], ga3[i * 256:i * 256 + 128, :], writes=[a])
                    cx.dma("sp", b[:], ga3[i * 256 + 128:(i + 1) * 256, :], writes=[b])
                    cx.op("dve", lambda en, a=a: en.tensor_scalar(out=a[:], in0=a[:], scalar1=sel[:, 0:1], scalar2=None,
                                                                 op0=ALU.mult), reads=[a, sel], writes=[a])
                    cx.op("dve", lambda en, a=a, b=b: en.scalar_tensor_tensor(out=a[:], in0=b[:], scalar=sel[:, 1:2], in1=a[:],
                                                                            op0=ALU.mult, op1=ALU.add),
                          reads=[a, b, sel], writes=[a])
                    cx.dma("pool", halo3[1024 + i * 128:1024 + (i + 1) * 128, :], a[:], reads=[a])
                cx.barrier()
            for (g, d) in GROUPS:
                attn_group_pass(cx, T, g, d, x3, cs, a_w_in[1], (ngT, 3), parts[g], consts, halo_x=halo3, halo_cs=cs_h3)
            with ExitStack() as s4:
                fgb = cx.sb(s4, "fgb", [128, D], F32)
                cx.dma("sp", fgb[:], final_gb[:, :], writes=[fgb])
                attn_final_pass(cx, T, x3, parts, a_w_in[1], a_w_out[1], (ngT, 3), y, consts, final_g=fgb)
        cx.barrier()
    return nc


def fm(v):
    v = np.asarray(v, np.float32)
    lead = v.shape[:-1]
    return np.ascontiguousarray(np.moveaxis(v.reshape(lead + (8, 128)), -1, 0))


def const_inputs():
    ident = np.eye(128, dtype=np.float32).astype(ml_dtypes.bfloat16)
    p = np.arange(128)[:, None]
    f = np.arange(128)[None, :]
    NEG = -30000.0
    m = np.zeros((128, 4, 128), np.float32)
    m[:, 3, :] = np.where((127 - p) <= f - 64, 0.0, NEG)
    m[:, 0, :] = np.where(p >= f + 64, 0.0, NEG)
    m[:, 1, :] = np.where(np.abs(f - p) <= 64, 0.0, NEG)
    m[:, 2, :] = np.where(p <= f - 64, 0.0, NEG)
    cm = np.zeros((128, 2, 128), np.float32)
    cm[:, 0, :] = (p <= f)
    cm[:, 1, :] = (p >= f)
    return {"ident": ident, "maskb": m.astype(ml_dtypes.bfloat16), "cmask": cm.astype(ml_dtypes.bfloat16)}


def rope_table(pos):
    half = 16
    inv = (np.float32(500000.0) ** (-(np.arange(half, dtype=np.float32) * np.float32(2.0)) / np.float32(32.0))).astype(np.float32)
    ang = pos.astype(np.float32)[:, None] * inv[None, :]
    return np.concatenate([np.cos(ang), np.sin(ang)], axis=1).astype(np.float32)


def core_inputs(x_ext, pos, flag, W, reverse, sel=(0.0, 0.0), pos_h3=None):
    d = dict(const_inputs())
    d["x0"] = np.ascontiguousarray(x_ext, np.float32)
    d["cs"] = rope_table(pos)
    d["a_w_in"] = W["a_w_in"]
    d["a_w_out"] = W["a_w_out"]
    d["norm_gT"] = fm(W["norm_g"])
    d["final_gb"] = np.ascontiguousarray(np.broadcast_to(W["final_g"][None, :], (128, D)), np.float32)
    d["flag8"] = np.full((128, 8, 1), flag, np.float32)
    dd = [1, 0] if reverse else [0, 1]
    d["b_w_in"] = W["b_w_in"][0]
    d["b_w_out"] = W["b_w_out"][0]
    d["b_w_r"] = np.ascontiguousarray(W["b_w_r"][0][dd])
    d["b_w_i"] = np.ascontiguousarray(W["b_w_i"][0][dd])
    d["b_b_rT"] = fm(W["b_b_r"][0][dd])
    d["b_b_iT"] = fm(W["b_b_i"][0][dd])
    d["b_lamT"] = fm(W["b_lambda"][0][dd])
    cw = W["b_conv_w"][0]
    z = np.zeros((1, D), np.float32)
    w5 = np.concatenate([cw, z], 0) if not reverse else np.concatenate([z, cw[::-1]], 0)
    d["b_w5T"] = fm(w5)
    d["b_cbT"] = fm(W["b_conv_b"][0])
    cw_in = W["c_w_in"][0]
    if reverse:
        cw_in = np.concatenate([cw_in[:, 0:1024], cw_in[:, 2048:3072], cw_in[:, 1024:2048], cw_in[:, 3072:]], 1)
    d["c_w_in"] = np.ascontiguousarray(cw_in)
    d["c_w_out"] = W["c_w_out"][0]
    d["c_lbT"] = fm(W["c_lower_bounds"][:, dd, :]).reshape(128, 4, 16)
    d["sel"] = np.ascontiguousarray(np.broadcast_to(np.asarray(sel, np.float32)[None, :], (128, 2)))
    d["cs_h3"] = rope_table(pos_h3 if pos_h3 is not None else np.zeros(2048))
    d["c_gnb"] = np.ascontiguousarray(np.broadcast_to(W["c_gnorm_g"][0][None, :], (128, 128)), np.float32)
    return d


def rglru_stage1(cx, T, x_src, w_in, gl, xbT, sgT, consts):
    ident = consts["ident"]
    NT = 512
    with ExitStack() as sc:
        wt = cx.sb(sc, "wt", [128, 8, 2048], BF16)
        stg = [cx.sb(sc, "stg", [128, 2048], F32) for _ in range(3)]
        load_weight(cx, stg, wt, 0, w_in[:, :], 2048, g=gl)
        xt = [cx.sb(sc, "xt", [128, D], F32) for _ in range(2)]
        junk = cx.sb(sc, "junk", [128, D], F32)
        ssq = [cx.sb(sc, "ssq", [128, 1], F32) for _ in range(2)]
        std = [cx.sb(sc, "std", [128, 1], F32) for _ in range(2)]
        rstd = [cx.sb(sc, "rstd", [128, 1], F32) for _ in range(2)]
        xn = [cx.sb(sc, "xn", [128, D], BF16) for _ in range(2)]
        xnT = [cx.sb(sc, "xnT", [128, 8, NT], BF16) for _ in range(2)]
        ob = [cx.sb(sc, "ob", [128, NT], F32) for _ in range(4)]
        zt = cx.sb(sc, "zt", [128, 8, 2], F32)
        pj = [cx.ps(sc, "pj", [128, 512], F32) for _ in range(2)]
        tp = [cx.ps(sc, "tp", [128, 1024], BF16) for _ in range(2)]
        cx.op("dve", lambda en: en.memset(zt[:], 0.0), writes=[zt])
        cx.dma("pool", xbT[:, :, 0:2].rearrange("b p c -> p b c"), zt[:, :, :], reads=[zt])
        cx.dma("pool", xbT[:, :, T + 2:T + 4].rearrange("b p c -> p b c"), zt[:, :, :], reads=[zt])
        k = 0
        for ci in range(T // NT):
            t0 = ci * NT
            xs_ = xnT[ci % 2]
            for tt in range(4):
                s2 = k % 2
                k += 1
                cx.dma("sp", xt[s2][:], x_src[t0 + tt * 128:t0 + (tt + 1) * 128, :], writes=[xt[s2]])
                norm_tile(cx, xt[s2], junk, ssq[s2], std[s2], rstd[s2], xn[s2])
                tpb = tp[k % 2]
                for kc in range(8):
                    cx.op("pe", lambda en, kc=kc, tpb=tpb, s2=s2: en.transpose(
                        out=tpb[:, kc * 128:(kc + 1) * 128], in_=xn[s2][:, kc * 128:(kc + 1) * 128],
                        identity=ident[:]), reads=[xn[s2], ident], writes=[tpb])
                cx.op("dve", lambda en, tpb=tpb, tt=tt: en.tensor_copy(
                    out=xs_[:, :, tt * 128:(tt + 1) * 128], in_=tpb[:, :].rearrange("p (c t) -> p c t", t=128)),
                    reads=[tpb], writes=[xs_])
            for cb in range(16):
                pb = pj[cb % 2]
                o_ = ob[cb % 4]
                for kc in range(8):
                    cx.op("pe", lambda en, kc=kc, cb=cb, pb=pb: en.matmul(
                        pb[:, :], lhsT=wt[:, kc, cb * 128:(cb + 1) * 128], rhs=xs_[:, kc, :],
                        start=(kc == 0), stop=(kc == 7)), reads=[wt, xs_], writes=[pb])
                if cb < 8:
                    cx.op("dve", lambda en, pb=pb, o_=o_: en.tensor_copy(out=o_[:, :], in_=pb[:, :]),
                          reads=[pb], writes=[o_])
                    cx.dma("pool", xbT[cb, :, 2 + t0:2 + t0 + NT], o_[:, :], reads=[o_])
                else:
                    cx.op("act", lambda en, pb=pb, o_=o_: en.activation(out=o_[:, :], in_=pb[:, :], func=AF.Silu),
                          reads=[pb], writes=[o_])
                    cx.dma("pool", sgT[cb - 8, :, t0:t0 + NT], o_[:, :], reads=[o_])
        cx.barrier()


def rglru_pass(cx, T, dr, xbT, sgT, res_src, dst, P, state, consts):
    NT = 512
    RD = 4
    ones8 = consts["ones8"]
    wr, wi, wo = P["wr"], P["wi"], P["wo"]
    with ExitStack() as sc:
        def ring(name, shape, dt, k=RD):
            return [cx.sb(sc, name, shape, dt) for _ in range(k)]
        xbh = ring("xbh", [128, NT + 4], F32)
        sgc = ring("sgc", [128, NT], F32)
        xc = ring("xc", [128, NT], F32)
        xcb = ring("xcb", [128, NT], BF16)
        rr = ring("rr", [128, NT], F32)
        ii = ring("ii", [128, NT], F32)
        aa = ring("aa", [128, NT], F32)
        mm = ring("mm", [128, NT], F32)
        uu = ring("uu", [128, NT], F32)
        hh = ring("hh", [128, NT], F32)
        yg = [cx.sb(sc, "yg", [128, 8, NT], BF16) for _ in range(2)]
        rt_ = [cx.sb(sc, "rt", [128, D], F32) for _ in range(3)]
        xo = [cx.sb(sc, "xo", [128, D], F32) for _ in range(2)]
        pr = [cx.ps(sc, "pr", [128, 512], F32) for _ in range(3)]
        pi = [cx.ps(sc, "pi", [128, 512], F32) for _ in range(3)]
        pj = [cx.ps(sc, "pj", [128, 512], F32) for _ in range(2)]
        nch = T // NT
        order = list(range(nch)) if dr == 0 else list(range(nch - 1, -1, -1))
        items = [(oi, ci, cb) for oi, ci in enumerate(order) for cb in range(8)]
        NI = len(items)
        kk = [0]

        def s1(u):
            oi, ci, cb = items[u]
            t0 = ci * NT
            s = u % RD
            X, XC = xbh[s], xc[s]
            cx.dma("sp", X[:], xbT[cb, :, t0:t0 + NT + 4], writes=[X])
            cx.dma("sp", sgc[s][:], sgT[cb, :, t0:t0 + NT], writes=[sgc[s]])
            cx.op("act", lambda en: en.activation(
                out=XC[:, :], in_=X[:, 0:NT], func=AF.Identity, scale=P["w5"][:, 0, cb:cb + 1],
                bias=P["cb"][:, cb:cb + 1]), reads=[X, P["w5"], P["cb"]], writes=[XC])
            for j in range(1, 5):
                cx.op("dve", lambda en, j=j: en.scalar_tensor_tensor(
                    out=XC[:, :], in0=X[:, j:j + NT], scalar=P["w5"][:, j, cb:cb + 1], in1=XC[:, :],
                    op0=ALU.mult, op1=ALU.add), reads=[X, P["w5"], XC], writes=[XC])
            cx.op("pool", lambda en: en.tensor_copy(out=xcb[s][:, :], in_=XC[:, :]), reads=[XC], writes=[xcb[s]])
            p3 = u % 3
            cx.op("pe", lambda en: en.matmul(pr[p3][:, :], lhsT=wr[:, dr * 8 + cb, :], rhs=xcb[s][:, :],
                                             start=True, stop=True), reads=[wr, xcb[s]], writes=[pr[p3]])
            cx.op("pe", lambda en: en.matmul(pi[p3][:, :], lhsT=wi[:, dr * 8 + cb, :], rhs=xcb[s][:, :],
                                             start=True, stop=True), reads=[wi, xcb[s]], writes=[pi[p3]])

        def s2(u):
            oi, ci, cb = items[u]
            s = u % RD
            p3 = u % 3
            cx.op("act", lambda en: en.activation(out=rr[s][:, :], in_=pr[p3][:, :], func=AF.Sigmoid,
                                                  bias=P["br"][:, dr, cb:cb + 1]), reads=[pr[p3], P["br"]], writes=[rr[s]])
            cx.op("act", lambda en: en.activation(out=ii[s][:, :], in_=pi[p3][:, :], func=AF.Sigmoid,
                                                  bias=P["bi"][:, dr, cb:cb + 1]), reads=[pi[p3], P["bi"]], writes=[ii[s]])
            cx.op("act", lambda en: en.activation(out=aa[s][:, :], in_=rr[s][:, :], func=AF.Exp,
                                                  scale=P["cneg"][:, dr, cb:cb + 1]), reads=[rr[s], P["cneg"]], writes=[aa[s]])
            cx.op("act", lambda en: en.activation(out=mm[s][:, :], in_=rr[s][:, :], func=AF.Exp,
                                                  scale=P["cneg2"][:, dr, cb:cb + 1]), reads=[rr[s], P["cneg2"]], writes=[mm[s]])
            cx.op("act", lambda en: en.activation(out=mm[s][:, :], in_=mm[s][:, :], func=AF.Sqrt, scale=-1.0,
                                                  bias=ones8[:, 0, :]), reads=[mm[s], ones8], writes=[mm[s]])
            cx.op("pool", lambda en: en.tensor_tensor(out=uu[s][:, :], in0=ii[s][:, :], in1=xc[s][:, :], op=ALU.mult),
                  reads=[ii[s], xc[s]], writes=[uu[s]])
            cx.op("pool", lambda en: en.tensor_tensor(out=uu[s][:, :], in0=uu[s][:, :], in1=mm[s][:, :], op=ALU.mult),
                  reads=[uu[s], mm[s]], writes=[uu[s]])

        def s3(u):
            oi, ci, cb = items[u]
            s = u % RD
            ygc = yg[oi % 2]
            if dr == 0:
                cx.op("dve", lambda en: en.tensor_tensor_scan(
                    out=hh[s][:, :], data0=aa[s][:, :], data1=uu[s][:, :], initial=state[:, cb:cb + 1],
                    op0=ALU.mult, op1=ALU.add), reads=[aa[s], uu[s], state], writes=[hh[s]])
                lastc = NT - 1
            else:
                cx.op("dve", lambda en: en.tensor_tensor_scan(
                    out=hh[s][:, ::-1], data0=aa[s][:, ::-1], data1=uu[s][:, ::-1], initial=state[:, cb:cb + 1],
                    op0=ALU.mult, op1=ALU.add), reads=[aa[s], uu[s], state], writes=[hh[s]])
                lastc = 0
            cx.op("dve", lambda en: en.tensor_copy(out=state[:, cb:cb + 1], in_=hh[s][:, lastc:lastc + 1]),
                  reads=[hh[s]], writes=[state])
            cx.op("pool", lambda en: en.tensor_tensor(out=ygc[:, cb, :], in0=hh[s][:, :], in1=sgc[s][:, :], op=ALU.mult),
                  reads=[hh[s], sgc[s]], writes=[ygc])
            if cb == 7:
                out_chunk(oi, ci)

        def out_chunk(oi, ci):
            t0 = ci * NT
            ygc = yg[oi % 2]
            for tt in range(4):
                k = kk[0]
                kk[0] += 1
                r3 = rt_[k % 3]
                x2 = xo[k % 2]
                rows = slice(t0 + tt * 128, t0 + (tt + 1) * 128)
                cx.dma("sp", r3[:], res_src[rows, :], writes=[r3])
                for c2 in range(2):
                    pb = pj[c2]
                    for cb in range(8):
                        cx.op("pe", lambda en, cb=cb, c2=c2, pb=pb, tt=tt: en.matmul(
                            pb[:, :], lhsT=ygc[:, cb, tt * 128:(tt + 1) * 128], rhs=wo[:, cb, c2 * 512:(c2 + 1) * 512],
                            start=(cb == 0), stop=(cb == 7)), reads=[ygc, wo], writes=[pb])
                    cx.op("dve", lambda en, c2=c2, pb=pb, x2=x2, r3=r3: en.tensor_tensor(
                        out=x2[:, c2 * 512:(c2 + 1) * 512], in0=pb[:, :], in1=r3[:, c2 * 512:(c2 + 1) * 512],
                        op=ALU.add), reads=[pb, r3], writes=[x2])
                cx.dma("pool", dst[rows, :], x2[:], reads=[x2])

        for u in range(NI + 2):
            if u < NI:
                s1(u)
            if 0 <= u - 1 < NI:
                s2(u - 1)
            if 0 <= u - 2 < NI:
                s3(u - 2)
        cx.barrier()


def layer_b(cx, T, x_src, x_dst, din, ngT, layer, scr, consts):
    ones8 = consts["ones8"]
    with ExitStack() as sc:
        P = {}
        P["wr"] = cx.sb(sc, "wr", [128, 16, 128], BF16)
        P["wi"] = cx.sb(sc, "wi", [128, 16, 128], BF16)
        P["wo"] = cx.sb(sc, "wo", [128, 8, 1024], BF16)
        for nm in ("br", "bi", "lam", "cneg", "cneg2"):
            P[nm] = cx.sb(sc, nm, [128, 2, 8], F32)
        P["w5"] = cx.sb(sc, "w5", [128, 5, 8], F32)
        P["cb"] = cx.sb(sc, "cb", [128, 8], F32)
        state = cx.sb(sc, "state", [128, 8], F32)
        with ExitStack() as s2:
            stg = [cx.sb(s2, "stg", [128, 2048], F32) for _ in range(2)]
            for (nm, src) in (("wr", din["b_w_r"]), ("wi", din["b_w_i"])):
                cx.dma("sp", stg[0][:, :].rearrange("p (a o) -> p a o", o=128),
                       src.rearrange("d b c o -> c (d b) o"), writes=[stg[0]])
                cx.op("dve", lambda en, nm=nm: en.tensor_copy(out=P[nm][:, :, :],
                                                              in_=stg[0][:, :].rearrange("p (a o) -> p a o", o=128)),
                      reads=[stg[0]], writes=[P[nm]])
            load_weight(cx, stg, P["wo"], 0, din["b_w_out"][:, :], 1024, g=None)
            cx.dma("sp", P["br"][:], din["b_b_rT"][:, :, :], writes=[P["br"]])
            cx.dma("sp", P["bi"][:], din["b_b_iT"][:, :, :], writes=[P["bi"]])
            cx.dma("sp", P["lam"][:], din["b_lamT"][:, :, :], writes=[P["lam"]])
            cx.dma("sp", P["w5"][:], din["b_w5T"][:, :, :], writes=[P["w5"]])
            cx.dma("sp", P["cb"][:], din["b_cbT"][:, :], writes=[P["cb"]])
            cx.op("act", lambda en: en.activation(out=P["cneg"][:], in_=P["lam"][:], func=AF.Exp, scale=-1.0),
                  reads=[P["lam"]], writes=[P["cneg"]])
            cx.op("act", lambda en: en.activation(out=P["cneg"][:], in_=P["cneg"][:], func=AF.Ln, bias=ones8[:, 0, :]),
                  reads=[P["cneg"], ones8], writes=[P["cneg"]])
            cx.op("dve", lambda en: en.tensor_scalar(out=P["cneg2"][:], in0=P["cneg"][:], scalar1=-16.0, scalar2=None,
                                                     op0=ALU.mult), reads=[P["cneg"]], writes=[P["cneg2"]])
            cx.op("dve", lambda en: en.tensor_scalar(out=P["cneg"][:], in0=P["cneg"][:], scalar1=-8.0, scalar2=None,
                                                     op0=ALU.mult), reads=[P["cneg"]], writes=[P["cneg"]])
            cx.barrier()
        rglru_stage1(cx, T, x_src, din["b_w_in"], (ngT, layer), scr["xbT"], scr["sgT"], consts)
        sel = consts["sel"]
        state2 = cx.sb(sc, "state2", [128, 8], F32)
        with ExitStack() as s3:
            hin = cx.sb(s3, "hin", [128, 16], F32)
            hout = cx.sb(s3, "hout", [128, 16], F32)
            hsw = cx.sb(s3, "hsw", [128, 8, 2], F32)
            cx.dma("sp", hin[:, :].rearrange("p (b c) -> p b c", c=2),
                   scr["xbT"][:, :, T:T + 2].rearrange("b p c -> p b c"), writes=[hin])
            cx.barrier()
            exchange_sb(cx, s3, hin[:, :], hout[:, :], 16, sel)
            hv = hout[:, :].rearrange("p (b c) -> p b c", c=2)
            cx.op("dve", lambda en: en.tensor_copy(out=hsw[:, :, 0:1], in_=hv[:, :, 1:2]), reads=[hout], writes=[hsw])
            cx.op("dve", lambda en: en.tensor_copy(out=hsw[:, :, 1:2], in_=hv[:, :, 0:1]), reads=[hout], writes=[hsw])
            cx.dma("pool", scr["xbT"][:, :, T + 2:T + 4].rearrange("b p c -> p b c"), hsw[:, :, :], reads=[hsw])
            cx.barrier()
        cx.op("dve", lambda en: en.memset(state[:], 0.0), writes=[state])
        rglru_pass(cx, T, 0, scr["xbT"], scr["sgT"], x_src, scr["xp"], P, state, consts)
        with ExitStack() as s3:
            exchange_sb(cx, s3, state[:, :], state2[:, :], 8, sel)
        rglru_pass(cx, T, 1, scr["xbT"], scr["sgT"], scr["xp"], x_dst, P, state2, consts)


def hgrn_pass(cx, T, dr, x_src, din, gl, P, of, dst, consts, final, S):
    ident = consts["ident"]
    NT = 256
    NC = NT // 128
    w_in = din["c_w_in"]
    with ExitStack() as sc:
        ncol = 4096 if final else 3072
        wt = cx.sb(sc, "wt", [128, 8, ncol], BF16)
        if final:
            wo = cx.sb(sc, "wo", [128, 8, 1024], BF16)
        with ExitStack() as s2:
            stg = [cx.sb(s2, "stg", [128, 1024], F32) for _ in range(4)]
            zc = 1024 + 1024 * dr
            load_weight(cx, stg, wt, 0, w_in[:, 0:1024], 1024, g=gl)
            load_weight(cx, stg, wt, 1024, w_in[:, zc:zc + 1024], 1024, g=gl)
            load_weight(cx, stg, wt, 2048, w_in[:, 3072:4096], 1024, g=gl)
            if final:
                load_weight(cx, stg, wt, 3072, w_in[:, 4096:5120], 1024, g=gl)
                load_weight(cx, stg, wo, 0, din["c_w_out"][:, :], 1024, g=None)
            cx.barrier()
        Sp = cx.sb(sc, "Sp", [128, 8, 128], BF16)
        tmpS = cx.sb(sc, "tmpS", [128, 8, 128], F32)
        xt = [cx.sb(sc, "xt", [128, D], F32)] * 2
        ssq = [cx.sb(sc, "ssq", [128, 1], F32) for _ in range(2)]
        std = [cx.sb(sc, "std", [128, 1], F32) for _ in range(2)]
        rstd = [cx.sb(sc, "rstd", [128, 1], F32) for _ in range(2)]
        xn = [cx.sb(sc, "xn", [128, D], BF16)] * 2
        xnT = cx.sb(sc, "xnT", [128, 8, NT], BF16)
        nbq = 1 if final else 2
        qTs = [cx.sb(sc, "qT", [128, 8, NT], F32) for _ in range(nbq)]
        As = [cx.sb(sc, "A", [128, 8, NT], F32) for _ in range(nbq)]
        L = cx.sb(sc, "L", [128, 8, NT], F32)
        bb = cx.sb(sc, "bb", [128, 8, NT], F32)
        junk = cx.sb(sc, "junk", [128, D], BF16)
        qd = [cx.sb(sc, "qd", [128, 8, NT], BF16) for _ in range(2)]
        kd = [cx.sb(sc, "kd", [128, 8, NT], BF16) for _ in range(2)]
        vt = [cx.sb(sc, "vt", [128, NC, 1024], BF16) for _ in range(2)]
        gg = [cx.sb(sc, "gg", [128, 8 * NC], F32) for _ in range(2)]
        e2 = [cx.sb(sc, "e2", [128, 8 * NC], F32) for _ in range(2)]
        er = [cx.sb(sc, "er", [128, 8 * NC], F32) for _ in range(2)]
        attm = [cx.sb(sc, "attm", [128, 8, 128], BF16) for _ in range(2)]
        kdtok = [cx.sb(sc, "kdtok", [128, 8, 128], BF16) for _ in range(2)]
        osb = [cx.sb(sc, "osb", [128, D], F32)] * 2
        onesr = cx.sb(sc, "onesr", [128, NT], F32)
        cx.op("dve", lambda en: en.memset(onesr[:], 1.0), writes=[onesr])
        rc_ = 0 if dr == 0 else 127
        for c_ in range(NC):
            cx.op("dve", lambda en, c_=c_: en.memset(onesr[:, c_ * 128 + rc_:c_ * 128 + rc_ + 1], 0.0), writes=[onesr])
        if final:
            sgt = [cx.sb(sc, "sgt", [128, NC, 1024], F32) for _ in range(2)]
            oft = [cx.sb(sc, "oft", [128, D], F32)] * 2
            rt_ = [cx.sb(sc, "rt", [128, D], F32)] * 2
            s8 = cx.sb(sc, "s8", [128, 8], F32)
            r8 = cx.sb(sc, "r8", [128, 8], F32)
            og = [cx.sb(sc, "og", [128, D], BF16)] * 2
            ogT = [cx.sb(sc, "ogT", [128, 8, 128], BF16)] * 2
            xo = rt_
        pj = [cx.ps(sc, "pj", [128, 512], F32) for _ in range(2)]
        tp = cx.ps(sc, "tp", [128, 1024], BF16)
        scp = cx.ps(sc, "scp", [128, 512], F32)
        po = [cx.ps(sc, "po", [128, 512], F32) for _ in range(2)]
        su = [cx.ps(sc, "su", [128, 512], F32) for _ in range(2)]
        nst = T // NT
        order = list(range(nst)) if dr == 0 else list(range(nst - 1, -1, -1))
        corder = list(range(NC)) if dr == 0 else list(range(NC - 1, -1, -1))
        lastc = 127 if dr == 0 else 0
        kcnt = [0, 0]

        def X(q):
            si = order[q]
            t0 = si * NT
            vq = vt[q % 2]
            fm, tm = [], []
            qT, A = qTs[q % nbq], As[q % nbq]

            def tile_piece(tt):
                s2 = kcnt[0] % 2
                kcnt[0] += 1
                cx.dma("sp", xt[s2][:], x_src[t0 + tt * 128:t0 + (tt + 1) * 128, :], writes=[xt[s2]])
                norm_tile(cx, xt[s2], junk, ssq[s2], std[s2], rstd[s2], xn[s2])
                for kc in range(8):
                    cx.op("pe", lambda en, kc=kc, s2=s2: en.transpose(
                        out=tp[:, kc * 128:(kc + 1) * 128], in_=xn[s2][:, kc * 128:(kc + 1) * 128],
                        identity=ident[:]), reads=[xn[s2], ident], writes=[tp])
                cx.op("dve", lambda en, tt=tt: en.tensor_copy(
                    out=xnT[:, :, tt * 128:(tt + 1) * 128], in_=tp[:, :].rearrange("p (c t) -> p c t", t=128)),
                    reads=[tp], writes=[xnT])

            for tt in range(NC):
                fm.append(lambda tt=tt: tile_piece(tt))

            def fm_piece(cb):
                cbo = (cb % 8) + (8 if cb < 8 else 0)
                pb = pj[cb % 2]
                for kc in range(8):
                    cx.op("pe", lambda en, kc=kc, cbo=cbo, pb=pb: en.matmul(
                        pb[:, 0:NT], lhsT=wt[:, kc, cbo * 128:(cbo + 1) * 128], rhs=xnT[:, kc, :],
                        start=(kc == 0), stop=(kc == 7)), reads=[wt, xnT], writes=[pb])
                if cbo < 8:
                    cx.op("act", lambda en, pb=pb, cbo=cbo: en.activation(out=qT[:, cbo, :], in_=pb[:, 0:NT], func=AF.Copy),
                          reads=[pb], writes=[qT])
                else:
                    cx.op("act", lambda en, pb=pb, cbo=cbo: en.activation(out=A[:, cbo - 8, :], in_=pb[:, 0:NT],
                                                                         func=AF.Sigmoid), reads=[pb], writes=[A])

            for cb in range(16):
                fm.append(lambda cb=cb: fm_piece(cb))

            def tm_piece(tt, c):
                if True:
                    pb = pj[c % 2]
                    for kc in range(8):
                        cx.op("pe", lambda en, kc=kc, c=c, pb=pb, tt=tt: en.matmul(
                            pb[:, :], lhsT=xnT[:, kc, tt * 128:(tt + 1) * 128],
                            rhs=wt[:, kc, 2048 + c * 512:2048 + (c + 1) * 512],
                            start=(kc == 0), stop=(kc == 7)), reads=[xnT, wt], writes=[pb])
                    if c < 2:
                        cx.op("dve", lambda en, pb=pb, c=c, tt=tt: en.tensor_copy(
                            out=vq[:, tt, c * 512:(c + 1) * 512], in_=pb[:, :]), reads=[pb], writes=[vq])
                    else:
                        cx.op("act", lambda en, pb=pb, c=c, tt=tt: en.activation(
                            out=sgt[q % 2][:, tt, (c - 2) * 512:(c - 1) * 512], in_=pb[:, :], func=AF.Silu),
                            reads=[pb], writes=[sgt[q % 2]])

            for tt in range(NC):
                for c in range(4 if final else 2):
                    tm.append(lambda tt=tt, c=c: tm_piece(tt, c))
            return fm, tm

        def Y(q):
            qq = q % 2
            qT, A = qTs[q % nbq], As[q % nbq]
            cx.op("dve", lambda en: en.tensor_tensor(out=A[:, :, :], in0=A[:, :, :], in1=bcast(P["oml"][:, dr, :], 1, NT),
                                                     op=ALU.mult), reads=[A, P["oml"]], writes=[A])
            cx.op("dve", lambda en: en.tensor_tensor(out=A[:, :, :], in0=A[:, :, :], in1=bcast(P["lb"][:, dr, :], 1, NT),
                                                     op=ALU.add), reads=[A, P["lb"]], writes=[A])
            cx.op("act", lambda en: en.activation(out=L[:], in_=A[:], func=AF.Ln), reads=[A], writes=[L])
            cx.op("pool", lambda en: en.tensor_scalar(out=A[:], in0=A[:], scalar1=-1.0, scalar2=1.0,
                                                      op0=ALU.mult, op1=ALU.add), reads=[A], writes=[A])
            for h in range(8):
                if dr == 0:
                    cx.op("dve", lambda en, h=h: en.tensor_tensor_scan(
                        out=bb[:, h, :], data0=onesr[:, :], data1=L[:, h, :], initial=0.0,
                        op0=ALU.mult, op1=ALU.add), reads=[onesr, L], writes=[bb])
                else:
                    cx.op("dve", lambda en, h=h: en.tensor_tensor_scan(
                        out=bb[:, h, ::-1], data0=onesr[:, ::-1], data1=L[:, h, ::-1], initial=0.0,
                        op0=ALU.mult, op1=ALU.add), reads=[onesr, L], writes=[bb])
            bbv = bb[:, :, :].rearrange("p h (c t) -> p (h c) t", t=128)
            Lv = L[:, :, :].rearrange("p h (c t) -> p (h c) t", t=128)
            cx.op("dve", lambda en: en.tensor_tensor(out=Lv, in0=bbv, in1=bcast(bbv[:, :, 64], 1, 128), op=ALU.subtract),
                  reads=[bb], writes=[L])
            cx.op("act", lambda en: en.activation(out=gg[qq][:, :], in_=bbv[:, :, lastc], func=AF.Exp), reads=[bb], writes=[gg[qq]])
            cx.op("act", lambda en: en.activation(out=er[qq][:, :], in_=bbv[:, :, 64], func=AF.Exp), reads=[bb], writes=[er[qq]])
            cx.op("act", lambda en: en.activation(out=e2[qq][:, :], in_=Lv[:, :, lastc], func=AF.Exp), reads=[L], writes=[e2[qq]])
            cx.op("act", lambda en: en.activation(out=bb[:], in_=L[:], func=AF.Exp), reads=[L], writes=[bb])
            cx.op("act", lambda en: en.activation(out=L[:], in_=L[:], func=AF.Exp, scale=-1.0), reads=[L], writes=[L])
            cx.op("dve", lambda en: en.tensor_tensor(out=qd[qq][:], in0=qT[:], in1=bb[:], op=ALU.mult),
                  reads=[qT, bb], writes=[qd[qq]])
            cx.op("pool", lambda en: en.tensor_tensor(out=kd[qq][:], in0=A[:], in1=L[:], op=ALU.mult),
                  reads=[A, L], writes=[kd[qq]])

        def Z(q):
            si = order[q]
            t0 = si * NT
            qq = q % 2
            QD, KD, VT, GG, E2, ER = qd[qq], kd[qq], vt[qq], gg[qq], e2[qq], er[qq]
            segs1, segs2, segs3 = [], [], []
            for ci_, c in enumerate(corder):
                a2 = ci_ % 2
                segs1.append(lambda c=c, a2=a2: seg1(c, a2))
                segs2.append(lambda c=c, a2=a2: seg2(c, a2))
                if final:
                    segs3.append(lambda c=c, a2=a2: seg3(c, a2))

            def seg1(c, a2):
                cs_ = slice(c * 128, (c + 1) * 128)
                am, kt = attm[a2], kdtok[a2]
                for half in range(2):
                    for hh in range(4):
                        h = half * 4 + hh
                        cx.op("pe", lambda en, h=h, hh=hh: en.matmul(
                            scp[:, hh * 128:(hh + 1) * 128], lhsT=KD[:, h, cs_], rhs=QD[:, h, cs_],
                            start=True, stop=True), reads=[KD, QD], writes=[scp])
                    cx.op("dve", lambda en, half=half: en.tensor_tensor(
                        out=am[:, half * 4:half * 4 + 4, :], in0=scp[:, :].rearrange("p (h t) -> p h t", t=128),
                        in1=bcast(consts["cmask"][:, dr, :], 0, 4), op=ALU.mult),
                        reads=[scp, consts["cmask"]], writes=[am])
                for h in range(8):
                    cx.op("pe", lambda en, h=h: en.transpose(out=tp[:, h * 128:(h + 1) * 128], in_=KD[:, h, cs_],
                                                             identity=ident[:]),
                          reads=[KD, ident], writes=[tp])
                cx.op("act", lambda en: en.activation(out=kt[:, :, :], in_=tp[:, :].rearrange("p (c t) -> p c t", t=128),
                                                      func=AF.Copy), reads=[tp], writes=[kt])

            def seg2(c, a2):
                cs_ = slice(c * 128, (c + 1) * 128)
                am, kt = attm[a2], kdtok[a2]
                cx.op("pool", lambda en: en.tensor_tensor(out=Sp[:, :, :], in0=S[:, :, :],
                                                          in1=bcast(ER[:, c::NC], 1, 128), op=ALU.mult),
                      reads=[S, ER], writes=[Sp])
                for h in range(8):
                    pob = po[h // 4]
                    oc = slice((h % 4) * 128, (h % 4 + 1) * 128)
                    cx.op("pe", lambda en, h=h, pob=pob, oc=oc: en.matmul(
                        pob[:, oc], lhsT=QD[:, h, cs_], rhs=Sp[:, h, :], start=True, stop=False),
                        reads=[QD, Sp], writes=[pob])
                    cx.op("pe", lambda en, h=h, pob=pob, oc=oc: en.matmul(
                        pob[:, oc], lhsT=am[:, h, :], rhs=VT[:, c, h * 128:(h + 1) * 128], start=False, stop=True),
                        reads=[am, VT], writes=[pob])
                    cx.op("pe", lambda en, h=h, oc=oc: en.matmul(
                        su[h // 4][:, oc], lhsT=kt[:, h, :], rhs=VT[:, c, h * 128:(h + 1) * 128], start=True, stop=True),
                        reads=[kt, VT], writes=[su[h // 4]])
                cx.op("dve", lambda en: en.tensor_tensor(out=S[:, :, :], in0=S[:, :, :], in1=bcast(GG[:, c::NC], 1, 128),
                                                         op=ALU.mult), reads=[S, GG], writes=[S])
                for b2 in range(2):
                    e2v = E2[:, c::NC]
                    cx.op("dve", lambda en, b2=b2, e2v=e2v: en.tensor_tensor(
                        out=tmpS[:, b2 * 4:b2 * 4 + 4, :], in0=su[b2][:, :].rearrange("p (h v) -> p h v", v=128),
                        in1=bcast(e2v[:, b2 * 4:b2 * 4 + 4], 1, 128), op=ALU.mult), reads=[su[b2], E2], writes=[tmpS])
                cx.op("dve", lambda en: en.tensor_tensor(out=S[:, :, :], in0=S[:, :, :], in1=tmpS[:, :, :], op=ALU.add),
                      reads=[S, tmpS], writes=[S])
                rows = slice(t0 + c * 128, t0 + (c + 1) * 128)
                ob = osb[a2]
                if not final:
                    cx.op("act", lambda en: en.activation(out=ob[:, 0:512], in_=po[0][:, :], func=AF.Copy),
                          reads=[po[0]], writes=[ob])
                    cx.op("act", lambda en: en.activation(out=ob[:, 512:1024], in_=po[1][:, :], func=AF.Copy),
                          reads=[po[1]], writes=[ob])
                    cx.dma("pool", of[rows, :], ob[:, :], reads=[ob])
                    return
                cx.dma("sp", oft[a2][:], of[rows, :], writes=[oft[a2]])
                cx.dma("sp", rt_[a2][:], x_src[rows, :], writes=[rt_[a2]])
                for b2 in range(2):
                    cx.op("dve", lambda en, b2=b2: en.tensor_tensor(
                        out=ob[:, b2 * 512:(b2 + 1) * 512], in0=po[b2][:, :], in1=oft[a2][:, b2 * 512:(b2 + 1) * 512],
                        op=ALU.add), reads=[po[b2], oft[a2]], writes=[ob])

            def seg3(c, a2):
                rows = slice(t0 + c * 128, t0 + (c + 1) * 128)
                ob = osb[a2]
                obv = ob[:, :].rearrange("p (h v) -> p h v", v=128)
                sq = oft[a2]
                cx.op("pool", lambda en: en.tensor_tensor(out=sq[:, :], in0=ob[:, :], in1=ob[:, :], op=ALU.mult),
                      reads=[ob], writes=[sq])
                cx.op("dve", lambda en: en.tensor_reduce(out=s8[:, :], in_=sq[:, :].rearrange("p (h v) -> p h v", v=128),
                                                         axis=AX.X, op=ALU.add), reads=[sq], writes=[s8])
                cx.op("act", lambda en: en.activation(out=s8[:, :], in_=s8[:, :], func=AF.Sqrt, scale=1.0 / 128,
                                                      bias=EPS_AP[0][:]), reads=[s8], writes=[s8])
                cx.op("dve", lambda en: en.reciprocal(out=r8[:, :], in_=s8[:, :]), reads=[s8], writes=[r8])
                obv = ob[:, :].rearrange("p (h v) -> p h v", v=128)
                cx.op("dve", lambda en: en.tensor_tensor(out=obv, in0=obv, in1=bcast(r8[:, :], 1, 128), op=ALU.mult),
                      reads=[ob, r8], writes=[ob])
                cx.op("pool", lambda en: en.tensor_tensor(out=obv, in0=obv, in1=bcast(P["gnb"][:, :], 0, 8),
                                                          op=ALU.mult), reads=[ob, P["gnb"]], writes=[ob])
                cx.op("pool", lambda en: en.tensor_tensor(out=og[a2][:, :], in0=ob[:, :], in1=sgt[qq][:, c, :],
                                                          op=ALU.mult), reads=[ob, sgt[qq]], writes=[og[a2]])
                for kc in range(8):
                    cx.op("pe", lambda en, kc=kc: en.transpose(out=tp[:, kc * 128:(kc + 1) * 128],
                                                               in_=og[a2][:, kc * 128:(kc + 1) * 128], identity=ident[:]),
                          reads=[og[a2], ident], writes=[tp])
                cx.op("act", lambda en: en.activation(out=ogT[a2][:, :, :], in_=tp[:, :].rearrange("p (c t) -> p c t", t=128),
                                                      func=AF.Copy), reads=[tp], writes=[ogT[a2]])
                for c2 in range(2):
                    pb = pj[c2]
                    for kc in range(8):
                        cx.op("pe", lambda en, kc=kc, c2=c2, pb=pb: en.matmul(
                            pb[:, :], lhsT=ogT[a2][:, kc, :], rhs=wo[:, kc, c2 * 512:(c2 + 1) * 512],
                            start=(kc == 0), stop=(kc == 7)), reads=[ogT[a2], wo], writes=[pb])
                    cx.op("dve", lambda en, c2=c2, pb=pb: en.tensor_tensor(
                        out=xo[a2][:, c2 * 512:(c2 + 1) * 512], in0=pb[:, :], in1=rt_[a2][:, c2 * 512:(c2 + 1) * 512],
                        op=ALU.add), reads=[pb, rt_[a2]], writes=[xo[a2]])
                cx.dma("pool", dst[rows, :], xo[a2][:], reads=[xo[a2]])
            if final:
                return segs1 + [segs2[0], segs3[0], segs2[1], segs3[1]]
            return segs1 + segs2

        fm0, tm0 = X(0)
        for p_ in fm0 + tm0:
            p_()
        Y(0)
        for q in range(nst):
            zs = Z(q)
            if q + 1 < nst:
                fm, tm = X(q + 1)
                step = 10 ** 9 if (INTERLEAVE_OFF or final) else max(1, len(fm) // (len(zs) + 1))
                zi = 0
                for k_, p_ in enumerate(fm):
                    p_()
                    if (k_ + 1) % step == 0 and zi < len(zs):
                        zs[zi]()
                        zi += 1
                while zi < len(zs):
                    zs[zi]()
                    zi += 1
                for p_ in tm:
                    p_()
                Y(q + 1)
            else:
                for z_ in zs:
                    z_()
        cx.barrier()


def layer_c(cx, T, x_src, x_dst, din, ngT, layer, scr, consts):
    with ExitStack() as sc:
        P = {}
        P["lb"] = cx.sb(sc, "lb", [128, 2, 8], F32)
        P["oml"] = cx.sb(sc, "oml", [128, 2, 8], F32)
        P["gnb"] = cx.sb(sc, "gnb", [128, 128], F32)
        lbe = cx.sb(sc, "lbe", [128, 4, 16], F32)
        tot = cx.sb(sc, "tot", [128, 16], F32)
        cx.dma("sp", lbe[:], din["c_lbT"][:, :, :], writes=[lbe])
        cx.dma("sp", P["gnb"][:], din["c_gnb"][:, :], writes=[P["gnb"]])
        cx.op("act", lambda en: en.activation(out=lbe[:], in_=lbe[:], func=AF.Exp), reads=[lbe], writes=[lbe])
        lbv = P["lb"][:, :, :].rearrange("p a b -> p (a b)")
        omv = P["oml"][:, :, :].rearrange("p a b -> p (a b)")
        cx.op("dve", lambda en: en.tensor_tensor(out=tot[:, :], in0=lbe[:, 0, :], in1=lbe[:, 3, :], op=ALU.add),
              reads=[lbe], writes=[tot])
        cx.op("dve", lambda en: en.tensor_tensor(out=lbv, in0=lbe[:, 1, :], in1=lbe[:, 2, :], op=ALU.add),
              reads=[lbe], writes=[P["lb"]])
        cx.op("dve", lambda en: en.tensor_tensor(out=tot[:, :], in0=tot[:, :], in1=lbv, op=ALU.add),
              reads=[tot, P["lb"]], writes=[tot])
        cx.op("dve", lambda en: en.reciprocal(out=tot[:, :], in_=tot[:, :]), reads=[tot], writes=[tot])
        cx.op("dve", lambda en: en.tensor_tensor(out=lbv, in0=lbv, in1=tot[:, :], op=ALU.mult),
              reads=[tot, P["lb"]], writes=[P["lb"]])
        cx.op("dve", lambda en: en.tensor_scalar(out=omv, in0=lbv, scalar1=-1.0, scalar2=1.0, op0=ALU.mult, op1=ALU.add),
              reads=[P["lb"]], writes=[P["oml"]])
        cx.barrier()
        S = cx.sb(sc, "S", [128, 8, 128], F32)
        S2 = cx.sb(sc, "S2", [128, 8, 128], F32)
        cx.op("dve", lambda en: en.memset(S[:], 0.0), writes=[S])
        hgrn_pass(cx, T, 0, x_src, din, (ngT, layer), P, scr["of"], None, consts, False, S)
        with ExitStack() as s3:
            exchange_sb(cx, s3, S[:, :, :].rearrange("p h v -> p (h v)"), S2[:, :, :].rearrange("p h v -> p (h v)"),
                        1024, consts["sel"])
        hgrn_pass(cx, T, 1, x_src, din, (ngT, layer), P, scr["of"], x_dst, consts, True, S2)


def make_in_maps(xp, xsamp, W, T):
    maps = []
    meta = []
    for b in range(xp.shape[0]):
        seq = np.asarray(xp[b], np.float32)
        for half in range(2):
            if half == 0:
                x_ext = seq[0:T + HALO]
                pos = np.arange(T + HALO)
                pos_h3 = T + 2047 - np.arange(2048)
                maps.append(core_inputs(x_ext, pos, 1.0, W, False, (0.0, 1.0), pos_h3))
            else:
                x_ext = seq[::-1][0:T + HALO]
                pos = 2 * T - 1 - np.arange(T + HALO)
                pos_h3 = T - 2048 + np.arange(2048)
                maps.append(core_inputs(x_ext, pos, 1.0, W, True, (1.0, 0.0), pos_h3))
            meta.append(("p", b, half))
    for b in range(xsamp.shape[0]):
        seq = np.asarray(xsamp[b], np.float32)
        x_ext = np.concatenate([seq, np.zeros((HALO, D), np.float32)], 0)
        maps.append(core_inputs(x_ext, np.arange(T + HALO), 0.0, W, False, (0.0, 0.0), None))
        meta.append(("s", b, 0))
    return maps, meta


_NC_CACHE = {}


def kernel(**inputs):
    T = 8192
    W = {k: np.asarray(v, np.float32) for k, v in inputs.items() if k not in ("x_prompt", "x_sample")}
    xp = np.asarray(inputs["x_prompt"], np.float32)
    xsamp = np.asarray(inputs["x_sample"], np.float32)
    maps, meta = make_in_maps(xp, xsamp, W, T)
    if T not in _NC_CACHE:
        _NC_CACHE[T] = build(T, 4)
    res = run_bass_kernel_spmd(_NC_CACHE[T], maps, core_ids=list(range(8)))
    yp = np.zeros(xp.shape, np.float32)
    ys = np.zeros(xsamp.shape, np.float32)
    for c, (kind, b, half) in enumerate(meta):
        y = np.asarray(res.results[c]["y"], np.float32)
        if kind == "s":
            ys[b] = y
        elif half == 0:
            yp[b, 0:T] = y
        else:
            yp[b, T:2 * T] = y[::-1]
    return (yp, ys)
```

```python
import numpy as np
import ml_dtypes
from contextlib import ExitStack
import concourse.bass as bass
import concourse.mybir as mybir
from concourse.bass_utils import run_bass_kernel_spmd

F32 = mybir.dt.float32
BF16 = mybir.dt.bfloat16
ALU = mybir.AluOpType
AF = mybir.ActivationFunctionType
AX = mybir.AxisListType

D = 1024
EPS = 1e-6
HALO = 2048
SAME_SYNC = True
INTERLEAVE_OFF = True
ATT_SCALE = 128.0 ** -0.5
GROUPS = ((0, 1), (1, 4), (2, 16))


class Buf:
    def __init__(self, t, name):
        self.t = t
        self.name = name
        self.w = None
        self.r = {}
        self.dsem = None
        self.dcnt = 0

    def __getitem__(self, idx):
        return self.t[idx]


def bcast(ap, pos, n):
    l = [list(x) for x in ap.ap]
    l.insert(1 + pos, [0, n])
    return bass.AP(ap.tensor, ap.offset, l)


class Ctx:
    def __init__(self, nc, es):
        self.nc = nc
        self.es = es
        self.eng = {"pe": nc.tensor, "act": nc.scalar, "dve": nc.vector, "pool": nc.gpsimd, "sp": nc.sync}
        self.esem = {e: es.enter_context(nc.semaphore("s_" + e)) for e in ("pe", "act", "dve", "pool")}
        self.cnt = {e: 0 for e in self.esem}
        self.waited = {e: {} for e in self.eng}
        self.slots = []
        self.free_slots = {}
        self.active = []
        self.nsem = 0
        self.nb = 0
        self.capture = None

    def sb(self, scope, name, shape, dt):
        self.nb += 1
        return Buf(scope.enter_context(self.nc.sbuf_tensor("%s_%d" % (name, self.nb), shape, dt)), name)

    def ps(self, scope, name, shape, dt):
        self.nb += 1
        return Buf(scope.enter_context(self.nc.psum_tensor("%s_%d" % (name, self.nb), shape, dt)), name)

    def _deps(self, e, reads, writes):
        deps = []
        for b in reads:
            if b.w is not None:
                deps.append(b.w)
        for b in writes:
            if b.w is not None:
                deps.append(b.w)
            deps.extend(b.r.values())
        for (key, sem, val) in deps:
            if key == e and (e == "pe" or not SAME_SYNC):
                continue
            if self.waited[e].get(key, 0) >= val:
                continue
            self.eng[e].wait_ge(sem, val)
            self.waited[e][key] = val

    def op(self, e, fn, reads=(), writes=()):
        if self.capture is not None:
            self.capture.append((e, fn, tuple(reads), tuple(writes)))
            return None
        self._deps(e, reads, writes)
        ins = fn(self.eng[e])
        self.cnt[e] += 1
        ins.then_inc(self.esem[e], 1)
        tok = (e, self.esem[e], self.cnt[e])
        for b in reads:
            b.r[e] = tok
        for b in writes:
            b.w = tok
            b.r = {}
        return ins

    def dma(self, e, out, in_, reads=(), writes=()):
        self._deps(e, reads, writes)
        prim = writes[0] if writes else reads[0]
        if prim.dsem is None:
            prim.dsem = {}
        if e not in prim.dsem:
            fl = self.free_slots.setdefault(e, [])
            if fl:
                prim.dsem[e] = fl.pop()
            else:
                self.nsem += 1
                prim.dsem[e] = [self.es.enter_context(self.nc.semaphore("d%d" % self.nsem)), 0, self.nsem]
                self.slots.append(prim.dsem[e])
            self.active.append((prim, e))
        slot = prim.dsem[e]
        slot[1] += 16
        self.eng[e].dma_start(out=out, in_=in_).then_inc(slot[0], 16)
        tok = (("d", slot[2]), slot[0], slot[1])
        for b in reads:
            b.r[tok[0]] = tok
        for b in writes:
            b.w = tok
            b.r = {}

    def barrier(self):
        toks = [(e, self.esem[e], self.cnt[e]) for e in self.esem if self.cnt[e] > 0]
        toks += [(("d", sl[2]), sl[0], sl[1]) for sl in self.slots if sl[1] > 0]
        for e in self.eng:
            for (key, sem, val) in toks:
                if key == e and e == "pe":
                    continue
                if self.waited[e].get(key, 0) >= val:
                    continue
                self.eng[e].wait_ge(sem, val)
                self.waited[e][key] = val
        for (b, e) in self.active:
            self.free_slots.setdefault(e, []).append(b.dsem.pop(e))
        self.active = []


PAIRS = [[0, 1], [2, 3], [4, 5], [6, 7]]
XN = [0]


def allgather(cx, pk, ga):
    nc = cx.nc
    XN[0] += 1
    sem = cx.es.enter_context(nc.semaphore("cc%d" % XN[0]))
    nc.gpsimd.collective_compute("AllGather", ALU.bypass, replica_groups=PAIRS, ins=[pk], outs=[ga]).then_inc(sem)
    for e in cx.eng:
        cx.eng[e].wait_ge(sem, 1)


def exchange_sb(cx, sc, src, dst, F, sel):
    nc = cx.nc
    XN[0] += 1
    pk = nc.dram_tensor("pk%d" % XN[0], [128, F], F32, kind="Internal").ap()
    ga = nc.dram_tensor("ga%d" % XN[0], [256, F], F32, kind="Internal", addr_space="Local").ap()
    gt = cx.sb(sc, "gt", [128, 2, F], F32)
    cx.dma("pool", pk[:, :], src, reads=[gt])
    cx.barrier()
    allgather(cx, pk[:, :], ga[:, :])
    cx.dma("pool", gt[:], ga.rearrange("(r p) f -> p r f", p=128), writes=[gt])
    cx.op("dve", lambda en: en.tensor_scalar(out=dst, in0=gt[:, 0, :], scalar1=sel[:, 0:1], scalar2=None, op0=ALU.mult),
          reads=[gt, sel], writes=[gt])
    cx.op("dve", lambda en: en.scalar_tensor_tensor(out=dst, in0=gt[:, 1, :], scalar=sel[:, 1:2], in1=dst,
                                                    op0=ALU.mult, op1=ALU.add), reads=[gt, sel], writes=[gt])
    cx.barrier()


def load_weight(cx, stg, dst, col_off, w_ap, ncols, g=None, kchunks=8):
    for kc in range(kchunks):
        s = stg[kc % len(stg)]
        cx.dma("sp", s[:, 0:ncols], w_ap[kc * 128:(kc + 1) * 128, :], writes=[s])
        e = ("dve", "pool", "act")[kc % 3]
        if e == "act":
            if g is not None:
                cx.op("act", lambda en, kc=kc, s=s: en.activation(
                    out=dst[:, kc, col_off:col_off + ncols], in_=s[:, 0:ncols], func=AF.Copy,
                    scale=g[0][:, g[1], kc:kc + 1]), reads=[s, g[0]], writes=[dst])
            else:
                cx.op("act", lambda en, kc=kc, s=s: en.activation(
                    out=dst[:, kc, col_off:col_off + ncols], in_=s[:, 0:ncols], func=AF.Copy),
                    reads=[s], writes=[dst])
            continue
        if g is not None:
            cx.op(e, lambda en, kc=kc, s=s: en.tensor_scalar(
                out=dst[:, kc, col_off:col_off + ncols], in0=s[:, 0:ncols], scalar1=g[0][:, g[1], kc:kc + 1],
                scalar2=1.0, op0=ALU.mult, op1=ALU.mult), reads=[s, g[0]], writes=[dst])
        else:
            cx.op(e, lambda en, kc=kc, s=s: en.tensor_copy(
                out=dst[:, kc, col_off:col_off + ncols], in_=s[:, 0:ncols]), reads=[s], writes=[dst])


def norm_tile(cx, xt, junk, ssq, std, rstd, xn):
    cx.op("act", lambda en: en.activation(out=junk[:], in_=xt[:], func=AF.Square), reads=[xt], writes=[junk])
    cx.op("dve", lambda en: en.tensor_reduce(out=ssq[:], in_=junk[:], axis=AX.X, op=ALU.add), reads=[junk], writes=[ssq])
    cx.op("act", lambda en: en.activation(out=std[:], in_=ssq[:], func=AF.Sqrt, scale=1.0 / D, bias=EPS_AP[0][:]),
          reads=[ssq], writes=[std])
    cx.op("dve", lambda en: en.reciprocal(out=rstd[:], in_=std[:]), reads=[std], writes=[rstd])
    cx.op("pool", lambda en: en.tensor_scalar(out=xn[:], in0=xt[:], scalar1=rstd[:, 0:1], scalar2=1.0,
                                               op0=ALU.mult, op1=ALU.mult), reads=[xt, rstd], writes=[xn])


EPS_AP = [None]


def transpose8(cx, src, tp, dst, ident, evac_eng="dve", ncol=8, rev=None):
    for kc in range(ncol):
        if rev is None:
            cx.op("pe", lambda en, kc=kc: en.transpose(out=tp[:, kc * 128:(kc + 1) * 128],
                                                       in_=src[:, kc * 128:(kc + 1) * 128], identity=ident[:]),
                  reads=[src, ident], writes=[tp])
    if evac_eng == "act_copy":
        cx.op("act", lambda en: en.activation(out=dst[:, 0:ncol, :],
                                              in_=tp[:, 0:ncol * 128].rearrange("p (c t) -> p c t", t=128),
                                              func=AF.Copy), reads=[tp], writes=[dst])
    else:
        cx.op(evac_eng, lambda en: en.tensor_copy(out=dst[:, 0:ncol, :],
                                                  in_=tp[:, 0:ncol * 128].rearrange("p (c t) -> p c t", t=128)),
              reads=[tp], writes=[dst])


def attn_group_pass(cx, T, g, d, x_src, cs_src, w_in, gvec, part, consts, halo_x=None, halo_cs=None):
    n = T // (128 * d)
    ident, maskb, ones8, flag8 = consts["ident"], consts["maskb"], consts["ones8"], consts["flag8"]
    with ExitStack() as sc:
        wt = cx.sb(sc, "wt", [128, 8, 3072], BF16)
        stg = [cx.sb(sc, "stg", [128, 1024], F32) for _ in range(4)]
        for j in range(3):
            col = (j * 3 + g) * 1024
            load_weight(cx, stg, wt, j * 1024, w_in[:, col:col + 1024], 1024, g=gvec)
        xt = [cx.sb(sc, "xt", [128, D], F32) for _ in range(2)]
        cst = [cx.sb(sc, "cst", [128, 32], F32) for _ in range(3)]
        junk = cx.sb(sc, "junk", [128, D], F32)
        ssq = [cx.sb(sc, "ssq", [128, 1], F32) for _ in range(2)]
        std = [cx.sb(sc, "std", [128, 1], F32) for _ in range(2)]
        rstd = [cx.sb(sc, "rstd", [128, 1], F32) for _ in range(2)]
        xn = [cx.sb(sc, "xn", [128, D], BF16) for _ in range(2)]
        xnT = [cx.sb(sc, "xnT", [128, 8, 128], BF16) for _ in range(2)]
        qk = [cx.sb(sc, "qk", [128, 2048], F32) for _ in range(2)]
        qkr = [cx.sb(sc, "qkr", [128, 2048], BF16) for _ in range(2)]
        rt = [cx.sb(sc, "rt", [128, 4, 16, 16], F32) for _ in range(2)]
        QT = [cx.sb(sc, "QT", [128, 8, 128], BF16) for _ in range(3)]
        KT = [cx.sb(sc, "KT", [128, 8, 128], BF16) for _ in range(4)]
        Vp = [cx.sb(sc, "Vp", [128, 8, 132], BF16) for _ in range(4)]
        PT = [cx.sb(sc, "PT", [128, 384], BF16) for _ in range(2)]
        osb = [cx.sb(sc, "osb", [128, 8, 129], F32) for _ in range(2)]
        pj = [cx.ps(sc, "pj", [128, 512], F32) for _ in range(2)]
        tp = cx.ps(sc, "tp", [128, 1024], BF16)
        scp = [cx.ps(sc, "scp", [128, 512], F32) for _ in range(2)]
        po = [cx.ps(sc, "po", [128, 512], F32) for _ in range(3)]

        items = [(r, i) for r in range(d) for i in range(n + 1)]
        NI = len(items)

        def views(r):
            return (x_src.rearrange("(l d) f -> d l f", d=d)[r], cs_src.rearrange("(l d) f -> d l f", d=d)[r],
                    part.rearrange("(l d) f -> d l f", d=d)[r])

        def a_norm(t):
            r, i = items[t]
            s2 = t % 2
            xv, cv, _ = views(r)
            if i == n and halo_x is not None:
                u0 = 2047 - r - 127 * d
                cx.dma("sp", xt[s2][:], bass.AP(halo_x.tensor, halo_x.offset + u0 * D, [[d * D, 128], [1, D]]),
                       writes=[xt[s2]])
                cx.dma("sp", cst[t % 3][:], bass.AP(halo_cs.tensor, halo_cs.offset + u0 * 32, [[d * 32, 128], [1, 32]]),
                       writes=[cst[t % 3]])
            else:
                cx.dma("sp", xt[s2][:], xv[i * 128:(i + 1) * 128, :], writes=[xt[s2]])
                cx.dma("sp", cst[t % 3][:], cv[i * 128:(i + 1) * 128, :], writes=[cst[t % 3]])
            norm_tile(cx, xt[s2], junk, ssq[s2], std[s2], rstd[s2], xn[s2])

        def a_tr(t):
            transpose8(cx, xn[t % 2], tp, xnT[t % 2], ident)

        def proj(t):
            r, i = items[t]
            s2 = t % 2
            k4 = t % 4
            for c in range(6):
                if i == n and c < 2:
                    continue
                pb = pj[c % 2]
                for kc in range(8):
                    cx.op("pe", lambda en, kc=kc, c=c, pb=pb: en.matmul(
                        pb[:, :], lhsT=xnT[s2][:, kc, :], rhs=wt[:, kc, c * 512:(c + 1) * 512],
                        start=(kc == 0), stop=(kc == 7)), reads=[xnT[s2], wt], writes=[pb])
                if c < 4:
                    if c % 2 == 0:
                        cx.op("act", lambda en, c=c, pb=pb: en.activation(
                            out=qk[s2][:, c * 512:(c + 1) * 512], in_=pb[:, :], func=AF.Copy),
                            reads=[pb], writes=[qk[s2]])
                    else:
                        cx.op("dve", lambda en, c=c, pb=pb: en.tensor_copy(
                            out=qk[s2][:, c * 512:(c + 1) * 512], in_=pb[:, :]),
                            reads=[pb], writes=[qk[s2]])
                else:
                    h0 = (c - 4) * 4
                    cx.op("act", lambda en, pb=pb, h0=h0: en.activation(
                        out=Vp[k4][:, h0:h0 + 4, 0:128], in_=pb[:, :].rearrange("p (h e) -> p h e", e=128),
                        func=AF.Copy), reads=[pb], writes=[Vp[k4]])
            vsrc = flag8 if i == n else ones8
            cx.op("pool", lambda en, vsrc=vsrc: en.tensor_copy(out=Vp[k4][:, :, 128:129], in_=vsrc[:, :, 0:1]),
                  reads=[vsrc], writes=[Vp[k4]])

        def rope(t):
            r, i = items[t]
            s2 = t % 2
            c3 = cst[t % 3]
            lo = 8 if i == n else 0
            nh = 16 - lo
            v3 = qk[s2][:, :].rearrange("p (h e) -> p h e", e=128)
            o3 = qkr[s2][:, :].rearrange("p (h e) -> p h e", e=128)
            x1 = v3[:, lo:16, 0:16]
            x2 = v3[:, lo:16, 16:32]
            cosb = bcast(c3[:, 0:16], 0, nh)
            sinb = bcast(c3[:, 16:32], 0, nh)
            rtb = rt[s2]
            for (k_, a_, b_) in ((0, x1, cosb), (1, x2, sinb), (2, x2, cosb), (3, x1, sinb)):
                cx.op("dve", lambda en, k_=k_, a_=a_, b_=b_: en.tensor_tensor(
                    out=rtb[:, k_, lo:16, :], in0=a_, in1=b_, op=ALU.mult),
                    reads=[qk[s2], c3], writes=[rtb])
            cx.op("dve", lambda en: en.tensor_tensor(out=o3[:, lo:16, 0:16], in0=rtb[:, 0, lo:16, :],
                                                     in1=rtb[:, 1, lo:16, :], op=ALU.subtract),
                  reads=[rtb], writes=[qkr[s2]])
            cx.op("dve", lambda en: en.tensor_tensor(out=o3[:, lo:16, 16:32], in0=rtb[:, 2, lo:16, :],
                                                     in1=rtb[:, 3, lo:16, :], op=ALU.add),
                  reads=[rtb], writes=[qkr[s2]])
            cx.op("pool", lambda en: en.tensor_copy(out=o3[:, lo:16, 32:128], in_=v3[:, lo:16, 32:128]),
                  reads=[qk[s2]], writes=[qkr[s2]])

        def qk_tr(t):
            r, i = items[t]
            s2 = t % 2
            if i < n:
                for h in range(8):
                    cx.op("pe", lambda en, h=h: en.transpose(out=tp[:, h * 128:(h + 1) * 128],
                                                             in_=qkr[s2][:, h * 128:(h + 1) * 128],
                                                             identity=ident[:]),
                          reads=[qkr[s2], ident], writes=[tp])
                cx.op("dve", lambda en: en.tensor_copy(out=QT[t % 3][:, :, :],
                                                       in_=tp[:, :].rearrange("p (c t) -> p c t", t=128)),
                      reads=[tp], writes=[QT[t % 3]])
            for h in range(8):
                cx.op("pe", lambda en, h=h: en.transpose(out=tp[:, h * 128:(h + 1) * 128],
                                                         in_=qkr[s2][:, 1024 + h * 128:1024 + (h + 1) * 128],
                                                         identity=ident[:]),
                      reads=[qkr[s2], ident], writes=[tp])
            cx.op("act", lambda en: en.activation(out=KT[t % 4][:, :, :],
                                                  in_=tp[:, :].rearrange("p (c t) -> p c t", t=128), func=AF.Copy),
                  reads=[tp], writes=[KT[t % 4]])

        def att(t):
            r, j = items[t]
            if j >= n:
                return
            _, _, pv = views(r)
            blocks = []
            if j >= 1:
                blocks.append(((t - 1) % 4, 0))
            blocks.append((t % 4, 1))
            blocks.append(((t + 1) % 4, 3 if (j + 1 == n and halo_x is not None) else 2))
            nb = len(blocks)
            ob = osb[t % 2]
            qt = QT[t % 3]

            def scores(h):
                sp_ = scp[h % 2]
                for bi, (ks, mi) in enumerate(blocks):
                    cx.op("pe", lambda en, bi=bi, ks=ks, h=h, sp_=sp_: en.matmul(
                        sp_[:, bi * 128:(bi + 1) * 128], lhsT=KT[ks][:, h, :], rhs=qt[:, h, :],
                        start=True, stop=False), reads=[KT[ks], qt], writes=[sp_])
                    cx.op("pe", lambda en, bi=bi, mi=mi, sp_=sp_: en.matmul(
                        sp_[:, bi * 128:(bi + 1) * 128], lhsT=ident[:], rhs=maskb[:, mi, :],
                        start=False, stop=True), reads=[ident, maskb], writes=[sp_])
                cx.op("act", lambda en, sp_=sp_, h=h: en.activation(
                    out=PT[h % 2][:, 0:nb * 128], in_=sp_[:, 0:nb * 128], func=AF.Exp, scale=ATT_SCALE),
                    reads=[sp_], writes=[PT[h % 2]])

            def pvm(h):
                pt_ = PT[h % 2]
                pob = po[h // 3]
                off = (h % 3) * 129
                for bi, (ks, mi) in enumerate(blocks):
                    cx.op("pe", lambda en, bi=bi, ks=ks, h=h, pob=pob, off=off, pt_=pt_: en.matmul(
                        pob[:, off:off + 129], lhsT=pt_[:, bi * 128:(bi + 1) * 128], rhs=Vp[ks][:, h, 0:129],
                        start=(bi == 0), stop=(bi == nb - 1)), reads=[pt_, Vp[ks]], writes=[pob])

            scores(0)
            for h in range(8):
                if h + 1 < 8:
                    scores(h + 1)
                pvm(h)
            for bk in range(3):
                nhb = 3 if bk < 2 else 2
                if bk != 1:
                    cx.op("dve", lambda en, bk=bk, nhb=nhb: en.tensor_copy(
                        out=ob[:, bk * 3:bk * 3 + nhb, :],
                        in_=po[bk][:, 0:nhb * 129].rearrange("p (h e) -> p h e", e=129)),
                        reads=[po[bk]], writes=[ob])
                else:
                    cx.op("act", lambda en, bk=bk, nhb=nhb: en.activation(
                        out=ob[:, bk * 3:bk * 3 + nhb, :],
                        in_=po[bk][:, 0:nhb * 129].rearrange("p (h e) -> p h e", e=129), func=AF.Copy),
                        reads=[po[bk]], writes=[ob])
            cx.dma("pool", pv[j * 128:(j + 1) * 128, :], ob[:, :, :].rearrange("p h e -> p (h e)"), reads=[ob])

        a_norm(0)
        a_tr(0)
        for t in range(NI):
            if t + 1 < NI:
                a_norm(t + 1)
            proj(t)
            if t + 1 < NI:
                a_tr(t + 1)
            rope(t)
            if t - 2 >= 0:
                att(t - 2)
            qk_tr(t)
        att(NI - 2)
        att(NI - 1)
        cx.barrier()


def attn_final_pass(cx, T, x_src, parts, w_in, w_out, gvec, x_dst, consts, final_g=None):
    ident = consts["ident"]
    nt = T // 128
    with ExitStack() as sc:
        wg = cx.sb(sc, "wg", [128, 8, 1024], BF16)
        wo = cx.sb(sc, "wo", [128, 8, 1024], BF16)
        stg = [cx.sb(sc, "stg", [128, 1024], F32) for _ in range(4)]
        load_weight(cx, stg, wg, 0, w_in[:, 9216:10240], 1024, g=gvec)
        load_weight(cx, stg, wo, 0, w_out[:, :], 1024, g=None)
        xt = [cx.sb(sc, "xt", [128, D], F32) for _ in range(3)]
        pp = [[cx.sb(sc, "pp", [128, 8, 129], F32) for _ in range(3)] for _ in range(2)]
        junk = cx.sb(sc, "junk", [128, D], F32)
        junk2 = cx.sb(sc, "junk2", [128, D], F32)
        ssq = [cx.sb(sc, "ssq", [128, 1], F32) for _ in range(2)]
        std = [cx.sb(sc, "std", [128, 1], F32) for _ in range(2)]
        rstd = [cx.sb(sc, "rstd", [128, 1], F32) for _ in range(2)]
        xn = [cx.sb(sc, "xn", [128, D], BF16) for _ in range(2)]
        xnT = [cx.sb(sc, "xnT", [128, 8, 128], BF16) for _ in range(2)]
        sg = [cx.sb(sc, "sg", [128, D], F32) for _ in range(2)]
        den = [cx.sb(sc, "den", [128, 8, 1], F32) for _ in range(2)]
        rden = [cx.sb(sc, "rden", [128, 8, 1], F32) for _ in range(2)]
        o = [cx.sb(sc, "o", [128, 8, 128], F32) for _ in range(2)]
        og = [cx.sb(sc, "og", [128, D], BF16) for _ in range(2)]
        ogT = [cx.sb(sc, "ogT", [128, 8, 128], BF16) for _ in range(2)]
        xo = [cx.sb(sc, "xo", [128, D], F32) for _ in range(2)]
        yo = [cx.sb(sc, "yo", [128, D], F32) for _ in range(2)]
        ssq2 = [cx.sb(sc, "ssq2", [128, 1], F32) for _ in range(2)]
        std2 = [cx.sb(sc, "std2", [128, 1], F32) for _ in range(2)]
        rstd2 = [cx.sb(sc, "rstd2", [128, 1], F32) for _ in range(2)]
        pj = [cx.ps(sc, "pj", [128, 512], F32) for _ in range(4)]
        tp = [cx.ps(sc, "tp", [128, 1024], BF16) for _ in range(2)]

        def a_norm(i):
            s2 = i % 2
            rows = slice(i * 128, (i + 1) * 128)
            cx.dma("sp", xt[i % 3][:], x_src[rows, :], writes=[xt[i % 3]])
            for g in range(3):
                cx.dma("sp", pp[s2][g][:, :, :].rearrange("p h e -> p (h e)"), parts[g][rows, :], writes=[pp[s2][g]])
            norm_tile(cx, xt[i % 3], junk, ssq[s2], std[s2], rstd[s2], xn[s2])

        def a_tr(i):
            transpose8(cx, xn[i % 2], tp[0], xnT[i % 2], ident)

        def gate(i):
            s2 = i % 2
            for c in range(2):
                pb = pj[c]
                for kc in range(8):
                    cx.op("pe", lambda en, kc=kc, c=c, pb=pb: en.matmul(
                        pb[:, :], lhsT=xnT[s2][:, kc, :], rhs=wg[:, kc, c * 512:(c + 1) * 512],
                        start=(kc == 0), stop=(kc == 7)), reads=[xnT[s2], wg], writes=[pb])
                cx.op("act", lambda en, c=c, pb=pb: en.activation(out=sg[s2][:, c * 512:(c + 1) * 512], in_=pb[:, :],
                                                                 func=AF.Silu), reads=[pb], writes=[sg[s2]])

        def chain(i):
            s2 = i % 2
            p0, p1, p2 = pp[s2]
            cx.op("pool", lambda en: en.tensor_tensor(out=p0[:, :, :], in0=p0[:, :, :], in1=p1[:, :, :], op=ALU.add),
                  reads=[p0, p1], writes=[p0])
            cx.op("dve", lambda en: en.tensor_tensor(out=p0[:, :, :], in0=p0[:, :, :], in1=p2[:, :, :], op=ALU.add),
                  reads=[p0, p2], writes=[p0])
            cx.op("dve", lambda en: en.tensor_scalar(out=den[s2][:, :, :], in0=p0[:, :, 128:129], scalar1=1e-30,
                                                     scalar2=None, op0=ALU.max), reads=[p0], writes=[den[s2]])
            cx.op("dve", lambda en: en.reciprocal(out=rden[s2][:, :, :], in_=den[s2][:, :, :]),
                  reads=[den[s2]], writes=[rden[s2]])
            cx.op("dve", lambda en: en.tensor_tensor(out=o[s2][:, :, :], in0=p0[:, :, 0:128],
                                                     in1=bcast(rden[s2][:, :, 0], 1, 128), op=ALU.mult),
                  reads=[p0, rden[s2]], writes=[o[s2]])
            cx.op("pool", lambda en: en.tensor_tensor(out=og[s2][:, :], in0=o[s2][:, :, :].rearrange("p h e -> p (h e)"),
                                                      in1=sg[s2][:, :], op=ALU.mult),
                  reads=[o[s2], sg[s2]], writes=[og[s2]])

        def out(i):
            s2 = i % 2
            rows = slice(i * 128, (i + 1) * 128)
            x3 = xt[i % 3]
            transpose8(cx, og[s2], tp[1], ogT[s2], ident, evac_eng="act_copy")
            for c in range(2):
                pb = pj[2 + c]
                for kc in range(8):
                    cx.op("pe", lambda en, kc=kc, c=c, pb=pb: en.matmul(
                        pb[:, :], lhsT=ogT[s2][:, kc, :], rhs=wo[:, kc, c * 512:(c + 1) * 512],
                        start=(kc == 0), stop=(kc == 7)), reads=[ogT[s2], wo], writes=[pb])
                cx.op("dve", lambda en, c=c, pb=pb: en.tensor_tensor(
                    out=xo[s2][:, c * 512:(c + 1) * 512], in0=pb[:, :], in1=x3[:, c * 512:(c + 1) * 512],
                    op=ALU.add), reads=[pb, x3], writes=[xo[s2]])
            if final_g is None:
                cx.dma("pool", x_dst[rows, :], xo[s2][:], reads=[xo[s2]])
            else:
                cx.op("act", lambda en: en.activation(out=junk2[:], in_=xo[s2][:], func=AF.Square),
                      reads=[xo[s2]], writes=[junk2])
                cx.op("dve", lambda en: en.tensor_reduce(out=ssq2[s2][:], in_=junk2[:], axis=AX.X, op=ALU.add),
                      reads=[junk2], writes=[ssq2[s2]])
                cx.op("act", lambda en: en.activation(out=std2[s2][:], in_=ssq2[s2][:], func=AF.Sqrt, scale=1.0 / D,
                                                      bias=EPS_AP[0][:]), reads=[ssq2[s2]], writes=[std2[s2]])
                cx.op("dve", lambda en: en.reciprocal(out=rstd2[s2][:], in_=std2[s2][:]), reads=[std2[s2]], writes=[rstd2[s2]])
                cx.op("dve", lambda en: en.scalar_tensor_tensor(out=yo[s2][:], in0=xo[s2][:], scalar=rstd2[s2][:, 0:1],
                                                                in1=final_g[:, :], op0=ALU.mult, op1=ALU.mult),
                      reads=[xo[s2], rstd2[s2], final_g], writes=[yo[s2]])
                cx.dma("pool", x_dst[rows, :], yo[s2][:], reads=[yo[s2]])

        a_norm(0)
        a_tr(0)
        for i in range(nt):
            if i + 1 < nt:
                a_norm(i + 1)
            gate(i)
            if i + 1 < nt:
                a_tr(i + 1)
            if i >= 1:
                out(i - 1)
            chain(i)
        out(nt - 1)
        cx.barrier()


def build(T, NL=4):
    nc = bass.Bass("TRN2", target_bir_lowering=False)
    TH = T + HALO
    din = {}

    def inp(name, shape, dt=F32):
        din[name] = nc.dram_tensor(name, shape, dt, kind="ExternalInput").ap()
        return din[name]

    x0 = inp("x0", [TH, D])
    cs = inp("cs", [TH, 32])
    a_w_in = inp("a_w_in", [2, D, 10240])
    a_w_out = inp("a_w_out", [2, D, D])
    norm_gT = inp("norm_gT", [128, 4, 8])
    final_gb = inp("final_gb", [128, D])
    ident_d = inp("ident", [128, 128], BF16)
    maskb_d = inp("maskb", [128, 4, 128], BF16)
    flag8_d = inp("flag8", [128, 8, 1])
    inp("b_w_in", [D, 2048])
    inp("b_w_out", [D, D])
    inp("b_w_r", [2, 8, 128, 128])
    inp("b_w_i", [2, 8, 128, 128])
    inp("b_b_rT", [128, 2, 8])
    inp("b_b_iT", [128, 2, 8])
    inp("b_lamT", [128, 2, 8])
    inp("b_w5T", [128, 5, 8])
    inp("b_cbT", [128, 8])
    inp("c_w_in", [D, 5120])
    inp("c_w_out", [D, D])
    inp("c_lbT", [128, 4, 16])
    inp("c_gnb", [128, 128])
    cmask_d = inp("cmask", [128, 2, 128], BF16)
    sel_d = inp("sel", [128, 2])
    cs_h3 = inp("cs_h3", [2048, 32])
    y = nc.dram_tensor("y", [T, D], F32, kind="ExternalOutput").ap()
    scr = {}
    scr["of"] = nc.dram_tensor("of", [T, D], F32, kind="Internal").ap()
    scr["xbT"] = nc.dram_tensor("xbT", [8, 128, T + 4], F32, kind="Internal").ap()
    scr["sgT"] = nc.dram_tensor("sgT", [8, 128, T], F32, kind="Internal").ap()
    scr["xp"] = nc.dram_tensor("xp", [T, D], F32, kind="Internal").ap()
    xs = [nc.dram_tensor("xs%d" % i, [TH, D], F32, kind="Internal").ap() for i in range(2)]
    parts = [nc.dram_tensor("part%d" % i, [T, 8 * 129], F32, kind="Internal").ap() for i in range(3)]

    with ExitStack() as es:
        cx = Ctx(nc, es)
        consts = {}
        consts["ident"] = cx.sb(es, "ident", [128, 128], BF16)
        consts["maskb"] = cx.sb(es, "maskb", [128, 4, 128], BF16)
        consts["ones8"] = cx.sb(es, "ones8", [128, 8, 1], F32)
        consts["flag8"] = cx.sb(es, "flag8", [128, 8, 1], F32)
        ngT = cx.sb(es, "ngT", [128, 4, 8], F32)
        epsb = cx.sb(es, "epsb", [128, 1], F32)
        EPS_AP[0] = epsb
        cx.dma("sp", consts["ident"][:], ident_d[:, :], writes=[consts["ident"]])
        cx.dma("sp", consts["maskb"][:], maskb_d[:, :, :], writes=[consts["maskb"]])
        cx.dma("sp", consts["flag8"][:], flag8_d[:, :, :], writes=[consts["flag8"]])
        cx.dma("sp", ngT[:], norm_gT[:, :, :], writes=[ngT])
        consts["sel"] = cx.sb(es, "sel", [128, 2], F32)
        cx.dma("sp", consts["sel"][:], sel_d[:, :], writes=[consts["sel"]])
        consts["cmask"] = cx.sb(es, "cmask", [128, 2, 128], BF16)
        cx.dma("sp", consts["cmask"][:], cmask_d[:, :, :], writes=[consts["cmask"]])
        cx.op("dve", lambda en: en.memset(consts["ones8"][:], 1.0), writes=[consts["ones8"]])
        cx.op("dve", lambda en: en.memset(epsb[:], EPS), writes=[epsb])

        last = (NL == 1)
        for (g, d) in GROUPS:
            attn_group_pass(cx, T, g, d, x0, cs, a_w_in[0], (ngT, 0), parts[g], consts)
        attn_final_pass(cx, T, x0, parts, a_w_in[0], a_w_out[0], (ngT, 0), y if last else xs[0], consts,
                        final_g=None)
        if NL >= 2:
            layer_b(cx, T, xs[0], y if NL == 2 else xs[1], din, ngT, 1, scr, consts)
        if NL >= 3:
            layer_c(cx, T, xs[1], y if NL == 3 else xs[0], din, ngT, 2, scr, consts)
        if NL >= 4:
            x3 = xs[0]
            pk3 = nc.dram_tensor("pkx3", [1024, D], F32, kind="Internal").ap()
            ga3 = nc.dram_tensor("gax3", [2048, D], F32, kind="Internal", addr_space="Local").ap()
            halo3 = nc.dram_tensor("halo3", [2048, D], F32, kind="Internal").ap()
            with ExitStack() as s3:
                g0 = [cx.sb(s3, "g0", [128, D], F32) for _ in range(2)]
                g1 = [cx.sb(s3, "g1", [128, D], F32) for _ in range(2)]
                zt = cx.sb(s3, "zt3", [128, D], F32)
                cx.op("dve", lambda en: en.memset(zt[:], 0.0), writes=[zt])
                for i in range(8):
                    a = g0[i % 2]
                    cx.dma("sp", a[:], x3[T - 1024 + i * 128:T - 1024 + (i + 1) * 128, :], writes=[a])
                    cx.dma("pool", pk3[i * 128:(i + 1) * 128, :], a[:], reads=[a])
                    cx.dma("pool", halo3[i * 128:(i + 1) * 128, :], zt[:], reads=[zt])
                cx.barrier()
                for i in range(8):
                    allgather(cx, pk3[i * 128:(i + 1) * 128, :], ga3[i * 256:(i + 1) * 256, :])
                sel = consts["sel"]
                for i in range(8):
                    a, b = g0[i % 2], g1[i % 2]
                    cx.dma("sp", a[:], ga3[i * 256:i * 256 + 128, :], writes=[a])
                    cx.dma("sp", b[:], ga3[i * 256 + 128:(i + 1) * 256, :], writes=[b])
                    cx.op("dve", lambda en, a=a: en.tensor_scalar(out=a[:], in0=a[:], scalar1=sel[:, 0:1], scalar2=None,
                                                                 op0=ALU.mult), reads=[a, sel], writes=[a])
                    cx.op("dve", lambda en, a=a, b=b: en.scalar_tensor_tensor(out=a[:], in0=b[:], scalar=sel[:, 1:2], in1=a[:],
                                                                            op0=ALU.mult, op1=ALU.add),
                          reads=[a, b, sel], writes=[a])
                    cx.dma("pool", halo3[1024 + i * 128:1024 + (i + 1) * 128, :], a[:], reads=[a])
                cx.barrier()
            for (g, d) in GROUPS:
                attn_group_pass(cx, T, g, d, x3, cs, a_w_in[1], (ngT, 3), parts[g], consts, halo_x=halo3, halo_cs=cs_h3)
            with ExitStack() as s4:
                fgb = cx.sb(s4, "fgb", [128, D], F32)
                cx.dma("sp", fgb[:], final_gb[:, :], writes=[fgb])
                attn_final_pass(cx, T, x3, parts, a_w_in[1], a_w_out[1], (ngT, 3), y, consts, final_g=fgb)
        cx.barrier()
    return nc


def fm(v):
    v = np.asarray(v, np.float32)
    lead = v.shape[:-1]
    return np.ascontiguousarray(np.moveaxis(v.reshape(lead + (8, 128)), -1, 0))


def const_inputs():
    ident = np.eye(128, dtype=np.float32).astype(ml_dtypes.bfloat16)
    p = np.arange(128)[:, None]
    f = np.arange(128)[None, :]
    NEG = -30000.0
    m = np.zeros((128, 4, 128), np.float32)
    m[:, 3, :] = np.where((127 - p) <= f - 64, 0.0, NEG)
    m[:, 0, :] = np.where(p >= f + 64, 0.0, NEG)
    m[:, 1, :] = np.where(np.abs(f - p) <= 64, 0.0, NEG)
    m[:, 2, :] = np.where(p <= f - 64, 0.0, NEG)
    cm = np.zeros((128, 2, 128), np.float32)
    cm[:, 0, :] = (p <= f)
    cm[:, 1, :] = (p >= f)
    return {"ident": ident, "maskb": m.astype(ml_dtypes.bfloat16), "cmask": cm.astype(ml_dtypes.bfloat16)}


def rope_table(pos):
    half = 16
    inv = (np.float32(500000.0) ** (-(np.arange(half, dtype=np.float32) * np.float32(2.0)) / np.float32(32.0))).astype(np.float32)
    ang = pos.astype(np.float32)[:, None] * inv[None, :]
    return np.concatenate([np.cos(ang), np.sin(ang)], axis=1).astype(np.float32)


def core_inputs(x_ext, pos, flag, W, reverse, sel=(0.0, 0.0), pos_h3=None):
    d = dict(const_inputs())
    d["x0"] = np.ascontiguousarray(x_ext, np.float32)
    d["cs"] = rope_table(pos)
    d["a_w_in"] = W["a_w_in"]
    d["a_w_out"] = W["a_w_out"]
    d["norm_gT"] = fm(W["norm_g"])
    d["final_gb"] = np.ascontiguousarray(np.broadcast_to(W["final_g"][None, :], (128, D)), np.float32)
    d["flag8"] = np.full((128, 8, 1), flag, np.float32)
    dd = [1, 0] if reverse else [0, 1]
    d["b_w_in"] = W["b_w_in"][0]
    d["b_w_out"] = W["b_w_out"][0]
    d["b_w_r"] = np.ascontiguousarray(W["b_w_r"][0][dd])
    d["b_w_i"] = np.ascontiguousarray(W["b_w_i"][0][dd])
    d["b_b_rT"] = fm(W["b_b_r"][0][dd])
    d["b_b_iT"] = fm(W["b_b_i"][0][dd])
    d["b_lamT"] = fm(W["b_lambda"][0][dd])
    cw = W["b_conv_w"][0]
    z = np.zeros((1, D), np.float32)
    w5 = np.concatenate([cw, z], 0) if not reverse else np.concatenate([z, cw[::-1]], 0)
    d["b_w5T"] = fm(w5)
    d["b_cbT"] = fm(W["b_conv_b"][0])
    cw_in = W["c_w_in"][0]
    if reverse:
        cw_in = np.concatenate([cw_in[:, 0:1024], cw_in[:, 2048:3072], cw_in[:, 1024:2048], cw_in[:, 3072:]], 1)
    d["c_w_in"] = np.ascontiguousarray(cw_in)
    d["c_w_out"] = W["c_w_out"][0]
    d["c_lbT"] = fm(W["c_lower_bounds"][:, dd, :]).reshape(128, 4, 16)
    d["sel"] = np.ascontiguousarray(np.broadcast_to(np.asarray(sel, np.float32)[None, :], (128, 2)))
    d["cs_h3"] = rope_table(pos_h3 if pos_h3 is not None else np.zeros(2048))
    d["c_gnb"] = np.ascontiguousarray(np.broadcast_to(W["c_gnorm_g"][0][None, :], (128, 128)), np.float32)
    return d


def rglru_stage1(cx, T, x_src, w_in, gl, xbT, sgT, consts):
    ident = consts["ident"]
    NT = 512
    with ExitStack() as sc:
        wt = cx.sb(sc, "wt", [128, 8, 2048], BF16)
        stg = [cx.sb(sc, "stg", [128, 2048], F32) for _ in range(3)]
        load_weight(cx, stg, wt, 0, w_in[:, :], 2048, g=gl)
        xt = [cx.sb(sc, "xt", [128, D], F32) for _ in range(2)]
        junk = cx.sb(sc, "junk", [128, D], F32)
        ssq = [cx.sb(sc, "ssq", [128, 1], F32) for _ in range(2)]
        std = [cx.sb(sc, "std", [128, 1], F32) for _ in range(2)]
        rstd = [cx.sb(sc, "rstd", [128, 1], F32) for _ in range(2)]
        xn = [cx.sb(sc, "xn", [128, D], BF16) for _ in range(2)]
        xnT = [cx.sb(sc, "xnT", [128, 8, NT], BF16) for _ in range(2)]
        ob = [cx.sb(sc, "ob", [128, NT], F32) for _ in range(4)]
        zt = cx.sb(sc, "zt", [128, 8, 2], F32)
        pj = [cx.ps(sc, "pj", [128, 512], F32) for _ in range(2)]
        tp = [cx.ps(sc, "tp", [128, 1024], BF16) for _ in range(2)]
        cx.op("dve", lambda en: en.memset(zt[:], 0.0), writes=[zt])
        cx.dma("pool", xbT[:, :, 0:2].rearrange("b p c -> p b c"), zt[:, :, :], reads=[zt])
        cx.dma("pool", xbT[:, :, T + 2:T + 4].rearrange("b p c -> p b c"), zt[:, :, :], reads=[zt])
        k = 0
        for ci in range(T // NT):
            t0 = ci * NT
            xs_ = xnT[ci % 2]
            for tt in range(4):
                s2 = k % 2
                k += 1
                cx.dma("sp", xt[s2][:], x_src[t0 + tt * 128:t0 + (tt + 1) * 128, :], writes=[xt[s2]])
                norm_tile(cx, xt[s2], junk, ssq[s2], std[s2], rstd[s2], xn[s2])
                tpb = tp[k % 2]
                for kc in range(8):
                    cx.op("pe", lambda en, kc=kc, tpb=tpb, s2=s2: en.transpose(
                        out=tpb[:, kc * 128:(kc + 1) * 128], in_=xn[s2][:, kc * 128:(kc + 1) * 128],
                        identity=ident[:]), reads=[xn[s2], ident], writes=[tpb])
                cx.op("dve", lambda en, tpb=tpb, tt=tt: en.tensor_copy(
                    out=xs_[:, :, tt * 128:(tt + 1) * 128], in_=tpb[:, :].rearrange("p (c t) -> p c t", t=128)),
                    reads=[tpb], writes=[xs_])
            for cb in range(16):
                pb = pj[cb % 2]
                o_ = ob[cb % 4]
                for kc in range(8):
                    cx.op("pe", lambda en, kc=kc, cb=cb, pb=pb: en.matmul(
                        pb[:, :], lhsT=wt[:, kc, cb * 128:(cb + 1) * 128], rhs=xs_[:, kc, :],
                        start=(kc == 0), stop=(kc == 7)), reads=[wt, xs_], writes=[pb])
                if cb < 8:
                    cx.op("dve", lambda en, pb=pb, o_=o_: en.tensor_copy(out=o_[:, :], in_=pb[:, :]),
                          reads=[pb], writes=[o_])
                    cx.dma("pool", xbT[cb, :, 2 + t0:2 + t0 + NT], o_[:, :], reads=[o_])
                else:
                    cx.op("act", lambda en, pb=pb, o_=o_: en.activation(out=o_[:, :], in_=pb[:, :], func=AF.Silu),
                          reads=[pb], writes=[o_])
                    cx.dma("pool", sgT[cb - 8, :, t0:t0 + NT], o_[:, :], reads=[o_])
        cx.barrier()


def rglru_pass(cx, T, dr, xbT, sgT, res_src, dst, P, state, consts):
    NT = 512
    RD = 4
    ones8 = consts["ones8"]
    wr, wi, wo = P["wr"], P["wi"], P["wo"]
    with ExitStack() as sc:
        def ring(name, shape, dt, k=RD):
            return [cx.sb(sc, name, shape, dt) for _ in range(k)]
        xbh = ring("xbh", [128, NT + 4], F32)
        sgc = ring("sgc", [128, NT], F32)
        xc = ring("xc", [128, NT], F32)
        xcb = ring("xcb", [128, NT], BF16)
        rr = ring("rr", [128, NT], F32)
        ii = ring("ii", [128, NT], F32)
        aa = ring("aa", [128, NT], F32)
        mm = ring("mm", [128, NT], F32)
        uu = ring("uu", [128, NT], F32)
        hh = ring("hh", [128, NT], F32)
        yg = [cx.sb(sc, "yg", [128, 8, NT], BF16) for _ in range(2)]
        rt_ = [cx.sb(sc, "rt", [128, D], F32) for _ in range(3)]
        xo = [cx.sb(sc, "xo", [128, D], F32) for _ in range(2)]
        pr = [cx.ps(sc, "pr", [128, 512], F32) for _ in range(3)]
        pi = [cx.ps(sc, "pi", [128, 512], F32) for _ in range(3)]
        pj = [cx.ps(sc, "pj", [128, 512], F32) for _ in range(2)]
        nch = T // NT
        order = list(range(nch)) if dr == 0 else list(range(nch - 1, -1, -1))
        items = [(oi, ci, cb) for oi, ci in enumerate(order) for cb in range(8)]
        NI = len(items)
        kk = [0]

        def s1(u):
            oi, ci, cb = items[u]
            t0 = ci * NT
            s = u % RD
            X, XC = xbh[s], xc[s]
            cx.dma("sp", X[:], xbT[cb, :, t0:t0 + NT + 4], writes=[X])
            cx.dma("sp", sgc[s][:], sgT[cb, :, t0:t0 + NT], writes=[sgc[s]])
            cx.op("act", lambda en: en.activation(
                out=XC[:, :], in_=X[:, 0:NT], func=AF.Identity, scale=P["w5"][:, 0, cb:cb + 1],
                bias=P["cb"][:, cb:cb + 1]), reads=[X, P["w5"], P["cb"]], writes=[XC])
            for j in range(1, 5):
                cx.op("dve", lambda en, j=j: en.scalar_tensor_tensor(
                    out=XC[:, :], in0=X[:, j:j + NT], scalar=P["w5"][:, j, cb:cb + 1], in1=XC[:, :],
                    op0=ALU.mult, op1=ALU.add), reads=[X, P["w5"], XC], writes=[XC])
            cx.op("pool", lambda en: en.tensor_copy(out=xcb[s][:, :], in_=XC[:, :]), reads=[XC], writes=[xcb[s]])
            p3 = u % 3
            cx.op("pe", lambda en: en.matmul(pr[p3][:, :], lhsT=wr[:, dr * 8 + cb, :], rhs=xcb[s][:, :],
                                             start=True, stop=True), reads=[wr, xcb[s]], writes=[pr[p3]])
            cx.op("pe", lambda en: en.matmul(pi[p3][:, :], lhsT=wi[:, dr * 8 + cb, :], rhs=xcb[s][:, :],
                                             start=True, stop=True), reads=[wi, xcb[s]], writes=[pi[p3]])

        def s2(u):
            oi, ci, cb = items[u]
            s = u % RD
            p3 = u % 3
            cx.op("act", lambda en: en.activation(out=rr[s][:, :], in_=pr[p3][:, :], func=AF.Sigmoid,
                                                  bias=P["br"][:, dr, cb:cb + 1]), reads=[pr[p3], P["br"]], writes=[rr[s]])
            cx.op("act", lambda en: en.activation(out=ii[s][:, :], in_=pi[p3][:, :], func=AF.Sigmoid,
                                                  bias=P["bi"][:, dr, cb:cb + 1]), reads=[pi[p3], P["bi"]], writes=[ii[s]])
            cx.op("act", lambda en: en.activation(out=aa[s][:, :], in_=rr[s][:, :], func=AF.Exp,
                                                  scale=P["cneg"][:, dr, cb:cb + 1]), reads=[rr[s], P["cneg"]], writes=[aa[s]])
            cx.op("act", lambda en: en.activation(out=mm[s][:, :], in_=rr[s][:, :], func=AF.Exp,
                                                  scale=P["cneg2"][:, dr, cb:cb + 1]), reads=[rr[s], P["cneg2"]], writes=[mm[s]])
            cx.op("act", lambda en: en.activation(out=mm[s][:, :], in_=mm[s][:, :], func=AF.Sqrt, scale=-1.0,
                                                  bias=ones8[:, 0, :]), reads=[mm[s], ones8], writes=[mm[s]])
            cx.op("pool", lambda en: en.tensor_tensor(out=uu[s][:, :], in0=ii[s][:, :], in1=xc[s][:, :], op=ALU.mult),
                  reads=[ii[s], xc[s]], writes=[uu[s]])
            cx.op("pool", lambda en: en.tensor_tensor(out=uu[s][:, :], in0=uu[s][:, :], in1=mm[s][:, :], op=ALU.mult),
                  reads=[uu[s], mm[s]], writes=[uu[s]])

        def s3(u):
            oi, ci, cb = items[u]
            s = u % RD
            ygc = yg[oi % 2]
            if dr == 0:
                cx.op("dve", lambda en: en.tensor_tensor_scan(
                    out=hh[s][:, :], data0=aa[s][:, :], data1=uu[s][:, :], initial=state[:, cb:cb + 1],
                    op0=ALU.mult, op1=ALU.add), reads=[aa[s], uu[s], state], writes=[hh[s]])
                lastc = NT - 1
            else:
                cx.op("dve", lambda en: en.tensor_tensor_scan(
                    out=hh[s][:, ::-1], data0=aa[s][:, ::-1], data1=uu[s][:, ::-1], initial=state[:, cb:cb + 1],
                    op0=ALU.mult, op1=ALU.add), reads=[aa[s], uu[s], state], writes=[hh[s]])
                lastc = 0
            cx.op("dve", lambda en: en.tensor_copy(out=state[:, cb:cb + 1], in_=hh[s][:, lastc:lastc + 1]),
                  reads=[hh[s]], writes=[state])
            cx.op("pool", lambda en: en.tensor_tensor(out=ygc[:, cb, :], in0=hh[s][:, :], in1=sgc[s][:, :], op=ALU.mult),
                  reads=[hh[s], sgc[s]], writes=[ygc])
            if cb == 7:
                out_chunk(oi, ci)

        def out_chunk(oi, ci):
            t0 = ci * NT
            ygc = yg[oi % 2]
            for tt in range(4):
                k = kk[0]
                kk[0] += 1
                r3 = rt_[k % 3]
                x2 = xo[k % 2]
                rows = slice(t0 + tt * 128, t0 + (tt + 1) * 128)
                cx.dma("sp", r3[:], res_src[rows, :], writes=[r3])
                for c2 in range(2):
                    pb = pj[c2]
                    for cb in range(8):
                        cx.op("pe", lambda en, cb=cb, c2=c2, pb=pb, tt=tt: en.matmul(
                            pb[:, :], lhsT=ygc[:, cb, tt * 128:(tt + 1) * 128], rhs=wo[:, cb, c2 * 512:(c2 + 1) * 512],
                            start=(cb == 0), stop=(cb == 7)), reads=[ygc, wo], writes=[pb])
                    cx.op("dve", lambda en, c2=c2, pb=pb, x2=x2, r3=r3: en.tensor_tensor(
                        out=x2[:, c2 * 512:(c2 + 1) * 512], in0=pb[:, :], in1=r3[:, c2 * 512:(c2 + 1) * 512],
                        op=ALU.add), reads=[pb, r3], writes=[x2])
                cx.dma("pool", dst[rows, :], x2[:], reads=[x2])

        for u in range(NI + 2):
            if u < NI:
                s1(u)
            if 0 <= u - 1 < NI:
                s2(u - 1)
            if 0 <= u - 2 < NI:
                s3(u - 2)
        cx.barrier()


def layer_b(cx, T, x_src, x_dst, din, ngT, layer, scr, consts):
    ones8 = consts["ones8"]
    with ExitStack() as sc:
        P = {}
        P["wr"] = cx.sb(sc, "wr", [128, 16, 128], BF16)
        P["wi"] = cx.sb(sc, "wi", [128, 16, 128], BF16)
        P["wo"] = cx.sb(sc, "wo", [128, 8, 1024], BF16)
        for nm in ("br", "bi", "lam", "cneg", "cneg2"):
            P[nm] = cx.sb(sc, nm, [128, 2, 8], F32)
        P["w5"] = cx.sb(sc, "w5", [128, 5, 8], F32)
        P["cb"] = cx.sb(sc, "cb", [128, 8], F32)
        state = cx.sb(sc, "state", [128, 8], F32)
        with ExitStack() as s2:
            stg = [cx.sb(s2, "stg", [128, 2048], F32) for _ in range(2)]
            for (nm, src) in (("wr", din["b_w_r"]), ("wi", din["b_w_i"])):
                cx.dma("sp", stg[0][:, :].rearrange("p (a o) -> p a o", o=128),
                       src.rearrange("d b c o -> c (d b) o"), writes=[stg[0]])
                cx.op("dve", lambda en, nm=nm: en.tensor_copy(out=P[nm][:, :, :],
                                                              in_=stg[0][:, :].rearrange("p (a o) -> p a o", o=128)),
                      reads=[stg[0]], writes=[P[nm]])
            load_weight(cx, stg, P["wo"], 0, din["b_w_out"][:, :], 1024, g=None)
            cx.dma("sp", P["br"][:], din["b_b_rT"][:, :, :], writes=[P["br"]])
            cx.dma("sp", P["bi"][:], din["b_b_iT"][:, :, :], writes=[P["bi"]])
            cx.dma("sp", P["lam"][:], din["b_lamT"][:, :, :], writes=[P["lam"]])
            cx.dma("sp", P["w5"][:], din["b_w5T"][:, :, :], writes=[P["w5"]])
            cx.dma("sp", P["cb"][:], din["b_cbT"][:, :], writes=[P["cb"]])
            cx.op("act", lambda en: en.activation(out=P["cneg"][:], in_=P["lam"][:], func=AF.Exp, scale=-1.0),
                  reads=[P["lam"]], writes=[P["cneg"]])
            cx.op("act", lambda en: en.activation(out=P["cneg"][:], in_=P["cneg"][:], func=AF.Ln, bias=ones8[:, 0, :]),
                  reads=[P["cneg"], ones8], writes=[P["cneg"]])
            cx.op("dve", lambda en: en.tensor_scalar(out=P["cneg2"][:], in0=P["cneg"][:], scalar1=-16.0, scalar2=None,
                                                     op0=ALU.mult), reads=[P["cneg"]], writes=[P["cneg2"]])
            cx.op("dve", lambda en: en.tensor_scalar(out=P["cneg"][:], in0=P["cneg"][:], scalar1=-8.0, scalar2=None,
                                                     op0=ALU.mult), reads=[P["cneg"]], writes=[P["cneg"]])
            cx.barrier()
        rglru_stage1(cx, T, x_src, din["b_w_in"], (ngT, layer), scr["xbT"], scr["sgT"], consts)
        sel = consts["sel"]
        state2 = cx.sb(sc, "state2", [128, 8], F32)
        with ExitStack() as s3:
            hin = cx.sb(s3, "hin", [128, 16], F32)
            hout = cx.sb(s3, "hout", [128, 16], F32)
            hsw = cx.sb(s3, "hsw", [128, 8, 2], F32)
            cx.dma("sp", hin[:, :].rearrange("p (b c) -> p b c", c=2),
                   scr["xbT"][:, :, T:T + 2].rearrange("b p c -> p b c"), writes=[hin])
            cx.barrier()
            exchange_sb(cx, s3, hin[:, :], hout[:, :], 16, sel)
            hv = hout[:, :].rearrange("p (b c) -> p b c", c=2)
            cx.op("dve", lambda en: en.tensor_copy(out=hsw[:, :, 0:1], in_=hv[:, :, 1:2]), reads=[hout], writes=[hsw])
            cx.op("dve", lambda en: en.tensor_copy(out=hsw[:, :, 1:2], in_=hv[:, :, 0:1]), reads=[hout], writes=[hsw])
            cx.dma("pool", scr["xbT"][:, :, T + 2:T + 4].rearrange("b p c -> p b c"), hsw[:, :, :], reads=[hsw])
            cx.barrier()
        cx.op("dve", lambda en: en.memset(state[:], 0.0), writes=[state])
        rglru_pass(cx, T, 0, scr["xbT"], scr["sgT"], x_src, scr["xp"], P, state, consts)
        with ExitStack() as s3:
            exchange_sb(cx, s3, state[:, :], state2[:, :], 8, sel)
        rglru_pass(cx, T, 1, scr["xbT"], scr["sgT"], scr["xp"], x_dst, P, state2, consts)


def hgrn_pass(cx, T, dr, x_src, din, gl, P, of, dst, consts, final, S):
    ident = consts["ident"]
    NT = 256
    NC = NT // 128
    w_in = din["c_w_in"]
    with ExitStack() as sc:
        ncol = 4096 if final else 3072
        wt = cx.sb(sc, "wt", [128, 8, ncol], BF16)
        if final:
            wo = cx.sb(sc, "wo", [128, 8, 1024], BF16)
        with ExitStack() as s2:
            stg = [cx.sb(s2, "stg", [128, 1024], F32) for _ in range(4)]
            zc = 1024 + 1024 * dr
            load_weight(cx, stg, wt, 0, w_in[:, 0:1024], 1024, g=gl)
            load_weight(cx, stg, wt, 1024, w_in[:, zc:zc + 1024], 1024, g=gl)
            load_weight(cx, stg, wt, 2048, w_in[:, 3072:4096], 1024, g=gl)
            if final:
                load_weight(cx, stg, wt, 3072, w_in[:, 4096:5120], 1024, g=gl)
                load_weight(cx, stg, wo, 0, din["c_w_out"][:, :], 1024, g=None)
            cx.barrier()
        Sp = cx.sb(sc, "Sp", [128, 8, 128], BF16)
        tmpS = cx.sb(sc, "tmpS", [128, 8, 128], F32)
        xt = [cx.sb(sc, "xt", [128, D], F32)] * 2
        ssq = [cx.sb(sc, "ssq", [128, 1], F32) for _ in range(2)]
        std = [cx.sb(sc, "std", [128, 1], F32) for _ in range(2)]
        rstd = [cx.sb(sc, "rstd", [128, 1], F32) for _ in range(2)]
        xn = [cx.sb(sc, "xn", [128, D], BF16)] * 2
        xnT = cx.sb(sc, "xnT", [128, 8, NT], BF16)
        nbq = 1 if final else 2
        qTs = [cx.sb(sc, "qT", [128, 8, NT], F32) for _ in range(nbq)]
        As = [cx.sb(sc, "A", [128, 8, NT], F32) for _ in range(nbq)]
        L = cx.sb(sc, "L", [128, 8, NT], F32)
        bb = cx.sb(sc, "bb", [128, 8, NT], F32)
        junk = cx.sb(sc, "junk", [128, D], BF16)
        qd = [cx.sb(sc, "qd", [128, 8, NT], BF16) for _ in range(2)]
        kd = [cx.sb(sc, "kd", [128, 8, NT], BF16) for _ in range(2)]
        vt = [cx.sb(sc, "vt", [128, NC, 1024], BF16) for _ in range(2)]
        gg = [cx.sb(sc, "gg", [128, 8 * NC], F32) for _ in range(2)]
        e2 = [cx.sb(sc, "e2", [128, 8 * NC], F32) for _ in range(2)]
        er = [cx.sb(sc, "er", [128, 8 * NC], F32) for _ in range(2)]
        attm = [cx.sb(sc, "attm", [128, 8, 128], BF16) for _ in range(2)]
        kdtok = [cx.sb(sc, "kdtok", [128, 8, 128], BF16) for _ in range(2)]
        osb = [cx.sb(sc, "osb", [128, D], F32)] * 2
        onesr = cx.sb(sc, "onesr", [128, NT], F32)
        cx.op("dve", lambda en: en.memset(onesr[:], 1.0), writes=[onesr])
        rc_ = 0 if dr == 0 else 127
        for c_ in range(NC):
            cx.op("dve", lambda en, c_=c_: en.memset(onesr[:, c_ * 128 + rc_:c_ * 128 + rc_ + 1], 0.0), writes=[onesr])
        if final:
            sgt = [cx.sb(sc, "sgt", [128, NC, 1024], F32) for _ in range(2)]
            oft = [cx.sb(sc, "oft", [128, D], F32)] * 2
            rt_ = [cx.sb(sc, "rt", [128, D], F32)] * 2
            s8 = cx.sb(sc, "s8", [128, 8], F32)
            r8 = cx.sb(sc, "r8", [128, 8], F32)
            og = [cx.sb(sc, "og", [128, D], BF16)] * 2
            ogT = [cx.sb(sc, "ogT", [128, 8, 128], BF16)] * 2
            xo = rt_
        pj = [cx.ps(sc, "pj", [128, 512], F32) for _ in range(2)]
        tp = cx.ps(sc, "tp", [128, 1024], BF16)
        scp = cx.ps(sc, "scp", [128, 512], F32)
        po = [cx.ps(sc, "po", [128, 512], F32) for _ in range(2)]
        su = [cx.ps(sc, "su", [128, 512], F32) for _ in range(2)]
        nst = T // NT
        order = list(range(nst)) if dr == 0 else list(range(nst - 1, -1, -1))
        corder = list(range(NC)) if dr == 0 else list(range(NC - 1, -1, -1))
        lastc = 127 if dr == 0 else 0
        kcnt = [0, 0]

        def X(q):
            si = order[q]
            t0 = si * NT
            vq = vt[q % 2]
            fm, tm = [], []
            qT, A = qTs[q % nbq], As[q % nbq]

            def tile_piece(tt):
                s2 = kcnt[0] % 2
                kcnt[0] += 1
                cx.dma("sp", xt[s2][:], x_src[t0 + tt * 128:t0 + (tt + 1) * 128, :], writes=[xt[s2]])
                norm_tile(cx, xt[s2], junk, ssq[s2], std[s2], rstd[s2], xn[s2])
                for kc in range(8):
                    cx.op("pe", lambda en, kc=kc, s2=s2: en.transpose(
                        out=tp[:, kc * 128:(kc + 1) * 128], in_=xn[s2][:, kc * 128:(kc + 1) * 128],
                        identity=ident[:]), reads=[xn[s2], ident], writes=[tp])
                cx.op("dve", lambda en, tt=tt: en.tensor_copy(
                    out=xnT[:, :, tt * 128:(tt + 1) * 128], in_=tp[:, :].rearrange("p (c t) -> p c t", t=128)),
                    reads=[tp], writes=[xnT])

            for tt in range(NC):
                fm.append(lambda tt=tt: tile_piece(tt))

            def fm_piece(cb):
                cbo = (cb % 8) + (8 if cb < 8 else 0)
                pb = pj[cb % 2]
                for kc in range(8):
                    cx.op("pe", lambda en, kc=kc, cbo=cbo, pb=pb: en.matmul(
                        pb[:, 0:NT], lhsT=wt[:, kc, cbo * 128:(cbo + 1) * 128], rhs=xnT[:, kc, :],
                        start=(kc == 0), stop=(kc == 7)), reads=[wt, xnT], writes=[pb])
                if cbo < 8:
                    cx.op("act", lambda en, pb=pb, cbo=cbo: en.activation(out=qT[:, cbo, :], in_=pb[:, 0:NT], func=AF.Copy),
                          reads=[pb], writes=[qT])
                else:
                    cx.op("act", lambda en, pb=pb, cbo=cbo: en.activation(out=A[:, cbo - 8, :], in_=pb[:, 0:NT],
                                                                         func=AF.Sigmoid), reads=[pb], writes=[A])

            for cb in range(16):
                fm.append(lambda cb=cb: fm_piece(cb))

            def tm_piece(tt, c):
                if True:
                    pb = pj[c % 2]
                    for kc in range(8):
                        cx.op("pe", lambda en, kc=kc, c=c, pb=pb, tt=tt: en.matmul(
                            pb[:, :], lhsT=xnT[:, kc, tt * 128:(tt + 1) * 128],
                            rhs=wt[:, kc, 2048 + c * 512:2048 + (c + 1) * 512],
                            start=(kc == 0), stop=(kc == 7)), reads=[xnT, wt], writes=[pb])
                    if c < 2:
                        cx.op("dve", lambda en, pb=pb, c=c, tt=tt: en.tensor_copy(
                            out=vq[:, tt, c * 512:(c + 1) * 512], in_=pb[:, :]), reads=[pb], writes=[vq])
                    else:
                        cx.op("act", lambda en, pb=pb, c=c, tt=tt: en.activation(
                            out=sgt[q % 2][:, tt, (c - 2) * 512:(c - 1) * 512], in_=pb[:, :], func=AF.Silu),
                            reads=[pb], writes=[sgt[q % 2]])

            for tt in range(NC):
                for c in range(4 if final else 2):
                    tm.append(lambda tt=tt, c=c: tm_piece(tt, c))
            return fm, tm

        def Y(q):
            qq = q % 2
            qT, A = qTs[q % nbq], As[q % nbq]
            cx.op("dve", lambda en: en.tensor_tensor(out=A[:, :, :], in0=A[:, :, :], in1=bcast(P["oml"][:, dr, :], 1, NT),
                                                     op=ALU.mult), reads=[A, P["oml"]], writes=[A])
            cx.op("dve", lambda en: en.tensor_tensor(out=A[:, :, :], in0=A[:, :, :], in1=bcast(P["lb"][:, dr, :], 1, NT),
                                                     op=ALU.add), reads=[A, P["lb"]], writes=[A])
            cx.op("act", lambda en: en.activation(out=L[:], in_=A[:], func=AF.Ln), reads=[A], writes=[L])
            cx.op("pool", lambda en: en.tensor_scalar(out=A[:], in0=A[:], scalar1=-1.0, scalar2=1.0,
                                                      op0=ALU.mult, op1=ALU.add), reads=[A], writes=[A])
            for h in range(8):
                if dr == 0:
                    cx.op("dve", lambda en, h=h: en.tensor_tensor_scan(
                        out=bb[:, h, :], data0=onesr[:, :], data1=L[:, h, :], initial=0.0,
                        op0=ALU.mult, op1=ALU.add), reads=[onesr, L], writes=[bb])
                else:
                    cx.op("dve", lambda en, h=h: en.tensor_tensor_scan(
                        out=bb[:, h, ::-1], data0=onesr[:, ::-1], data1=L[:, h, ::-1], initial=0.0,
                        op0=ALU.mult, op1=ALU.add), reads=[onesr, L], writes=[bb])
            bbv = bb[:, :, :].rearrange("p h (c t) -> p (h c) t", t=128)
            Lv = L[:, :, :].rearrange("p h (c t) -> p (h c) t", t=128)
            cx.op("dve", lambda en: en.tensor_tensor(out=Lv, in0=bbv, in1=bcast(bbv[:, :, 64], 1, 128), op=ALU.subtract),
                  reads=[bb], writes=[L])
            cx.op("act", lambda en: en.activation(out=gg[qq][:, :], in_=bbv[:, :, lastc], func=AF.Exp), reads=[bb], writes=[gg[qq]])
            cx.op("act", lambda en: en.activation(out=er[qq][:, :], in_=bbv[:, :, 64], func=AF.Exp), reads=[bb], writes=[er[qq]])
            cx.op("act", lambda en: en.activation(out=e2[qq][:, :], in_=Lv[:, :, lastc], func=AF.Exp), reads=[L], writes=[e2[qq]])
            cx.op("act", lambda en: en.activation(out=bb[:], in_=L[:], func=AF.Exp), reads=[L], writes=[bb])
            cx.op("act", lambda en: en.activation(out=L[:], in_=L[:], func=AF.Exp, scale=-1.0), reads=[L], writes=[L])
            cx.op("dve", lambda en: en.tensor_tensor(out=qd[qq][:], in0=qT[:], in1=bb[:], op=ALU.mult),
                  reads=[qT, bb], writes=[qd[qq]])
            cx.op("pool", lambda en: en.tensor_tensor(out=kd[qq][:], in0=A[:], in1=L[:], op=ALU.mult),
                  reads=[A, L], writes=[kd[qq]])

        def Z(q):
            si = order[q]
            t0 = si * NT
            qq = q % 2
            QD, KD, VT, GG, E2, ER = qd[qq], kd[qq], vt[qq], gg[qq], e2[qq], er[qq]
            segs1, segs2, segs3 = [], [], []
            for ci_, c in enumerate(corder):
                a2 = ci_ % 2
                segs1.append(lambda c=c, a2=a2: seg1(c, a2))
                segs2.append(lambda c=c, a2=a2: seg2(c, a2))
                if final:
                    segs3.append(lambda c=c, a2=a2: seg3(c, a2))

            def seg1(c, a2):
                cs_ = slice(c * 128, (c + 1) * 128)
                am, kt = attm[a2], kdtok[a2]
                for half in range(2):
                    for hh in range(4):
                        h = half * 4 + hh
                        cx.op("pe", lambda en, h=h, hh=hh: en.matmul(
                            scp[:, hh * 128:(hh + 1) * 128], lhsT=KD[:, h, cs_], rhs=QD[:, h, cs_],
                            start=True, stop=True), reads=[KD, QD], writes=[scp])
                    cx.op("dve", lambda en, half=half: en.tensor_tensor(
                        out=am[:, half * 4:half * 4 + 4, :], in0=scp[:, :].rearrange("p (h t) -> p h t", t=128),
                        in1=bcast(consts["cmask"][:, dr, :], 0, 4), op=ALU.mult),
                        reads=[scp, consts["cmask"]], writes=[am])
                for h in range(8):
                    cx.op("pe", lambda en, h=h: en.transpose(out=tp[:, h * 128:(h + 1) * 128], in_=KD[:, h, cs_],
                                                             identity=ident[:]),
                          reads=[KD, ident], writes=[tp])
                cx.op("act", lambda en: en.activation(out=kt[:, :, :], in_=tp[:, :].rearrange("p (c t) -> p c t", t=128),
                                                      func=AF.Copy), reads=[tp], writes=[kt])

            def seg2(c, a2):
                cs_ = slice(c * 128, (c + 1) * 128)
                am, kt = attm[a2], kdtok[a2]
                cx.op("pool", lambda en: en.tensor_tensor(out=Sp[:, :, :], in0=S[:, :, :],
                                                          in1=bcast(ER[:, c::NC], 1, 128), op=ALU.mult),
                      reads=[S, ER], writes=[Sp])
                for h in range(8):
                    pob = po[h // 4]
                    oc = slice((h % 4) * 128, (h % 4 + 1) * 128)
                    cx.op("pe", lambda en, h=h, pob=pob, oc=oc: en.matmul(
                        pob[:, oc], lhsT=QD[:, h, cs_], rhs=Sp[:, h, :], start=True, stop=False),
                        reads=[QD, Sp], writes=[pob])
                    cx.op("pe", lambda en, h=h, pob=pob, oc=oc: en.matmul(
                        pob[:, oc], lhsT=am[:, h, :], rhs=VT[:, c, h * 128:(h + 1) * 128], start=False, stop=True),
                        reads=[am, VT], writes=[pob])
                    cx.op("pe", lambda en, h=h, oc=oc: en.matmul(
                        su[h // 4][:, oc], lhsT=kt[:, h, :], rhs=VT[:, c, h * 128:(h + 1) * 128], start=True, stop=True),
                        reads=[kt, VT], writes=[su[h // 4]])
                cx.op("dve", lambda en: en.tensor_tensor(out=S[:, :, :], in0=S[:, :, :], in1=bcast(GG[:, c::NC], 1, 128),
                                                         op=ALU.mult), reads=[S, GG], writes=[S])
                for b2 in range(2):
                    e2v = E2[:, c::NC]
                    cx.op("dve", lambda en, b2=b2, e2v=e2v: en.tensor_tensor(
                        out=tmpS[:, b2 * 4:b2 * 4 + 4, :], in0=su[b2][:, :].rearrange("p (h v) -> p h v", v=128),
                        in1=bcast(e2v[:, b2 * 4:b2 * 4 + 4], 1, 128), op=ALU.mult), reads=[su[b2], E2], writes=[tmpS])
                cx.op("dve", lambda en: en.tensor_tensor(out=S[:, :, :], in0=S[:, :, :], in1=tmpS[:, :, :], op=ALU.add),
                      reads=[S, tmpS], writes=[S])
                rows = slice(t0 + c * 128, t0 + (c + 1) * 128)
                ob = osb[a2]
                if not final:
                    cx.op("act", lambda en: en.activation(out=ob[:, 0:512], in_=po[0][:, :], func=AF.Copy),
                          reads=[po[0]], writes=[ob])
                    cx.op("act", lambda en: en.activation(out=ob[:, 512:1024], in_=po[1][:, :], func=AF.Copy),
                          reads=[po[1]], writes=[ob])
                    cx.dma("pool", of[rows, :], ob[:, :], reads=[ob])
                    return
                cx.dma("sp", oft[a2][:], of[rows, :], writes=[oft[a2]])
                cx.dma("sp", rt_[a2][:], x_src[rows, :], writes=[rt_[a2]])
                for b2 in range(2):
                    cx.op("dve", lambda en, b2=b2: en.tensor_tensor(
                        out=ob[:, b2 * 512:(b2 + 1) * 512], in0=po[b2][:, :], in1=oft[a2][:, b2 * 512:(b2 + 1) * 512],
                        op=ALU.add), reads=[po[b2], oft[a2]], writes=[ob])

            def seg3(c, a2):
                rows = slice(t0 + c * 128, t0 + (c + 1) * 128)
                ob = osb[a2]
                obv = ob[:, :].rearrange("p (h v) -> p h v", v=128)
                sq = oft[a2]
                cx.op("pool", lambda en: en.tensor_tensor(out=sq[:, :], in0=ob[:, :], in1=ob[:, :], op=ALU.mult),
                      reads=[ob], writes=[sq])
                cx.op("dve", lambda en: en.tensor_reduce(out=s8[:, :], in_=sq[:, :].rearrange("p (h v) -> p h v", v=128),
                                                         axis=AX.X, op=ALU.add), reads=[sq], writes=[s8])
                cx.op("act", lambda en: en.activation(out=s8[:, :], in_=s8[:, :], func=AF.Sqrt, scale=1.0 / 128,
                                                      bias=EPS_AP[0][:]), reads=[s8], writes=[s8])
                cx.op("dve", lambda en: en.reciprocal(out=r8[:, :], in_=s8[:, :]), reads=[s8], writes=[r8])
                obv = ob[:, :].rearrange("p (h v) -> p h v", v=128)
                cx.op("dve", lambda en: en.tensor_tensor(out=obv, in0=obv, in1=bcast(r8[:, :], 1, 128), op=ALU.mult),
                      reads=[ob, r8], writes=[ob])
                cx.op("pool", lambda en: en.tensor_tensor(out=obv, in0=obv, in1=bcast(P["gnb"][:, :], 0, 8),
                                                          op=ALU.mult), reads=[ob, P["gnb"]], writes=[ob])
                cx.op("pool", lambda en: en.tensor_tensor(out=og[a2][:, :], in0=ob[:, :], in1=sgt[qq][:, c, :],
                                                          op=ALU.mult), reads=[ob, sgt[qq]], writes=[og[a2]])
                for kc in range(8):
                    cx.op("pe", lambda en, kc=kc: en.transpose(out=tp[:, kc * 128:(kc + 1) * 128],
                                                               in_=og[a2][:, kc * 128:(kc + 1) * 128], identity=ident[:]),
                          reads=[og[a2], ident], writes=[tp])
                cx.op("act", lambda en: en.activation(out=ogT[a2][:, :, :], in_=tp[:, :].rearrange("p (c t) -> p c t", t=128),
                                                      func=AF.Copy), reads=[tp], writes=[ogT[a2]])
                for c2 in range(2):
                    pb = pj[c2]
                    for kc in range(8):
                        cx.op("pe", lambda en, kc=kc, c2=c2, pb=pb: en.matmul(
                            pb[:, :], lhsT=ogT[a2][:, kc, :], rhs=wo[:, kc, c2 * 512:(c2 + 1) * 512],
                            start=(kc == 0), stop=(kc == 7)), reads=[ogT[a2], wo], writes=[pb])
                    cx.op("dve", lambda en, c2=c2, pb=pb: en.tensor_tensor(
                        out=xo[a2][:, c2 * 512:(c2 + 1) * 512], in0=pb[:, :], in1=rt_[a2][:, c2 * 512:(c2 + 1) * 512],
                        op=ALU.add), reads=[pb, rt_[a2]], writes=[xo[a2]])
                cx.dma("pool", dst[rows, :], xo[a2][:], reads=[xo[a2]])
            if final:
                return segs1 + [segs2[0], segs3[0], segs2[1], segs3[1]]
            return segs1 + segs2

        def Y_ops(q):
            cx.capture = []
            Y(q)
            ops = cx.capture
            cx.capture = None
            return ops

        fm0, tm0 = X(0)
        for p_ in fm0 + tm0:
            p_()
        if nbq == 1:
            Y(0)
        for q in range(nst):
            zs = Z(q)
            if nbq == 2:
                yops = Y_ops(q)
                if q + 1 < nst:
                    fm, tm = X(q + 1)
                    yi = 0
                    for p_ in fm:
                        p_()
                        if yi < len(yops):
                            cx.op(*yops[yi])
                            yi += 1
                    while yi < len(yops):
                        cx.op(*yops[yi])
                        yi += 1
                    for z_ in zs:
                        z_()
                    for p_ in tm:
                        p_()
                else:
                    for y_ in yops:
                        cx.op(*y_)
                    for z_ in zs:
                        z_()
                continue
            if q + 1 < nst:
                fm, tm = X(q + 1)
                step = 10 ** 9 if (INTERLEAVE_OFF or final) else max(1, len(fm) // (len(zs) + 1))
                zi = 0
                for k_, p_ in enumerate(fm):
                    p_()
                    if (k_ + 1) % step == 0 and zi < len(zs):
                        zs[zi]()
                        zi += 1
                while zi < len(zs):
                    zs[zi]()
                    zi += 1
                for p_ in tm:
                    p_()
                Y(q + 1)
            else:
                for z_ in zs:
                    z_()
        cx.barrier()


def layer_c(cx, T, x_src, x_dst, din, ngT, layer, scr, consts):
    with ExitStack() as sc:
        P = {}
        P["lb"] = cx.sb(sc, "lb", [128, 2, 8], F32)
        P["oml"] = cx.sb(sc, "oml", [128, 2, 8], F32)
        P["gnb"] = cx.sb(sc, "gnb", [128, 128], F32)
        lbe = cx.sb(sc, "lbe", [128, 4, 16], F32)
        tot = cx.sb(sc, "tot", [128, 16], F32)
        cx.dma("sp", lbe[:], din["c_lbT"][:, :, :], writes=[lbe])
        cx.dma("sp", P["gnb"][:], din["c_gnb"][:, :], writes=[P["gnb"]])
        cx.op("act", lambda en: en.activation(out=lbe[:], in_=lbe[:], func=AF.Exp), reads=[lbe], writes=[lbe])
        lbv = P["lb"][:, :, :].rearrange("p a b -> p (a b)")
        omv = P["oml"][:, :, :].rearrange("p a b -> p (a b)")
        cx.op("dve", lambda en: en.tensor_tensor(out=tot[:, :], in0=lbe[:, 0, :], in1=lbe[:, 3, :], op=ALU.add),
              reads=[lbe], writes=[tot])
        cx.op("dve", lambda en: en.tensor_tensor(out=lbv, in0=lbe[:, 1, :], in1=lbe[:, 2, :], op=ALU.add),
              reads=[lbe], writes=[P["lb"]])
        cx.op("dve", lambda en: en.tensor_tensor(out=tot[:, :], in0=tot[:, :], in1=lbv, op=ALU.add),
              reads=[tot, P["lb"]], writes=[tot])
        cx.op("dve", lambda en: en.reciprocal(out=tot[:, :], in_=tot[:, :]), reads=[tot], writes=[tot])
        cx.op("dve", lambda en: en.tensor_tensor(out=lbv, in0=lbv, in1=tot[:, :], op=ALU.mult),
              reads=[tot, P["lb"]], writes=[P["lb"]])
        cx.op("dve", lambda en: en.tensor_scalar(out=omv, in0=lbv, scalar1=-1.0, scalar2=1.0, op0=ALU.mult, op1=ALU.add),
              reads=[P["lb"]], writes=[P["oml"]])
        cx.barrier()
        S = cx.sb(sc, "S", [128, 8, 128], F32)
        S2 = cx.sb(sc, "S2", [128, 8, 128], F32)
        cx.op("dve", lambda en: en.memset(S[:], 0.0), writes=[S])
        hgrn_pass(cx, T, 0, x_src, din, (ngT, layer), P, scr["of"], None, consts, False, S)
        with ExitStack() as s3:
            exchange_sb(cx, s3, S[:, :, :].rearrange("p h v -> p (h v)"), S2[:, :, :].rearrange("p h v -> p (h v)"),
                        1024, consts["sel"])
        hgrn_pass(cx, T, 1, x_src, din, (ngT, layer), P, scr["of"], x_dst, consts, True, S2)


def make_in_maps(xp, xsamp, W, T):
    maps = []
    meta = []
    for b in range(xp.shape[0]):
        seq = np.asarray(xp[b], np.float32)
        for half in range(2):
            if half == 0:
                x_ext = seq[0:T + HALO]
                pos = np.arange(T + HALO)
                pos_h3 = T + 2047 - np.arange(2048)
                maps.append(core_inputs(x_ext, pos, 1.0, W, False, (0.0, 1.0), pos_h3))
            else:
                x_ext = seq[::-1][0:T + HALO]
                pos = 2 * T - 1 - np.arange(T + HALO)
                pos_h3 = T - 2048 + np.arange(2048)
                maps.append(core_inputs(x_ext, pos, 1.0, W, True, (1.0, 0.0), pos_h3))
            meta.append(("p", b, half))
    for b in range(xsamp.shape[0]):
        seq = np.asarray(xsamp[b], np.float32)
        x_ext = np.concatenate([seq, np.zeros((HALO, D), np.float32)], 0)
        maps.append(core_inputs(x_ext, np.arange(T + HALO), 0.0, W, False, (0.0, 0.0), None))
        meta.append(("s", b, 0))
    return maps, meta


_NC_CACHE = {}


def kernel(**inputs):
    T = 8192
    W = {k: np.asarray(v, np.float32) for k, v in inputs.items() if k not in ("x_prompt", "x_sample")}
    xp = np.asarray(inputs["x_prompt"], np.float32)
    xsamp = np.asarray(inputs["x_sample"], np.float32)
    maps, meta = make_in_maps(xp, xsamp, W, T)
    if T not in _NC_CACHE:
        _NC_CACHE[T] = build(T, 4)
    res = run_bass_kernel_spmd(_NC_CACHE[T], maps, core_ids=list(range(8)))
    yp = np.zeros(xp.shape, np.float32)
    ys = np.zeros(xsamp.shape, np.float32)
    for c, (kind, b, half) in enumerate(meta):
        y = np.asarray(res.results[c]["y"], np.float32)
        if kind == "s":
            ys[b] = y
        elif half == 0:
            yp[b, 0:T] = y
        else:
            yp[b, T:2 * T] = y[::-1]
    return (yp, ys)
```
